# Optimizing a Trainium2 kernel written in Bass

```python
import math
import jax, jax.numpy as jnp
from jax import lax
import numpy as np

D_MODEL = 1024
BATCH = 32
SEQ = 2048
DEPTH = 2
DEC_BATCH = 8
DEC_SEQ = 8192
PAST_LEN = 128

N_HEADS = 8
HEAD_DIM = 64
V_DIM = 2 * HEAD_DIM
ATT_W = N_HEADS * 2 * HEAD_DIM
Q_BLOCK = 128
ROPE_THETA = 10000.0
LRU_W = 1024
N_LRU_BLOCKS = 8
LRU_BLOCK = LRU_W // N_LRU_BLOCKS
LRU_CONV = 4
LRU_CONV_LEFT = 2
LRU_C = 8.0
D_FF = 3072
FFN_CONV = 3
FFN_CONV_LEFT = 1
PLE_DIM = 256
EPS = 1e-6
IN_COLS = 3 * ATT_W + 2 * LRU_W + 2 * D_MODEL
SPLITS = (ATT_W, 2 * ATT_W, 3 * ATT_W, 3 * ATT_W + LRU_W, 3 * ATT_W + 2 * LRU_W,
          3 * ATT_W + 2 * LRU_W + D_MODEL)

kernel_name = "hybrid_diffattn_rglru_encoder"


def rms_norm(x, g):
    xf = x.astype(jnp.float32)
    y = xf * lax.rsqrt(jnp.mean(xf * xf, axis=-1, keepdims=True) + EPS)
    return (y * g.astype(jnp.float32)).astype(x.dtype)


def dwconv(x, w, b, left):
    K = w.shape[0]
    S = x.shape[1]
    xp = jnp.pad(x, ((0, 0), (left, K - 1 - left), (0, 0)))
    return sum(xp[:, k:k + S] * w[k] for k in range(K)) + b


def rope(x):
    S = x.shape[1]
    inv = ROPE_THETA ** (-jnp.arange(0, HEAD_DIM, 2, dtype=jnp.float32) / HEAD_DIM)
    ang = jnp.arange(S, dtype=jnp.float32)[:, None] * inv[None, :]
    cos = jnp.cos(ang)[:, None, None, :]
    sin = jnp.sin(ang)[:, None, None, :]
    xf = x.astype(jnp.float32)
    x1, x2 = xf[..., :HEAD_DIM // 2], xf[..., HEAD_DIM // 2:]
    out = jnp.concatenate([x1 * cos - x2 * sin, x2 * cos + x1 * sin], axis=-1)
    return out.astype(x.dtype)


def diff_attention(q, k, v, lam):
    B, S = q.shape[0], q.shape[1]
    nb = S // Q_BLOCK
    qt = q.transpose(0, 2, 3, 1, 4)
    kt = k.transpose(0, 2, 3, 1, 4)
    vt = v.transpose(0, 2, 1, 3)
    qb = qt.reshape(B, N_HEADS, 2, nb, Q_BLOCK, HEAD_DIM).transpose(3, 0, 1, 2, 4, 5)
    scale = HEAD_DIM ** -0.5

    def block(qi):
        s = jnp.einsum('bhcqd,bhckd->bhcqk', qi, kt).astype(jnp.float32) * scale
        pr = jax.nn.softmax(s, axis=-1)
        w = pr[:, :, 0] - lam * pr[:, :, 1]
        return jnp.einsum('bhqk,bhkv->bhqv', w.astype(vt.dtype), vt)

    o = lax.map(block, qb)
    return o.transpose(1, 0, 3, 2, 4).reshape(B, S, N_HEADS, V_DIM)


def linear_scan(a, b, reverse):
    def comb(l, r):
        return (l[0] * r[0], r[0] * l[1] + r[1])
    _, h = lax.associative_scan(comb, (a, b), reverse=reverse, axis=1)
    return h


def rg_lru_direction(x, wa, ba, wx, bx, lam, reverse):
    B, S, _ = x.shape
    xb = x.reshape(B, S, N_LRU_BLOCKS, LRU_BLOCK)
    r = jax.nn.sigmoid(jnp.einsum('bsni,nij->bsnj', xb, wa.astype(jnp.float32)).reshape(B, S, LRU_W)
                       + ba.astype(jnp.float32))
    i = jax.nn.sigmoid(jnp.einsum('bsni,nij->bsnj', xb, wx.astype(jnp.float32)).reshape(B, S, LRU_W)
                       + bx.astype(jnp.float32))
    log_a = -LRU_C * r * jax.nn.softplus(-lam.astype(jnp.float32))
    a = jnp.exp(log_a)
    b = jnp.sqrt(-jnp.expm1(2.0 * log_a)) * (i * x)
    return linear_scan(a, b, reverse)


def rg_lru_bidir(x, wa, ba, wx, bx, lam):
    xf = x.astype(jnp.float32)
    h = (rg_lru_direction(xf, wa[0], ba[0], wx[0], bx[0], lam[0], False)
         + rg_lru_direction(xf, wa[1], ba[1], wx[1], bx[1], lam[1], True))
    return h.astype(x.dtype)


def layer(x, p, lp, lam_init):
    (norm_mix_pre, norm_mix_post, w_in, lam_q1, lam_k1, lam_q2, lam_k2, subln_g,
     lru_conv_w, lru_conv_b, rg_wa, rg_ba, rg_wx, rg_bx, rg_lambda,
     w_branch_attn, w_branch_lru, w_out, norm_ffn_pre, norm_ffn_post,
     w_ffn_up, ffn_conv_w, ffn_conv_b, w_ffn_down, w_ple_proj, w_ple_gate) = lp
    B, S, _ = x.shape
    h = rms_norm(x, norm_mix_pre)
    z = h @ w_in
    q, k, v, u, gl, ga, gb = jnp.split(z, SPLITS, axis=-1)
    q = rope(q.reshape(B, S, N_HEADS, 2, HEAD_DIM))
    k = rope(k.reshape(B, S, N_HEADS, 2, HEAD_DIM))
    v = v.reshape(B, S, N_HEADS, V_DIM)
    lam = (jnp.exp(jnp.sum(lam_q1.astype(jnp.float32) * lam_k1.astype(jnp.float32)))
           - jnp.exp(jnp.sum(lam_q2.astype(jnp.float32) * lam_k2.astype(jnp.float32)))
           + lam_init)
    o = diff_attention(q, k, v, lam)
    o_attn = (rms_norm(o, subln_g) * (1.0 - lam_init)).reshape(B, S, ATT_W)
    u = dwconv(u, lru_conv_w, lru_conv_b, LRU_CONV_LEFT)
    o_lru = jax.nn.gelu(gl) * rg_lru_bidir(u, rg_wa, rg_ba, rg_wx, rg_bx, rg_lambda)
    m = jax.nn.sigmoid(ga) * (o_attn @ w_branch_attn) + jax.nn.sigmoid(gb) * (o_lru @ w_branch_lru)
    x = x + rms_norm(m @ w_out, norm_mix_post)
    h = rms_norm(x, norm_ffn_pre)
    c, lin = jnp.split(h @ w_ffn_up, 2, axis=-1)
    c = dwconv(c, ffn_conv_w, ffn_conv_b, FFN_CONV_LEFT)
    f = (jax.nn.gelu(c) * lin) @ w_ffn_down
    x = x + rms_norm(f, norm_ffn_post)
    x = x + jax.nn.sigmoid(x @ w_ple_gate) * (p @ w_ple_proj)
    return x


def setup_inputs(seed: int = 0) -> dict:
    key = jax.random.key(seed)
    ks = jax.random.split(key, 40)
    f32 = jnp.float32
    L = DEPTH

    def nrm(k, shape, scale):
        return jax.random.normal(k, shape, f32) * scale

    def gain(k, shape):
        return 1.0 + 0.05 * jax.random.normal(k, shape, f32)

    a0 = jax.random.uniform(ks[14], (L, 2, LRU_W), f32, 0.9, 0.999)
    s0 = a0 ** (1.0 / LRU_C)
    rg_lambda = jnp.log(s0) - jnp.log1p(-s0)
    return {
        "x_prompt": nrm(ks[0], (BATCH, SEQ, D_MODEL), 1.0),
        "x_sample": nrm(ks[1], (DEC_BATCH, DEC_SEQ, D_MODEL), 1.0),
        "p_prompt": nrm(ks[2], (DEPTH, BATCH, SEQ, PLE_DIM), 1.0),
        "p_sample": nrm(ks[3], (DEPTH, DEC_BATCH, DEC_SEQ, PLE_DIM), 1.0),
        "norm_mix_pre": gain(ks[4], (L, D_MODEL)),
        "norm_mix_post": gain(ks[5], (L, D_MODEL)),
        "w_in": nrm(ks[6], (L, D_MODEL, IN_COLS), D_MODEL ** -0.5),
        "lam_q1": nrm(ks[7], (L, HEAD_DIM), 0.1),
        "lam_k1": nrm(ks[8], (L, HEAD_DIM), 0.1),
        "lam_q2": nrm(ks[9], (L, HEAD_DIM), 0.1),
        "lam_k2": nrm(ks[10], (L, HEAD_DIM), 0.1),
        "subln_g": gain(ks[11], (L, V_DIM)),
        "lru_conv_w": nrm(ks[12], (L, LRU_CONV, LRU_W), LRU_CONV ** -0.5),
        "lru_conv_b": nrm(ks[13], (L, LRU_W), 0.02),
        "rg_wa": nrm(ks[15], (L, 2, N_LRU_BLOCKS, LRU_BLOCK, LRU_BLOCK), LRU_BLOCK ** -0.5),
        "rg_ba": nrm(ks[16], (L, 2, LRU_W), 0.02),
        "rg_wx": nrm(ks[17], (L, 2, N_LRU_BLOCKS, LRU_BLOCK, LRU_BLOCK), LRU_BLOCK ** -0.5),
        "rg_bx": nrm(ks[18], (L, 2, LRU_W), 0.02),
        "rg_lambda": rg_lambda,
        "w_branch_attn": nrm(ks[19], (L, ATT_W, D_MODEL), ATT_W ** -0.5),
        "w_branch_lru": nrm(ks[20], (L, LRU_W, D_MODEL), LRU_W ** -0.5),
        "w_out": nrm(ks[21], (L, D_MODEL, D_MODEL), D_MODEL ** -0.5),
        "norm_ffn_pre": gain(ks[22], (L, D_MODEL)),
        "norm_ffn_post": gain(ks[23], (L, D_MODEL)),
        "w_ffn_up": nrm(ks[24], (L, D_MODEL, 2 * D_FF), D_MODEL ** -0.5),
        "ffn_conv_w": nrm(ks[25], (L, FFN_CONV, D_FF), FFN_CONV ** -0.5),
        "ffn_conv_b": nrm(ks[26], (L, D_FF), 0.02),
        "w_ffn_down": nrm(ks[27], (L, D_FF, D_MODEL), D_FF ** -0.5),
        "w_ple_proj": nrm(ks[28], (L, PLE_DIM, D_MODEL), PLE_DIM ** -0.5),
        "w_ple_gate": nrm(ks[29], (L, D_MODEL, D_MODEL), D_MODEL ** -0.5),
    }


def reference(x_prompt, x_sample, p_prompt, p_sample, norm_mix_pre, norm_mix_post, w_in,
              lam_q1, lam_k1, lam_q2, lam_k2, subln_g, lru_conv_w, lru_conv_b,
              rg_wa, rg_ba, rg_wx, rg_bx, rg_lambda, w_branch_attn, w_branch_lru, w_out,
              norm_ffn_pre, norm_ffn_post, w_ffn_up, ffn_conv_w, ffn_conv_b, w_ffn_down,
              w_ple_proj, w_ple_gate):
    xp = x_prompt
    xs = x_sample
    for i in range(DEPTH):
        lp = (norm_mix_pre[i], norm_mix_post[i], w_in[i], lam_q1[i], lam_k1[i], lam_q2[i],
              lam_k2[i], subln_g[i], lru_conv_w[i], lru_conv_b[i], rg_wa[i], rg_ba[i],
              rg_wx[i], rg_bx[i], rg_lambda[i], w_branch_attn[i], w_branch_lru[i], w_out[i],
              norm_ffn_pre[i], norm_ffn_post[i], w_ffn_up[i], ffn_conv_w[i], ffn_conv_b[i],
              w_ffn_down[i], w_ple_proj[i], w_ple_gate[i])
        lam_init = 0.8 - 0.6 * math.exp(-0.3 * i)
        xp = layer(xp, p_prompt[i], lp, lam_init)
        xs = layer(xs, p_sample[i], lp, lam_init)
    return (xp, xs)
```

```python
import math
from contextlib import ExitStack
import numpy as np
import concourse.bass as bass
import concourse.mybir as mybir
from concourse.bass_utils import run_bass_kernel_spmd

F32 = mybir.dt.float32
BF16 = mybir.dt.bfloat16
AF = mybir.ActivationFunctionType
ALU = mybir.AluOpType

D = 1024
NH = 8
DFF = 3072
PLE = 256
INC = 7168
EPS = 1e-6
LRU_C = 8.0
THETA = 10000.0
TT = 512
UNROLL = True


class Sem:
    def __init__(self, handle, name):
        self.h = handle
        self.name = name
        self.const = 0


class Frame:
    def __init__(self, n):
        self.n = n
        self.per = None
        self.ivar = None


class Ev:
    __slots__ = ("sem", "const", "frames")

    def __init__(self, sem, const, frames):
        self.sem = sem
        self.const = const
        self.frames = frames


class Slot:
    def __init__(self, P, name):
        self.name = name
        self.w = None
        self.r = {}
        self.dsem = None
        P.slots.append(self)


class Tile:
    def __init__(self, P, t, name):
        self.t = t
        self.s = Slot(P, name)


class Prog:
    def __init__(self, nc, es):
        self.nc = nc
        self.es = es
        self.dry = 0
        self.frames = []
        self.slots = []
        self.sems = []
        self.eng = {"pe": nc.tensor, "act": nc.scalar, "dve": nc.vector, "pool": nc.gpsimd, "sp": nc.sync}
        self.esem = {k: self.new_sem("e_" + k) for k in ("pe", "act", "dve", "pool")}
        self.waited = {}
        self.free_dsems = []

    def alloc_dsem(self):
        if self.free_dsems:
            return self.free_dsems.pop()
        return self.new_sem("d_%d" % len(self.sems))

    def phase_begin(self):
        return len(self.slots)

    def phase_end(self, mark):
        self.barrier()
        for sl in self.slots[mark:]:
            if sl.dsem is not None:
                self.free_dsems.append(sl.dsem)
                sl.dsem = None
        del self.slots[mark:]

    def new_sem(self, name):
        h = self.es.enter_context(self.nc.semaphore(name))
        s = Sem(h, name)
        self.sems.append(s)
        return s

    def tile(self, es, name, shape, dtype, psum=False):
        self.ntile = getattr(self, "ntile", 0) + 1
        name = "t%d_%s" % (self.ntile, name)
        if psum:
            t = es.enter_context(self.nc.psum_tensor(name, shape, dtype))
        else:
            t = es.enter_context(self.nc.sbuf_tensor(name, shape, dtype))
        return Tile(self, t, name)

    def _val(self, ev):
        v = ev.const
        for f in ev.frames:
            c = f.per.get(ev.sem.name, 0)
            if c:
                v = f.ivar * c + v
        return v

    def _wait(self, eng, ev):
        if ev is None:
            return
        if eng == "pe" and ev.sem is self.esem["pe"]:
            return
        key = (eng, ev.sem.name)
        last = self.waited.get(key)
        if last is not None and last[1] == ev.frames and last[0] >= ev.const:
            return
        self.waited[key] = (ev.const, ev.frames)
        if not self.dry:
            self.eng[eng].wait_ge(ev.sem.h, self._val(ev))

    def _deps(self, eng, reads, writes):
        for sl in reads:
            self._wait(eng, sl.w)
        for sl in writes:
            self._wait(eng, sl.w)
            for ev in list(sl.r.values()):
                self._wait(eng, ev)

    def _update(self, key, ev, reads, writes):
        for sl in writes:
            sl.w = ev
            sl.r = {}
        for sl in reads:
            if sl not in writes:
                sl.r[key] = ev

    def op(self, eng, fn, reads=(), writes=()):
        self._deps(eng, reads, writes)
        s = self.esem[eng]
        s.const += 1
        if not self.dry:
            fn(self.eng[eng]).then_inc(s.h, 1)
        ev = Ev(s, s.const, tuple(self.frames))
        self._update(eng, ev, reads, writes)
        return ev

    def mm(self, fns, reads=(), writes=()):
        self._deps("pe", reads, writes)
        s = self.esem["pe"]
        s.const += 1
        if not self.dry:
            ins = None
            for fn in fns:
                ins = fn(self.nc.tensor)
            ins.then_inc(s.h, 1)
        ev = Ev(s, s.const, tuple(self.frames))
        self._update("pe", ev, reads, writes)
        return ev

    def dma(self, q, fn, slot, reads=(), writes=(), **kw):
        self._deps(q, reads, writes)
        if slot.dsem is None:
            slot.dsem = self.alloc_dsem()
        s = slot.dsem
        s.const += 16
        if not self.dry:
            o, i = fn()
            try:
                self.eng[q].dma_start(out=o, in_=i, **kw).then_inc(s.h, 16)
            except Exception:
                print("DMA FAIL", q, o, i)
                raise
        ev = Ev(s, s.const, tuple(self.frames))
        self._update(("dma", s.name), ev, reads, writes)
        return ev

    def barrier(self):
        if self.dry:
            return
        fr = tuple(self.frames)
        for eng in ("pe", "act", "dve", "pool", "sp"):
            for s in self.sems:
                if s.const == 0 and not any(f.per.get(s.name, 0) for f in fr):
                    continue
                self.eng[eng].wait_ge(s.h, self._val(Ev(s, s.const, fr)))
        self.waited = {}

    def loop(self, n, body):
        if getattr(self, "unroll", False):
            for it in range(n):
                body(it)
            return
        fr = Frame(n)
        snap = {s.name: s.const for s in self.sems}
        nsem0 = len(self.sems)
        self.frames.append(fr)
        self.dry += 1
        self.waited = {}
        body(None)
        self.dry -= 1
        self.frames.pop()
        fr.per = {s.name: s.const - snap.get(s.name, 0) for s in self.sems}
        final = {sl: (sl.w, dict(sl.r)) for sl in self.slots}
        for s in self.sems:
            s.const = snap.get(s.name, 0)

        def settle(ev):
            if ev is None or fr not in ev.frames:
                return ev
            return Ev(ev.sem, ev.const + (n - 1) * fr.per[ev.sem.name], tuple(f for f in ev.frames if f is not fr))

        def finish():
            for s in self.sems:
                s.const = snap.get(s.name, 0) + n * fr.per[s.name]
            for sl, (w, r) in final.items():
                sl.w = settle(w)
                sl.r = {k: settle(e) for k, e in r.items()}
            self.waited = {}

        if self.dry:
            finish()
            return
        self.barrier()

        def prev(ev):
            if ev is None or fr not in ev.frames:
                return None
            return Ev(ev.sem, ev.const - fr.per[ev.sem.name], ev.frames)

        with self.nc.Fori(0, n) as i:
            fr.ivar = i
            self.frames.append(fr)
            for sl, (w, r) in final.items():
                sl.w = prev(w)
                sl.r = {k: e2 for k, e2 in ((k, prev(e)) for k, e in r.items()) if e2 is not None}
            self.waited = {}
            body(i)
            self.frames.pop()
        finish()


class Cfg:
    def __init__(self, n_prompt, s_p, s_s, L):
        self.n_prompt, self.s_p, self.s_s, self.L = n_prompt, s_p, s_s, L
        self.seqs = [(k * s_p, s_p) for k in range(n_prompt)] + [(n_prompt * s_p, s_s)]
        self.NT = n_prompt * s_p + s_s
        self.nseq = n_prompt + 1
        self.NTP = self.NT + 2 * self.nseq


def build(cfg, debug=False):
    nc = bass.Bass("TRN2", target_bir_lowering=False)
    NT, L = cfg.NT, cfg.L
    ntile = NT // TT

    def din(name, shape, dt=F32):
        return nc.dram_tensor(name, shape, dt, kind="ExternalInput").ap()

    def dscr(name, shape, dt):
        if debug:
            return nc.dram_tensor(name, shape, dt, kind="ExternalOutput").ap()
        return nc.dram_tensor(name, shape, dt).ap()

    x_in = din("x", [NT, D])
    p_in = din("p", [L, NT, PLE])
    ropeC = din("ropeC", [128, NT])
    ropeS = din("ropeS", [128, NT])
    ident_in = din("ident", [128, 128])
    w_in = din("w_in", [L, D, INC])
    g_pre = din("norm_mix_pre", [L, D])
    g_post = din("norm_mix_post", [L, D])
    lamv = din("lamv", [L, 4, 64])
    subln = din("subln_g", [L, 128])
    lruprm = din("lruprm", [L, 8, 128, 11])
    rgw = din("rgw", [L, 8, 4, 128, 128])
    w_ba = din("w_branch_attn", [L, D, D])
    w_bl = din("w_branch_lru", [L, D, D])
    w_out = din("w_out", [L, D, D])
    g_fpre = din("norm_ffn_pre", [L, D])
    g_fpost = din("norm_ffn_post", [L, D])
    w_up = din("w_ffn_up", [L, D, 2 * DFF])
    ffnprm = din("ffnprm", [L, 128, 24, 4])
    w_dn = din("w_ffn_down", [L, DFF, D])
    w_pp = din("w_ple_proj", [L, PLE, D])
    w_pg = din("w_ple_gate", [L, D, D])
    y_out = nc.dram_tensor("y", [NT, D], F32, kind="ExternalOutput").ap()

    QT = dscr("QT", [D, NT], BF16)
    KT = dscr("KT", [D, NT], BF16)
    VV = dscr("VV", [NT, D], BF16)
    UT = dscr("UT", [D, NT], F32)
    GLT = dscr("GLT", [D, NT], BF16)
    TGA = dscr("TGA", [D, NT], BF16)
    TGB = dscr("TGB", [D, NT], BF16)
    OAT = dscr("OAT", [D, NT], BF16)
    OLT = dscr("OLT", [D, NT], BF16)
    X1 = dscr("X1", [NT, D], F32)
    H2T = dscr("H2T", [D, cfg.NTP], BF16)
    GT = dscr("GT", [DFF, NT], BF16)
    XS = dscr("XS", [NT, D], F32)

    top = ExitStack()
    with top:
        P = Prog(nc, top)
        P.unroll = UNROLL
        ident = P.tile(top, "ident", [128, 128], BF16)
        ones = P.tile(top, "ones", [128, 128], BF16)
        mhalf = P.tile(top, "mhalf", [128, 512], F32)
        zcol = P.tile(top, "zcol", [128, 8, 2], BF16)
        P.dma("pool", lambda: (ident.t[:], ident_in[:, :]), ident.s, writes=[ident.s])
        P.op("dve", lambda e: e.memset(ones.t[:], 1.0), writes=[ones.s])
        P.op("pool", lambda e: e.memset(mhalf.t[:], -0.5), writes=[mhalf.s])
        P.op("dve", lambda e: e.memset(zcol.t[:], 0.0), writes=[zcol.s])
        for k, (st, S) in enumerate(cfg.seqs):
            c0 = st + 2 * k
            for cc in (c0, c0 + S + 1):
                P.dma("sp", lambda cc=cc: (H2T[:, cc:cc + 1].rearrange("(c p) t -> p c t", p=128), zcol.t[:, :, 0:1]),
                      zcol.s, reads=[zcol.s], allow_slow_non_contiguous=True)

        def rms_to_bf16(src, dst, gt, ms, rs, junk, eps=EPS):
            for b in range(4):
                P.op("act", lambda e, b=b: e.activation(out=junk.t[:], in_=src.t[:, b, :], func=AF.Square,
                                                         scale=1.0 / 32.0, accum_out=ms.t[:, b:b + 1]),
                     reads=[src.s], writes=[junk.s, ms.s])
            P.op("pool", lambda e: e.tensor_scalar(out=rs.t[:], in0=ms.t[:], scalar1=eps, scalar2=None, op0=ALU.add),
                 reads=[ms.s], writes=[rs.s])
            P.op("pool", lambda e: e.tensor_tensor(out=rs.t[:], in0=rs.t[:], in1=mhalf.t[:, 0:4], op=ALU.pow),
                 reads=[mhalf.s], writes=[rs.s])
            for b in range(4):
                P.op("dve", lambda e, b=b: e.scalar_tensor_tensor(out=dst.t[:, b, :], in0=src.t[:, b, :],
                                                                   scalar=rs.t[:, b:b + 1], in1=gt.t[:],
                                                                   op0=ALU.mult, op1=ALU.mult),
                     reads=[src.s, rs.s, gt.s], writes=[dst.s])

        def transpose_blocks(src, nchunk, dstT, pTs, cnt):
            for fc in range(nchunk):
                pT = pTs[cnt[0] % len(pTs)]
                P.mm([lambda e, b=b, fc=fc, pT=pT: e.transpose(out=pT.t[:, b * 128:(b + 1) * 128],
                                                              in_=src.t[:, b, fc * 128:(fc + 1) * 128],
                                                              identity=ident.t[:]) for b in range(4)],
                     reads=[src.s, ident.s], writes=[pT.s])
                eng = "act" if cnt[0] % 2 == 0 else "dve"
                if eng == "act":
                    P.op("act", lambda e, fc=fc, pT=pT: e.activation(out=dstT.t[:, fc, :], in_=pT.t[:], func=AF.Copy),
                         reads=[pT.s], writes=[dstT.s])
                else:
                    P.op("dve", lambda e, fc=fc, pT=pT: e.tensor_copy(out=dstT.t[:, fc, :], in_=pT.t[:]),
                         reads=[pT.s], writes=[dstT.s])
                cnt[0] += 1

        def load_w(Wt, src, nk, eng="pool"):
            for kc in range(nk):
                P.dma(eng, lambda kc=kc: (Wt.t[:, kc, :], src[kc * 128:(kc + 1) * 128, :]), Wt.s, writes=[Wt.s])

        def load_bcast(Tt, vec):
            P.dma("sp", lambda: (Tt.t[:], vec.partition_broadcast(128)), Tt.s, writes=[Tt.s])

        def phase1(l, Xsrc):
            mark_ = P.phase_begin()
            with ExitStack() as es:
                W = P.tile(es, "w1", [128, 8, INC], BF16)
                g1 = P.tile(es, "g1", [128, D], F32)
                xt = P.tile(es, "xt1", [128, 4, D], F32)
                h = P.tile(es, "h1", [128, 4, D], BF16)
                hT = P.tile(es, "hT1", [128, 8, TT], BF16)
                junk = P.tile(es, "junk1", [128, D], BF16)
                ms = P.tile(es, "ms1", [128, 4], F32)
                rs = P.tile(es, "rs1", [128, 4], F32)
                cs = P.tile(es, "cos1", [128, TT], F32)
                sn = P.tile(es, "sin1", [128, TT], F32)
                fa = P.tile(es, "fa1", [128, TT], F32)
                fb = P.tile(es, "fb1", [128, TT], F32)
                t1 = P.tile(es, "t11", [128, TT], F32)
                t2 = P.tile(es, "t21", [128, TT], F32)
                t3 = P.tile(es, "t31", [128, TT], F32)
                t4 = P.tile(es, "t41", [128, TT], F32)
                stg = [P.tile(es, "stg1_%d" % k, [128, TT], BF16) for k in range(4)]
                stf = [P.tile(es, "stf1_%d" % k, [128, TT], F32) for k in range(2)]
                pTs = [P.tile(es, "pT1_%d" % k, [128, TT], BF16, psum=True) for k in range(2)]
                acc = [P.tile(es, "acc1_%d" % k, [128, TT], F32, psum=True) for k in range(6)]
                load_w(W, w_in[l], 8)
                load_bcast(g1, g_pre[l])

                def body(i):
                    tok0 = None if i is None else i * TT
                    P.dma("sp", lambda: (xt.t[:], Xsrc[bass.ds(tok0, TT), :].rearrange("(b p) d -> p b d", p=128)),
                          xt.s, writes=[xt.s])
                    P.dma("sp", lambda: (cs.t[:], ropeC[:, bass.ds(tok0, TT)]), cs.s, writes=[cs.s])
                    P.dma("sp", lambda: (sn.t[:], ropeS[:, bass.ds(tok0, TT)]), sn.s, writes=[sn.s])
                    rms_to_bf16(xt, h, g1, ms, rs, junk)
                    cnt = [0]
                    transpose_blocks(h, 8, hT, pTs, cnt)
                    na = [0]
                    ns = [0]
                    nf = [0]

                    def group_fm(col0):
                        a = acc[na[0] % len(acc)]
                        na[0] += 1
                        P.mm([lambda e, kc=kc, a=a: e.matmul(a.t[:], lhsT=W.t[:, kc, col0:col0 + 128], rhs=hT.t[:, kc, :],
                                                             start=(kc == 0), stop=(kc == 7)) for kc in range(8)],
                             reads=[W.s, hT.s], writes=[a.s])
                        return a

                    def next_stg():
                        s_ = stg[ns[0] % len(stg)]
                        ns[0] += 1
                        return s_

                    for which, dst in ((0, QT), (1, KT)):
                        for g in range(4):
                            base = which * D + g * 256
                            pa = group_fm(base)
                            pb = group_fm(base + 128)
                            P.op("act", lambda e, pa=pa: e.activation(out=fa.t[:], in_=pa.t[:], func=AF.Copy),
                                 reads=[pa.s], writes=[fa.s])
                            P.op("act", lambda e, pb=pb: e.activation(out=fb.t[:], in_=pb.t[:], func=AF.Copy),
                                 reads=[pb.s], writes=[fb.s])
                            P.op("dve", lambda e: e.tensor_tensor(out=t1.t[:], in0=fa.t[:], in1=cs.t[:], op=ALU.mult),
                                 reads=[fa.s, cs.s], writes=[t1.s])
                            P.op("dve", lambda e: e.tensor_tensor(out=t2.t[:], in0=fb.t[:], in1=sn.t[:], op=ALU.mult),
                                 reads=[fb.s, sn.s], writes=[t2.s])
                            P.op("dve", lambda e: e.tensor_tensor(out=t3.t[:], in0=fb.t[:], in1=cs.t[:], op=ALU.mult),
                                 reads=[fb.s, cs.s], writes=[t3.s])
                            P.op("dve", lambda e: e.tensor_tensor(out=t4.t[:], in0=fa.t[:], in1=sn.t[:], op=ALU.mult),
                                 reads=[fa.s, sn.s], writes=[t4.s])
                            sa = next_stg()
                            P.op("dve", lambda e, sa=sa: e.tensor_tensor(out=sa.t[:], in0=t1.t[:], in1=t2.t[:], op=ALU.subtract),
                                 reads=[t1.s, t2.s], writes=[sa.s])
                            sb = next_stg()
                            P.op("dve", lambda e, sb=sb: e.tensor_tensor(out=sb.t[:], in0=t3.t[:], in1=t4.t[:], op=ALU.add),
                                 reads=[t3.s, t4.s], writes=[sb.s])
                            for half, st_ in ((0, sa), (1, sb)):
                                for m in range(4):
                                    r0 = (4 * g + m) * 64 + half * 32
                                    P.dma("sp", lambda st_=st_, m=m, r0=r0, dst=dst: (
                                        dst[r0:r0 + 32, bass.ds(tok0, TT)], st_.t[m * 32:(m + 1) * 32, :]),
                                        st_.s, reads=[st_.s])
                    for tb in range(4):
                        for hf in range(2):
                            a = acc[na[0] % len(acc)]
                            na[0] += 1
                            P.mm([lambda e, kc=kc, a=a, tb=tb, hf=hf: e.matmul(
                                a.t[:], lhsT=hT.t[:, kc, tb * 128:(tb + 1) * 128],
                                rhs=W.t[:, kc, 2 * D + hf * 512:2 * D + (hf + 1) * 512],
                                start=(kc == 0), stop=(kc == 7)) for kc in range(8)],
                                reads=[W.s, hT.s], writes=[a.s])
                            s_ = next_stg()
                            P.op("act", lambda e, a=a, s_=s_: e.activation(out=s_.t[:], in_=a.t[:], func=AF.Copy),
                                 reads=[a.s], writes=[s_.s])
                            P.dma("sp", lambda s_=s_, tb=tb, hf=hf: (
                                VV[bass.ds(tok0 + tb * 128, 128), hf * 512:(hf + 1) * 512], s_.t[:]), s_.s, reads=[s_.s])
                    for oc in range(8):
                        a = group_fm(3 * D + oc * 128)
                        f_ = stf[nf[0] % 2]
                        nf[0] += 1
                        P.op("dve", lambda e, a=a, f_=f_: e.tensor_copy(out=f_.t[:], in_=a.t[:]), reads=[a.s], writes=[f_.s])
                        P.dma("sp", lambda f_=f_, oc=oc: (UT[oc * 128:(oc + 1) * 128, bass.ds(tok0, TT)], f_.t[:]),
                              f_.s, reads=[f_.s])
                    for sec, dst, fn, sc in ((4, GLT, AF.Gelu_apprx_tanh, 1.0), (5, TGA, AF.Tanh, 0.5), (6, TGB, AF.Tanh, 0.5)):
                        for oc in range(8):
                            a = group_fm(sec * D + oc * 128)
                            s_ = next_stg()
                            P.op("act", lambda e, a=a, s_=s_, fn=fn, sc=sc: e.activation(out=s_.t[:], in_=a.t[:], func=fn, scale=sc),
                                 reads=[a.s], writes=[s_.s])
                            P.dma("sp", lambda s_=s_, oc=oc, dst=dst: (dst[oc * 128:(oc + 1) * 128, bass.ds(tok0, TT)], s_.t[:]),
                                  s_.s, reads=[s_.s])

                P.loop(ntile, body)
                P.phase_end(mark_)

        def phase2(l, lam_init):
            mark_ = P.phase_begin()
            Smax = max(cfg.s_p, cfg.s_s)
            with ExitStack() as es:
                KThs = [P.tile(es, "kth%d" % k, [128, Smax], BF16) for k in range(2)]
                QThs = [P.tile(es, "qth%d" % k, [128, Smax], BF16) for k in range(2)]
                Vhs = [P.tile(es, "vh%d" % k, [128, Smax // 128, 128], BF16) for k in range(2)]
                ET = [P.tile(es, "et%d" % k, [128, 1024], BF16) for k in range(3)]
                sacc = P.tile(es, "sacc", [128, 1024], F32)
                onesf = P.tile(es, "onesf", [128, 128], F32)
                P.op("dve", lambda e: e.memset(onesf.t[:], 1.0), writes=[onesf.s])
                lv = P.tile(es, "lv", [128, 4, 64], F32)
                lj = P.tile(es, "lj", [128, 64], F32)
                ls = P.tile(es, "ls", [128, 2], F32)
                nlam = P.tile(es, "nlam", [128, 1], F32)
                gs = P.tile(es, "gs", [128, 1], F32)
                epsc = P.tile(es, "epsc", [128, 1], F32)
                P.op("dve", lambda e: e.memset(epsc.t[:], EPS), writes=[epsc.s])
                rs0 = P.tile(es, "rs0", [128, TT], F32)
                rs1 = P.tile(es, "rs1", [128, TT], F32)
                o0 = P.tile(es, "o0", [128, TT], F32)
                o1 = P.tile(es, "o1", [128, TT], F32)
                o2 = P.tile(es, "o2", [128, TT], F32)
                sq = P.tile(es, "sq", [128, TT], BF16)
                vv = P.tile(es, "vv2", [128, TT], F32)
                on = P.tile(es, "on", [128, TT], F32)
                oa = P.tile(es, "oa", [128, TT], BF16)
                ST = [P.tile(es, "st%d" % k, [128, 1024], F32, psum=True) for k in range(2)]
                O = [P.tile(es, "oacc%d" % k, [128, TT], F32, psum=True) for k in range(2)]
                SM = [P.tile(es, "sacc%d" % k, [128, TT], F32, psum=True) for k in range(2)]
                P.dma("sp", lambda: (lv.t[:], lamv[l].partition_broadcast(128)), lv.s, writes=[lv.s])
                for k in range(2):
                    P.op("dve", lambda e, k=k: e.scalar_tensor_tensor(out=lj.t[:], in0=lv.t[:, 2 * k, :], scalar=1.0,
                                                                       in1=lv.t[:, 2 * k + 1, :], op0=ALU.mult, op1=ALU.mult,
                                                                       accum_out=ls.t[:, k:k + 1]),
                         reads=[lv.s], writes=[lj.s, ls.s])
                P.op("act", lambda e: e.activation(out=ls.t[:], in_=ls.t[:], func=AF.Exp), writes=[ls.s])
                P.op("dve", lambda e: e.tensor_tensor(out=nlam.t[:], in0=ls.t[:, 1:2], in1=ls.t[:, 0:1], op=ALU.subtract),
                     reads=[ls.s], writes=[nlam.s])
                P.op("dve", lambda e: e.tensor_scalar(out=nlam.t[:], in0=nlam.t[:], scalar1=-lam_init, scalar2=None, op0=ALU.add),
                     writes=[nlam.s])
                P.dma("sp", lambda: (gs.t[:], subln[l].rearrange("(p o) -> p o", o=1)), gs.s, writes=[gs.s])
                P.op("dve", lambda e: e.tensor_scalar(out=gs.t[:], in0=gs.t[:], scalar1=1.0 - lam_init, scalar2=None, op0=ALU.mult),
                     writes=[gs.s])

                def head_body(S, tokf):
                    nkb = S // 128
                    nqt = S // TT

                    def body(hi):
                        tok0 = None if hi is None else tokf()
                        def loads(h):
                            K_, Q_, V_ = KThs[h % 2], QThs[h % 2], Vhs[h % 2]
                            P.dma("sp", lambda: (K_.t[:, :S], KT[bass.ds(h * 128, 128), bass.ds(tok0, S)]), K_.s, writes=[K_.s])
                            P.dma("sp", lambda: (Q_.t[:, :S], QT[bass.ds(h * 128, 128), bass.ds(tok0, S)]), Q_.s, writes=[Q_.s])
                            P.dma("sp", lambda: (V_.t[:, :nkb, :],
                                                 VV[bass.ds(tok0, S), bass.ds(h * 128, 128)].rearrange("(kb p) v -> p kb v", p=128)),
                                  V_.s, writes=[V_.s])

                        if hi == 0:
                            loads(0)
                        if hi + 1 < NH:
                            loads(hi + 1)
                        KTh, QTh, Vh = KThs[hi % 2], QThs[hi % 2], Vhs[hi % 2]
                        step = 0
                        for qt in range(nqt):
                            qs = slice(qt * TT, (qt + 1) * TT)

                            def qk(kb, st):
                                ks = slice(kb * 128, (kb + 1) * 128)
                                P.mm([lambda e: e.matmul(st.t[:, 0:512], lhsT=KTh.t[0:64, ks], rhs=QTh.t[0:64, qs], start=True, stop=True),
                                      lambda e: e.matmul(st.t[:, 512:1024], lhsT=KTh.t[64:128, ks], rhs=QTh.t[64:128, qs], start=True, stop=True)],
                                     reads=[KTh.s, QTh.s], writes=[st.s])

                            def pv(kb, et):
                                fns = []
                                for mp in range(2):
                                    fns.append(lambda e, mp=mp: e.matmul(
                                        O[mp].t[:], lhsT=Vh.t[:, kb, :], rhs=et.t[:, mp * 512:(mp + 1) * 512],
                                        start=(kb == 0), stop=(kb == nkb - 1)))
                                fns.append(lambda e: e.matmul(SM[1].t[:], lhsT=ones.t[:], rhs=et.t[:, 512:1024],
                                                              start=(kb == 0), stop=(kb == nkb - 1)))
                                P.mm(fns, reads=[Vh.s, et.s, ones.s], writes=[O[0].s, O[1].s, SM[1].s])
                                if kb == 0:
                                    P.op("dve", lambda e: e.tensor_copy(out=sacc.t[:, 0:512], in_=et.t[:, 0:512]), reads=[et.s], writes=[sacc.s])
                                else:
                                    P.op("dve", lambda e: e.tensor_tensor(out=sacc.t[:, 0:512], in0=sacc.t[:, 0:512], in1=et.t[:, 0:512], op=ALU.add),
                                         reads=[et.s], writes=[sacc.s])

                            base = step
                            qk(0, ST[base % 2])
                            for kb in range(nkb):
                                st = ST[(base + kb) % 2]
                                et = ET[(base + kb) % 3]
                                P.op("act", lambda e, st=st, et=et: e.activation(out=et.t[:], in_=st.t[:], func=AF.Exp, scale=0.125),
                                     reads=[st.s], writes=[et.s])
                                if kb + 1 < nkb:
                                    qk(kb + 1, ST[(base + kb + 1) % 2])
                                pv(kb, et)
                            step = base + nkb
                            P.mm([lambda e: e.matmul(SM[0].t[:], lhsT=onesf.t[:], rhs=sacc.t[:, 0:512], start=True, stop=True)],
                                 reads=[onesf.s, sacc.s], writes=[SM[0].s])
                            P.op("dve", lambda e: e.reciprocal(out=rs0.t[:], in_=SM[0].t[:]), reads=[SM[0].s], writes=[rs0.s])
                            P.op("dve", lambda e: e.tensor_tensor(out=o0.t[:], in0=O[0].t[:], in1=rs0.t[:], op=ALU.mult),
                                 reads=[O[0].s, rs0.s], writes=[o0.s])
                            P.op("dve", lambda e: e.reciprocal(out=rs1.t[:], in_=SM[1].t[:]), reads=[SM[1].s], writes=[rs1.s])
                            P.op("dve", lambda e: e.tensor_tensor(out=o1.t[:], in0=O[1].t[:], in1=rs1.t[:], op=ALU.mult),
                                 reads=[O[1].s, rs1.s], writes=[o1.s])
                            P.op("dve", lambda e: e.scalar_tensor_tensor(out=o2.t[:], in0=o1.t[:], scalar=nlam.t[:, 0:1], in1=o0.t[:],
                                                                          op0=ALU.mult, op1=ALU.add),
                                 reads=[o1.s, o0.s, nlam.s], writes=[o2.s])
                            P.op("act", lambda e: e.activation(out=sq.t[:], in_=o2.t[:], func=AF.Square), reads=[o2.s], writes=[sq.s])
                            st = ST[step % 2]
                            step += 1
                            P.mm([lambda e, st=st: e.matmul(st.t[:, 0:512], lhsT=ones.t[:], rhs=sq.t[:], start=True, stop=True)],
                                 reads=[ones.s, sq.s], writes=[st.s])
                            P.op("act", lambda e, st=st: e.activation(out=vv.t[:], in_=st.t[:, 0:512], func=AF.Ln,
                                                                       scale=1.0 / 128.0, bias=epsc.t[:, 0:1]),
                                 reads=[st.s, epsc.s], writes=[vv.s])
                            P.op("act", lambda e: e.activation(out=vv.t[:], in_=vv.t[:], func=AF.Exp, scale=-0.5), writes=[vv.s])
                            P.op("dve", lambda e: e.tensor_tensor(out=on.t[:], in0=o2.t[:], in1=vv.t[:], op=ALU.mult),
                                 reads=[o2.s, vv.s], writes=[on.s])
                            P.op("dve", lambda e: e.tensor_scalar(out=oa.t[:], in0=on.t[:], scalar1=gs.t[:, 0:1], scalar2=None, op0=ALU.mult),
                                 reads=[on.s, gs.s], writes=[oa.s])
                            P.dma("sp", lambda qt=qt: (OAT[bass.ds(hi * 128, 128), bass.ds(tok0 + qt * TT, TT)], oa.t[:]),
                                  oa.s, reads=[oa.s])
                    return body

                if cfg.n_prompt > 0:
                    def seq_body(si):
                        P.loop(NH, head_body(cfg.s_p, lambda: si * cfg.s_p))
                    P.loop(cfg.n_prompt, seq_body)
                P.loop(NH, head_body(cfg.s_s, lambda: cfg.n_prompt * cfg.s_p))
                P.phase_end(mark_)

        def phase3(l):
            mark_ = P.phase_begin()
            Smax = max(cfg.s_p, cfg.s_s)
            with ExitStack() as es:
                upad = P.tile(es, "upad", [128, Smax + 3], F32)
                xc = P.tile(es, "xc", [128, Smax], F32)
                xcb = P.tile(es, "xcb", [128, Smax], BF16)
                glt = P.tile(es, "glt", [128, Smax], BF16)
                hf = P.tile(es, "hf", [128, Smax], F32)
                prm = P.tile(es, "prm", [128, 11], F32)
                drv = P.tile(es, "drv", [128, 8], F32)
                wrg = P.tile(es, "wrg", [128, 4, 128], BF16)
                er = P.tile(es, "er", [128, TT], F32)
                ei = P.tile(es, "ei", [128, TT], F32)
                av = P.tile(es, "av", [128, TT], F32)
                a2 = P.tile(es, "a2", [128, TT], F32)
                ix = P.tile(es, "ix", [128, TT], F32)
                bv = P.tile(es, "bv", [128, TT], F32)
                hb = [P.tile(es, "hb%d" % k, [128, TT], F32) for k in range(2)]
                ob = [P.tile(es, "ob%d" % k, [128, TT], BF16) for k in range(2)]
                psr = [P.tile(es, "psr%d" % k, [128, TT], F32, psum=True) for k in range(2)]
                psi = [P.tile(es, "psi%d" % k, [128, TT], F32, psum=True) for k in range(2)]
                P.op("dve", lambda e: e.memset(upad.t[:], 0.0), writes=[upad.s])

                def chunk_body(S, tokf):
                    nT = S // TT

                    def body(ci):
                        tok0 = None if ci is None else tokf()
                        P.dma("sp", lambda: (prm.t[:], lruprm[l, bass.ds(ci, 1), :, :].rearrange("o p k -> (o p) k")), prm.s, writes=[prm.s])
                        P.dma("pool", lambda: (wrg.t[:], rgw[l, bass.ds(ci, 1), :, :, :].rearrange("o f i j -> (o i) f j")), wrg.s, writes=[wrg.s])
                        P.dma("sp", lambda: (upad.t[:, 2:S + 2], UT[bass.ds(ci * 128, 128), bass.ds(tok0, S)]), upad.s, writes=[upad.s])
                        P.dma("sp", lambda: (glt.t[:, :S], GLT[bass.ds(ci * 128, 128), bass.ds(tok0, S)]), glt.s, writes=[glt.s])
                        P.op("dve", lambda e: e.tensor_scalar(out=drv.t[:, 0:4], in0=prm.t[:, 5:9], scalar1=-1.0, scalar2=None, op0=ALU.mult),
                             reads=[prm.s], writes=[drv.s])
                        P.op("act", lambda e: e.activation(out=drv.t[:, 4:6], in_=prm.t[:, 9:11], func=AF.Exp, scale=-1.0),
                             reads=[prm.s], writes=[drv.s])
                        P.op("act", lambda e: e.activation(out=drv.t[:, 4:6], in_=drv.t[:, 4:6], func=AF.Ln, bias=1.0, scale=1.0),
                             writes=[drv.s])
                        P.op("dve", lambda e: e.tensor_scalar(out=drv.t[:, 6:8], in0=drv.t[:, 4:6], scalar1=-2.0 * LRU_C, scalar2=None, op0=ALU.mult),
                             writes=[drv.s])
                        P.op("dve", lambda e: e.tensor_scalar(out=drv.t[:, 4:6], in0=drv.t[:, 4:6], scalar1=-LRU_C, scalar2=None, op0=ALU.mult),
                             writes=[drv.s])
                        P.op("dve", lambda e: e.tensor_scalar(out=xc.t[:, :S], in0=upad.t[:, 0:S], scalar1=prm.t[:, 0:1], scalar2=prm.t[:, 4:5],
                                                              op0=ALU.mult, op1=ALU.add),
                             reads=[upad.s, prm.s], writes=[xc.s])
                        for k in range(1, 4):
                            P.op("dve", lambda e, k=k: e.scalar_tensor_tensor(out=xc.t[:, :S], in0=upad.t[:, k:S + k], scalar=prm.t[:, k:k + 1],
                                                                               in1=xc.t[:, :S], op0=ALU.mult, op1=ALU.add),
                                 reads=[upad.s, prm.s], writes=[xc.s])
                        P.op("act", lambda e: e.activation(out=xcb.t[:, :S], in_=xc.t[:, :S], func=AF.Copy), reads=[xc.s], writes=[xcb.s])
                        n = 0
                        for d in range(2):
                            order = range(nT) if d == 0 else range(nT - 1, -1, -1)
                            prev_hb = None
                            for j in order:
                                cs_ = slice(j * TT, (j + 1) * TT)
                                pr = psr[n % 2]
                                pi = psi[n % 2]
                                P.mm([lambda e, pr=pr, cs_=cs_: e.matmul(pr.t[:], lhsT=wrg.t[:, 2 * d, :], rhs=xcb.t[:, cs_], start=True, stop=True)],
                                     reads=[wrg.s, xcb.s], writes=[pr.s])
                                P.mm([lambda e, pi=pi, cs_=cs_: e.matmul(pi.t[:], lhsT=wrg.t[:, 2 * d + 1, :], rhs=xcb.t[:, cs_], start=True, stop=True)],
                                     reads=[wrg.s, xcb.s], writes=[pi.s])
                                P.op("act", lambda e, pr=pr: e.activation(out=er.t[:], in_=pr.t[:], func=AF.Exp, scale=-1.0, bias=drv.t[:, d:d + 1]),
                                     reads=[pr.s, drv.s], writes=[er.s])
                                P.op("act", lambda e, pi=pi: e.activation(out=ei.t[:], in_=pi.t[:], func=AF.Exp, scale=-1.0, bias=drv.t[:, 2 + d:3 + d]),
                                     reads=[pi.s, drv.s], writes=[ei.s])
                                P.op("act", lambda e: e.activation(out=er.t[:], in_=er.t[:], func=AF.Ln, bias=1.0, scale=1.0), writes=[er.s])
                                P.op("act", lambda e: e.activation(out=er.t[:], in_=er.t[:], func=AF.Exp, scale=-1.0), writes=[er.s])
                                P.op("act", lambda e: e.activation(out=ei.t[:], in_=ei.t[:], func=AF.Ln, bias=1.0, scale=1.0), writes=[ei.s])
                                P.op("act", lambda e: e.activation(out=ei.t[:], in_=ei.t[:], func=AF.Exp, scale=-1.0), writes=[ei.s])
                                P.op("act", lambda e: e.activation(out=av.t[:], in_=er.t[:], func=AF.Exp, scale=drv.t[:, 4 + d:5 + d]),
                                     reads=[er.s, drv.s], writes=[av.s])
                                P.op("act", lambda e: e.activation(out=a2.t[:], in_=er.t[:], func=AF.Exp, scale=drv.t[:, 6 + d:7 + d]),
                                     reads=[er.s, drv.s], writes=[a2.s])
                                P.op("act", lambda e: e.activation(out=a2.t[:], in_=a2.t[:], func=AF.Ln, scale=-1.0, bias=1.0), writes=[a2.s])
                                P.op("act", lambda e: e.activation(out=a2.t[:], in_=a2.t[:], func=AF.Exp, scale=0.5), writes=[a2.s])
                                P.op("dve", lambda e, cs_=cs_: e.tensor_tensor(out=ix.t[:], in0=ei.t[:], in1=xc.t[:, cs_], op=ALU.mult),
                                     reads=[ei.s, xc.s], writes=[ix.s])
                                P.op("dve", lambda e: e.tensor_tensor(out=bv.t[:], in0=ix.t[:], in1=a2.t[:], op=ALU.mult),
                                     reads=[ix.s, a2.s], writes=[bv.s])
                                if d == 0:
                                    init = 0.0 if j == 0 else hf.t[:, j * TT - 1:j * TT]
                                    P.op("dve", lambda e, cs_=cs_, init=init: e.tensor_tensor_scan(
                                        out=hf.t[:, cs_], data0=av.t[:], data1=bv.t[:], initial=init, op0=ALU.mult, op1=ALU.add),
                                        reads=[av.s, bv.s], writes=[hf.s])
                                else:
                                    hbt = hb[n % 2]
                                    init = 0.0 if prev_hb is None else prev_hb.t[:, 0:1]
                                    rd = [av.s, bv.s] + ([prev_hb.s] if prev_hb is not None else [])
                                    P.op("dve", lambda e, hbt=hbt, init=init: e.tensor_tensor_scan(
                                        out=hbt.t[:, ::-1], data0=av.t[:, ::-1], data1=bv.t[:, ::-1], initial=init, op0=ALU.mult, op1=ALU.add),
                                        reads=rd, writes=[hbt.s])
                                    prev_hb = hbt
                                    P.op("pool", lambda e, hbt=hbt, cs_=cs_: e.tensor_tensor(out=ix.t[:], in0=hbt.t[:], in1=hf.t[:, cs_], op=ALU.add),
                                         reads=[hbt.s, hf.s], writes=[ix.s])
                                    obt = ob[n % 2]
                                    P.op("pool", lambda e, obt=obt, cs_=cs_: e.tensor_tensor(out=obt.t[:], in0=ix.t[:], in1=glt.t[:, cs_], op=ALU.mult),
                                         reads=[ix.s, glt.s], writes=[obt.s])
                                    P.dma("sp", lambda obt=obt, j=j: (OLT[bass.ds(ci * 128, 128), bass.ds(tok0 + j * TT, TT)], obt.t[:]),
                                          obt.s, reads=[obt.s])
                                n += 1
                    return body

                if cfg.n_prompt > 0:
                    def seq_body(si):
                        P.loop(8, chunk_body(cfg.s_p, lambda: si * cfg.s_p))
                    P.loop(cfg.n_prompt, seq_body)
                P.loop(8, chunk_body(cfg.s_s, lambda: cfg.n_prompt * cfg.s_p))
                P.phase_end(mark_)

        def phase4(l, Xsrc):
            mark_ = P.phase_begin()
            with ExitStack() as es:
                Wa = P.tile(es, "wa4", [128, 8, D], BF16)
                Wl = P.tile(es, "wl4", [128, 8, D], BF16)
                Wo = P.tile(es, "wo4", [128, 8, D], BF16)
                gpo = P.tile(es, "gpo4", [128, D], F32)
                gfp = P.tile(es, "gfp4", [128, D], F32)
                oat = P.tile(es, "oat4", [128, 8, TT], BF16)
                olt = P.tile(es, "olt4", [128, 8, TT], BF16)
                tga = P.tile(es, "tga4", [128, 8, TT], BF16)
                tgb = P.tile(es, "tgb4", [128, 8, TT], BF16)
                xt = P.tile(es, "xt4", [128, 4, D], F32)
                mT = P.tile(es, "mT4", [128, 8, TT], BF16)
                yt = P.tile(es, "yt4", [128, 4, D], F32)
                h2 = P.tile(es, "h24", [128, 4, D], BF16)
                h2T = P.tile(es, "h2T4", [128, 8, TT], BF16)
                junk = P.tile(es, "junk4", [128, D], BF16)
                ms = P.tile(es, "ms4", [128, 4], F32)
                rs = P.tile(es, "rs4", [128, 4], F32)
                ms2 = P.tile(es, "ms42", [128, 4], F32)
                rs2 = P.tile(es, "rs42", [128, 4], F32)
                ta = P.tile(es, "ta4", [128, TT], F32)
                tb_ = P.tile(es, "tb4", [128, TT], F32)
                pTs = [P.tile(es, "pT4_%d" % k, [128, TT], BF16, psum=True) for k in range(2)]
                acc = [P.tile(es, "acc4_%d" % k, [128, TT], F32, psum=True) for k in range(6)]
                load_w(Wa, w_ba[l], 8)
                load_w(Wl, w_bl[l], 8)
                load_w(Wo, w_out[l], 8)
                load_bcast(gpo, g_post[l])
                load_bcast(gfp, g_fpre[l])

                def mk_body(tokf, colf):
                    def body(i):
                        tok0 = None if i is None else tokf(i)
                        col0 = None if i is None else colf(i)
                        for T_, src in ((oat, OAT), (olt, OLT), (tga, TGA), (tgb, TGB)):
                            P.dma("sp", lambda T_=T_, src=src: (T_.t[:], src[:, bass.ds(tok0, TT)].rearrange("(c p) t -> p c t", p=128)),
                                  T_.s, writes=[T_.s])
                        P.dma("sp", lambda: (xt.t[:], Xsrc[bass.ds(tok0, TT), :].rearrange("(b p) d -> p b d", p=128)), xt.s, writes=[xt.s])
                        na = 0
                        for oc in range(8):
                            pa = acc[na % 6]
                            pb = acc[(na + 1) % 6]
                            na += 2
                            P.mm([lambda e, kc=kc, pa=pa, oc=oc: e.matmul(pa.t[:], lhsT=Wa.t[:, kc, oc * 128:(oc + 1) * 128], rhs=oat.t[:, kc, :],
                                                                           start=(kc == 0), stop=(kc == 7)) for kc in range(8)],
                                 reads=[Wa.s, oat.s], writes=[pa.s])
                            P.mm([lambda e, kc=kc, pb=pb, oc=oc: e.matmul(pb.t[:], lhsT=Wl.t[:, kc, oc * 128:(oc + 1) * 128], rhs=olt.t[:, kc, :],
                                                                           start=(kc == 0), stop=(kc == 7)) for kc in range(8)],
                                 reads=[Wl.s, olt.s], writes=[pb.s])
                            P.op("dve", lambda e, pa=pa, oc=oc: e.scalar_tensor_tensor(out=ta.t[:], in0=tga.t[:, oc, :], scalar=1.0, in1=pa.t[:],
                                                                                        op0=ALU.add, op1=ALU.mult),
                                 reads=[tga.s, pa.s], writes=[ta.s])
                            P.op("dve", lambda e, pb=pb, oc=oc: e.scalar_tensor_tensor(out=tb_.t[:], in0=tgb.t[:, oc, :], scalar=1.0, in1=pb.t[:],
                                                                                        op0=ALU.add, op1=ALU.mult),
                                 reads=[tgb.s, pb.s], writes=[tb_.s])
                            P.op("dve", lambda e, oc=oc: e.tensor_tensor(out=mT.t[:, oc, :], in0=ta.t[:], in1=tb_.t[:], op=ALU.add),
                                 reads=[ta.s, tb_.s], writes=[mT.s])
                        for tb in range(4):
                            for hf_ in range(2):
                                a = acc[na % 6]
                                na += 1
                                P.mm([lambda e, kc=kc, a=a, tb=tb, hf_=hf_: e.matmul(a.t[:], lhsT=mT.t[:, kc, tb * 128:(tb + 1) * 128],
                                                                                     rhs=Wo.t[:, kc, hf_ * 512:(hf_ + 1) * 512],
                                                                                     start=(kc == 0), stop=(kc == 7)) for kc in range(8)],
                                     reads=[Wo.s, mT.s], writes=[a.s])
                                P.op("act", lambda e, a=a, tb=tb, hf_=hf_: e.activation(out=yt.t[:, tb, hf_ * 512:(hf_ + 1) * 512], in_=a.t[:], func=AF.Copy),
                                     reads=[a.s], writes=[yt.s])
                        for b in range(4):
                            P.op("act", lambda e, b=b: e.activation(out=junk.t[:], in_=yt.t[:, b, :], func=AF.Square, scale=1.0 / 32.0,
                                                                     accum_out=ms.t[:, b:b + 1]),
                                 reads=[yt.s], writes=[junk.s, ms.s])
                        P.op("pool", lambda e: e.tensor_scalar(out=rs.t[:], in0=ms.t[:], scalar1=4.0 * EPS, scalar2=None, op0=ALU.add),
                             reads=[ms.s], writes=[rs.s])
                        P.op("pool", lambda e: e.tensor_tensor(out=rs.t[:], in0=rs.t[:], in1=mhalf.t[:, 0:4], op=ALU.pow), reads=[mhalf.s], writes=[rs.s])
                        for b in range(4):
                            P.op("dve", lambda e, b=b: e.scalar_tensor_tensor(out=yt.t[:, b, :], in0=yt.t[:, b, :], scalar=rs.t[:, b:b + 1], in1=gpo.t[:],
                                                                               op0=ALU.mult, op1=ALU.mult),
                                 reads=[rs.s, gpo.s], writes=[yt.s])
                            P.op("dve", lambda e, b=b: e.tensor_tensor(out=xt.t[:, b, :], in0=xt.t[:, b, :], in1=yt.t[:, b, :], op=ALU.add),
                                 reads=[yt.s], writes=[xt.s])
                        P.dma("sp", lambda: (X1[bass.ds(tok0, TT), :].rearrange("(b p) d -> p b d", p=128), xt.t[:]), xt.s, reads=[xt.s])
                        rms_to_bf16(xt, h2, gfp, ms2, rs2, junk)
                        cnt = [0]
                        transpose_blocks(h2, 8, h2T, pTs, cnt)
                        P.dma("sp", lambda: (H2T[:, bass.ds(col0, TT)].rearrange("(c p) t -> p c t", p=128), h2T.t[:]), h2T.s, reads=[h2T.s])
                    return body

                run_token_loops(mk_body)
                P.phase_end(mark_)

        def run_token_loops(mk_body):
            tp = cfg.s_p // TT
            if cfg.n_prompt > 0:
                def seq_body(si):
                    P.loop(tp, mk_body(lambda j: si * cfg.s_p + j * TT, lambda j: si * (cfg.s_p + 2) + j * TT + 1))
                P.loop(cfg.n_prompt, seq_body)
            b0 = cfg.n_prompt * cfg.s_p
            c0 = cfg.n_prompt * (cfg.s_p + 2) + 1
            P.loop(cfg.s_s // TT, mk_body(lambda j: j * TT + b0, lambda j: j * TT + c0))

        def phase5(l):
            mark_ = P.phase_begin()
            with ExitStack() as es:
                Wu = P.tile(es, "wu5", [128, 8, 2 * DFF], BF16)
                fp = P.tile(es, "fp5", [128, 24, 4], F32)
                h2e = P.tile(es, "h2e5", [128, 8, TT + 2], BF16)
                cext = [P.tile(es, "cext5_%d" % k, [128, TT + 2], F32) for k in range(2)]
                cacc = [P.tile(es, "cacc5_%d" % k, [128, TT], F32) for k in range(2)]
                gg = [P.tile(es, "gg5_%d" % k, [128, TT], F32) for k in range(2)]
                go = [P.tile(es, "go5_%d" % k, [128, TT], BF16) for k in range(3)]
                psc = [P.tile(es, "psc5_%d" % k, [128, TT], F32, psum=True) for k in range(3)]
                psl = [P.tile(es, "psl5_%d" % k, [128, TT], F32, psum=True) for k in range(3)]
                psh = [P.tile(es, "psh5_%d" % k, [128, 2], F32, psum=True) for k in range(2)]
                load_w(Wu, w_up[l], 8)
                P.dma("sp", lambda: (fp.t[:], ffnprm[l]), fp.s, writes=[fp.s])

                def mk_body(tokf, colf):
                    def body(i):
                        tok0 = None if i is None else tokf(i)
                        col0 = None if i is None else colf(i)
                        P.dma("sp", lambda: (h2e.t[:], H2T[:, bass.ds(col0 - 1, TT + 2)].rearrange("(c p) t -> p c t", p=128)), h2e.s, writes=[h2e.s])
                        for j in range(24):
                            pc = psc[j % 3]
                            pl = psl[j % 3]
                            ph = psh[j % 2]
                            ce = cext[j % 2]
                            ca = cacc[j % 2]
                            g_ = gg[j % 2]
                            o_ = go[j % 3]
                            P.mm([lambda e, kc=kc, pc=pc, j=j: e.matmul(pc.t[:], lhsT=Wu.t[:, kc, j * 128:(j + 1) * 128], rhs=h2e.t[:, kc, 1:TT + 1],
                                                                         start=(kc == 0), stop=(kc == 7)) for kc in range(8)],
                                 reads=[Wu.s, h2e.s], writes=[pc.s])
                            P.mm([lambda e, kc=kc, ph=ph, j=j: e.matmul(ph.t[:], lhsT=Wu.t[:, kc, j * 128:(j + 1) * 128], rhs=h2e.t[:, kc, 0:TT + 2:TT + 1],
                                                                         start=(kc == 0), stop=(kc == 7)) for kc in range(8)],
                                 reads=[Wu.s, h2e.s], writes=[ph.s])
                            P.mm([lambda e, kc=kc, pl=pl, j=j: e.matmul(pl.t[:], lhsT=Wu.t[:, kc, DFF + j * 128:DFF + (j + 1) * 128], rhs=h2e.t[:, kc, 1:TT + 1],
                                                                         start=(kc == 0), stop=(kc == 7)) for kc in range(8)],
                                 reads=[Wu.s, h2e.s], writes=[pl.s])
                            P.op("act", lambda e, pc=pc, ce=ce: e.activation(out=ce.t[:, 1:TT + 1], in_=pc.t[:], func=AF.Copy), reads=[pc.s], writes=[ce.s])
                            P.op("dve", lambda e, ph=ph, ce=ce: e.tensor_copy(out=ce.t[:, 0:TT + 2:TT + 1], in_=ph.t[:]), reads=[ph.s], writes=[ce.s])
                            P.op("dve", lambda e, ce=ce, ca=ca, j=j: e.tensor_scalar(out=ca.t[:], in0=ce.t[:, 0:TT], scalar1=fp.t[:, j, 0:1], scalar2=fp.t[:, j, 3:4],
                                                                                       op0=ALU.mult, op1=ALU.add),
                                 reads=[ce.s, fp.s], writes=[ca.s])
                            for k in (1, 2):
                                P.op("dve", lambda e, ce=ce, ca=ca, j=j, k=k: e.scalar_tensor_tensor(out=ca.t[:], in0=ce.t[:, k:TT + k], scalar=fp.t[:, j, k:k + 1],
                                                                                                       in1=ca.t[:], op0=ALU.mult, op1=ALU.add),
                                     reads=[ce.s, fp.s], writes=[ca.s])
                            P.op("act", lambda e, ca=ca, g_=g_: e.activation(out=g_.t[:], in_=ca.t[:], func=AF.Gelu_apprx_tanh), reads=[ca.s], writes=[g_.s])
                            P.op("dve", lambda e, g_=g_, pl=pl, o_=o_: e.tensor_tensor(out=o_.t[:], in0=g_.t[:], in1=pl.t[:], op=ALU.mult),
                                 reads=[g_.s, pl.s], writes=[o_.s])
                            P.dma("sp", lambda o_=o_, j=j: (GT[j * 128:(j + 1) * 128, bass.ds(tok0, TT)], o_.t[:]), o_.s, reads=[o_.s])
                    return body

                run_token_loops(mk_body)
                P.phase_end(mark_)

        def phase6(l, Xdst):
            mark_ = P.phase_begin()
            with ExitStack() as es:
                Wd = P.tile(es, "wd6", [128, 24, D], BF16)
                Wg = P.tile(es, "wg6", [128, 8, D], BF16)
                Wp = P.tile(es, "wp6", [128, 2, D], BF16)
                gfo = P.tile(es, "gfo6", [128, D], F32)
                gT = P.tile(es, "gT6", [128, 24, TT], BF16)
                xt = P.tile(es, "xt6", [128, 4, D], F32)
                pt = P.tile(es, "pt6", [128, 4, PLE], F32)
                ft = P.tile(es, "ft6", [128, 4, D], F32)
                xb = P.tile(es, "xb6", [128, 4, D], BF16)
                pb = P.tile(es, "pb6", [128, 4, PLE], BF16)
                xT = P.tile(es, "xT6", [128, 8, TT], BF16)
                pT_ = P.tile(es, "ppT6", [128, 2, TT], BF16)
                junk = P.tile(es, "junk6", [128, D], BF16)
                ms = P.tile(es, "ms6", [128, 4], F32)
                rs = P.tile(es, "rs6", [128, 4], F32)
                tg = [P.tile(es, "tg6_%d" % k, [128, TT], F32) for k in range(2)]
                pTs = [P.tile(es, "pT6_%d" % k, [128, TT], BF16, psum=True) for k in range(2)]
                acc = [P.tile(es, "acc6_%d" % k, [128, TT], F32, psum=True) for k in range(6)]
                load_w(Wd, w_dn[l], 24)
                load_w(Wg, w_pg[l], 8)
                load_w(Wp, w_pp[l], 2)
                load_bcast(gfo, g_fpost[l])

                def body(i):
                    tok0 = None if i is None else i * TT
                    P.dma("sp", lambda: (gT.t[:], GT[:, bass.ds(tok0, TT)].rearrange("(c p) t -> p c t", p=128)), gT.s, writes=[gT.s])
                    P.dma("sp", lambda: (xt.t[:], X1[bass.ds(tok0, TT), :].rearrange("(b p) d -> p b d", p=128)), xt.s, writes=[xt.s])
                    P.dma("sp", lambda: (pt.t[:], p_in[l, bass.ds(tok0, TT), :].rearrange("(b p) d -> p b d", p=128)), pt.s, writes=[pt.s])
                    na = 0
                    for tb in range(4):
                        for hf_ in range(2):
                            a = acc[na % 6]
                            na += 1
                            P.mm([lambda e, kc=kc, a=a, tb=tb, hf_=hf_: e.matmul(a.t[:], lhsT=gT.t[:, kc, tb * 128:(tb + 1) * 128],
                                                                                 rhs=Wd.t[:, kc, hf_ * 512:(hf_ + 1) * 512],
                                                                                 start=(kc == 0), stop=(kc == 23)) for kc in range(24)],
                                 reads=[Wd.s, gT.s], writes=[a.s])
                            P.op("act", lambda e, a=a, tb=tb, hf_=hf_: e.activation(out=ft.t[:, tb, hf_ * 512:(hf_ + 1) * 512], in_=a.t[:], func=AF.Copy),
                                 reads=[a.s], writes=[ft.s])
                    for b in range(4):
                        P.op("act", lambda e, b=b: e.activation(out=junk.t[:], in_=ft.t[:, b, :], func=AF.Square, scale=1.0 / 32.0,
                                                                 accum_out=ms.t[:, b:b + 1]),
                             reads=[ft.s], writes=[junk.s, ms.s])
                    P.op("pool", lambda e: e.tensor_scalar(out=rs.t[:], in0=ms.t[:], scalar1=EPS, scalar2=None, op0=ALU.add), reads=[ms.s], writes=[rs.s])
                    P.op("pool", lambda e: e.tensor_tensor(out=rs.t[:], in0=rs.t[:], in1=mhalf.t[:, 0:4], op=ALU.pow), reads=[mhalf.s], writes=[rs.s])
                    for b in range(4):
                        P.op("dve", lambda e, b=b: e.scalar_tensor_tensor(out=ft.t[:, b, :], in0=ft.t[:, b, :], scalar=rs.t[:, b:b + 1], in1=gfo.t[:],
                                                                           op0=ALU.mult, op1=ALU.mult),
                             reads=[rs.s, gfo.s], writes=[ft.s])
                        P.op("dve", lambda e, b=b: e.tensor_tensor(out=xt.t[:, b, :], in0=xt.t[:, b, :], in1=ft.t[:, b, :], op=ALU.add),
                             reads=[ft.s], writes=[xt.s])
                    P.op("act", lambda e: e.activation(out=xb.t[:], in_=xt.t[:], func=AF.Copy), reads=[xt.s], writes=[xb.s])
                    P.op("act", lambda e: e.activation(out=pb.t[:], in_=pt.t[:], func=AF.Copy), reads=[pt.s], writes=[pb.s])
                    cnt = [0]
                    transpose_blocks(xb, 8, xT, pTs, cnt)
                    transpose_blocks(pb, 2, pT_, pTs, cnt)
                    n2 = 0
                    for tb in range(4):
                        for hf_ in range(2):
                            ag = acc[na % 6]
                            ap_ = acc[(na + 1) % 6]
                            na += 2
                            t_ = tg[n2 % 2]
                            n2 += 1
                            P.mm([lambda e, kc=kc, ag=ag, tb=tb, hf_=hf_: e.matmul(ag.t[:], lhsT=xT.t[:, kc, tb * 128:(tb + 1) * 128],
                                                                                   rhs=Wg.t[:, kc, hf_ * 512:(hf_ + 1) * 512],
                                                                                   start=(kc == 0), stop=(kc == 7)) for kc in range(8)],
                                 reads=[Wg.s, xT.s], writes=[ag.s])
                            P.mm([lambda e, kc=kc, ap_=ap_, tb=tb, hf_=hf_: e.matmul(ap_.t[:], lhsT=pT_.t[:, kc, tb * 128:(tb + 1) * 128],
                                                                                     rhs=Wp.t[:, kc, hf_ * 512:(hf_ + 1) * 512],
                                                                                     start=(kc == 0), stop=(kc == 1)) for kc in range(2)],
                                 reads=[Wp.s, pT_.s], writes=[ap_.s])
                            P.op("act", lambda e, ag=ag, t_=t_: e.activation(out=t_.t[:], in_=ag.t[:], func=AF.Tanh, scale=0.5), reads=[ag.s], writes=[t_.s])
                            P.op("dve", lambda e, ap_=ap_, t_=t_: e.scalar_tensor_tensor(out=t_.t[:], in0=t_.t[:], scalar=1.0, in1=ap_.t[:],
                                                                                          op0=ALU.add, op1=ALU.mult),
                                 reads=[ap_.s], writes=[t_.s])
                            P.op("dve", lambda e, t_=t_, tb=tb, hf_=hf_: e.scalar_tensor_tensor(
                                out=ft.t[:, tb, hf_ * 512:(hf_ + 1) * 512], in0=t_.t[:], scalar=0.5, in1=xt.t[:, tb, hf_ * 512:(hf_ + 1) * 512],
                                op0=ALU.mult, op1=ALU.add),
                                reads=[t_.s, xt.s], writes=[ft.s])
                    P.dma("sp", lambda: (Xdst[bass.ds(tok0, TT), :].rearrange("(b p) d -> p b d", p=128), ft.t[:]), ft.s, reads=[ft.s])

                P.loop(ntile, body)
                P.phase_end(mark_)

        for l in range(L):
            lam_init = 0.8 - 0.6 * math.exp(-0.3 * l)
            Xsrc = x_in if l == 0 else XS
            Xdst = y_out if l == L - 1 else XS
            phase1(l, Xsrc)
            phase2(l, lam_init)
            phase3(l)
            phase4(l, Xsrc)
            phase5(l)
            phase6(l, Xdst)
        P.barrier()
    return nc


def _rope_tables(cfg):
    inv = THETA ** (-np.arange(0, 64, 2, dtype=np.float32) / 64.0)
    pos = np.concatenate([np.arange(S, dtype=np.float32) for (_, S) in cfg.seqs])
    ang = pos[None, :] * inv[:, None].astype(np.float32)
    c = np.tile(np.cos(ang).astype(np.float32), (4, 1))
    s = np.tile(np.sin(ang).astype(np.float32), (4, 1))
    return np.ascontiguousarray(c), np.ascontiguousarray(s)


def _qk_perm():
    perm = []
    for g in range(4):
        for half in range(2):
            for m in range(4):
                for d in range(32):
                    perm.append((4 * g + m) * 64 + half * 32 + d)
    return np.array(perm)


def prep_shared(inp, L):
    f = lambda k: np.ascontiguousarray(np.asarray(inp[k], dtype=np.float32))
    w_in = f("w_in").copy()
    perm = _qk_perm()
    w_in[:, :, 0:D] = w_in[:, :, 0:D][:, :, perm]
    w_in[:, :, D:2 * D] = w_in[:, :, D:2 * D][:, :, perm]
    lamv = np.stack([f("lam_q1"), f("lam_k1"), f("lam_q2"), f("lam_k2")], axis=1)
    cw = f("lru_conv_w")
    cols = [cw[:, k, :] for k in range(4)] + [f("lru_conv_b")]
    cols += [f("rg_ba")[:, 0], f("rg_ba")[:, 1], f("rg_bx")[:, 0], f("rg_bx")[:, 1], f("rg_lambda")[:, 0], f("rg_lambda")[:, 1]]
    lruprm = np.stack(cols, axis=-1).reshape(L, 8, 128, 11)
    wa, wx = f("rg_wa"), f("rg_wx")
    rgw = np.stack([wa[:, 0], wx[:, 0], wa[:, 1], wx[:, 1]], axis=2)
    fw = f("ffn_conv_w")
    fcols = [fw[:, k, :] for k in range(3)] + [f("ffn_conv_b")]
    ffnprm = np.stack(fcols, axis=-1).reshape(L, 24, 128, 4).transpose(0, 2, 1, 3)
    sh = {
        "ident": np.eye(128, dtype=np.float32),
        "w_in": np.ascontiguousarray(w_in),
        "norm_mix_pre": f("norm_mix_pre"), "norm_mix_post": f("norm_mix_post"),
        "lamv": np.ascontiguousarray(lamv), "subln_g": f("subln_g"),
        "lruprm": np.ascontiguousarray(lruprm), "rgw": np.ascontiguousarray(rgw),
        "w_branch_attn": f("w_branch_attn"), "w_branch_lru": f("w_branch_lru"), "w_out": f("w_out"),
        "norm_ffn_pre": f("norm_ffn_pre"), "norm_ffn_post": f("norm_ffn_post"),
        "w_ffn_up": f("w_ffn_up"), "ffnprm": np.ascontiguousarray(ffnprm),
        "w_ffn_down": f("w_ffn_down"), "w_ple_proj": f("w_ple_proj"), "w_ple_gate": f("w_ple_gate"),
    }
    return sh


def run(cfg, inp, n_cores, debug=False):
    L = cfg.L
    sh = prep_shared(inp, L)
    rc, rs = _rope_tables(cfg)
    xp = np.asarray(inp["x_prompt"], dtype=np.float32)
    xs = np.asarray(inp["x_sample"], dtype=np.float32)
    pp = np.asarray(inp["p_prompt"], dtype=np.float32)
    ps = np.asarray(inp["p_sample"], dtype=np.float32)
    in_maps = []
    npc = cfg.n_prompt
    for c in range(n_cores):
        xc = np.concatenate([xp[c * npc:(c + 1) * npc].reshape(-1, D), xs[c].reshape(-1, D)], axis=0)
        pc = np.concatenate([pp[:, c * npc:(c + 1) * npc].reshape(L, -1, PLE), ps[:, c].reshape(L, -1, PLE)], axis=1)
        m = dict(sh)
        m["x"] = np.ascontiguousarray(xc)
        m["p"] = np.ascontiguousarray(pc)
        m["ropeC"] = rc
        m["ropeS"] = rs
        in_maps.append(m)
    nc = build(cfg, debug=debug)
    res = run_bass_kernel_spmd(nc, in_maps, core_ids=list(range(n_cores)))
    ys = [r["y"] for r in res.results]
    yp = np.stack([y[:npc * cfg.s_p].reshape(npc, cfg.s_p, D) for y in ys]).reshape(n_cores * npc, cfg.s_p, D)
    ysm = np.stack([y[npc * cfg.s_p:].reshape(cfg.s_s, D) for y in ys])
    return (yp.astype(np.float32), ysm.astype(np.float32)), res


def run_seqs(S, L, inp, xs, ps, n_cores, rounds):
    cfg = Cfg(0, 512, S, L)
    sh = prep_shared(inp, L)
    rc, rs = _rope_tables(cfg)
    nc = build(cfg)
    outs = []
    for r in range(rounds):
        in_maps = []
        for c in range(n_cores):
            k = r * n_cores + c
            m = dict(sh)
            m["x"] = np.ascontiguousarray(xs[k].reshape(S, D))
            m["p"] = np.ascontiguousarray(ps[:, k].reshape(L, S, PLE))
            m["ropeC"] = rc
            m["ropeS"] = rs
            in_maps.append(m)
        res = run_bass_kernel_spmd(nc, in_maps, core_ids=list(range(n_cores)))
        outs.extend([np.asarray(rr["y"], dtype=np.float32).reshape(S, D) for rr in res.results])
    return np.stack(outs)


def kernel(**inputs):
    cfg = Cfg(4, 2048, 8192, 2)
    out, _ = run(cfg, inputs, 8)
    return out
```

```python
import math
from contextlib import ExitStack
import numpy as np
import concourse.bass as bass
import concourse.mybir as mybir
from concourse.bass_utils import run_bass_kernel_spmd

F32 = mybir.dt.float32
BF16 = mybir.dt.bfloat16
AF = mybir.ActivationFunctionType
ALU = mybir.AluOpType

D = 1024
NH = 8
DFF = 3072
PLE = 256
INC = 7168
EPS = 1e-6
LRU_C = 8.0
THETA = 10000.0
TT = 512
UNROLL = True


class Sem:
    def __init__(self, handle, name):
        self.h = handle
        self.name = name
        self.const = 0


class Frame:
    def __init__(self, n):
        self.n = n
        self.per = None
        self.ivar = None


class Ev:
    __slots__ = ("sem", "const", "frames")

    def __init__(self, sem, const, frames):
        self.sem = sem
        self.const = const
        self.frames = frames


class Slot:
    def __init__(self, P, name):
        self.name = name
        self.w = None
        self.r = {}
        self.dsem = None
        P.slots.append(self)


class Tile:
    def __init__(self, P, t, name):
        self.t = t
        self.s = Slot(P, name)


class Prog:
    def __init__(self, nc, es):
        self.nc = nc
        self.es = es
        self.dry = 0
        self.frames = []
        self.slots = []
        self.sems = []
        self.eng = {"pe": nc.tensor, "act": nc.scalar, "dve": nc.vector, "pool": nc.gpsimd, "sp": nc.sync}
        self.esem = {k: self.new_sem("e_" + k) for k in ("pe", "act", "dve", "pool")}
        self.waited = {}
        self.free_dsems = []

    def alloc_dsem(self):
        if self.free_dsems:
            return self.free_dsems.pop()
        return self.new_sem("d_%d" % len(self.sems))

    def phase_begin(self):
        return len(self.slots)

    def phase_end(self, mark):
        self.barrier()
        for sl in self.slots[mark:]:
            if sl.dsem is not None:
                self.free_dsems.append(sl.dsem)
                sl.dsem = None
        del self.slots[mark:]

    def new_sem(self, name):
        h = self.es.enter_context(self.nc.semaphore(name))
        s = Sem(h, name)
        self.sems.append(s)
        return s

    def tile(self, es, name, shape, dtype, psum=False):
        self.ntile = getattr(self, "ntile", 0) + 1
        name = "t%d_%s" % (self.ntile, name)
        if psum:
            t = es.enter_context(self.nc.psum_tensor(name, shape, dtype))
        else:
            t = es.enter_context(self.nc.sbuf_tensor(name, shape, dtype))
        return Tile(self, t, name)

    def _val(self, ev):
        v = ev.const
        for f in ev.frames:
            c = f.per.get(ev.sem.name, 0)
            if c:
                v = f.ivar * c + v
        return v

    def _wait(self, eng, ev):
        if ev is None:
            return
        if eng == "pe" and ev.sem is self.esem["pe"]:
            return
        key = (eng, ev.sem.name)
        last = self.waited.get(key)
        if last is not None and last[1] == ev.frames and last[0] >= ev.const:
            return
        self.waited[key] = (ev.const, ev.frames)
        if not self.dry:
            self.eng[eng].wait_ge(ev.sem.h, self._val(ev))

    def _deps(self, eng, reads, writes):
        for sl in reads:
            self._wait(eng, sl.w)
        for sl in writes:
            self._wait(eng, sl.w)
            for ev in list(sl.r.values()):
                self._wait(eng, ev)

    def _update(self, key, ev, reads, writes):
        for sl in writes:
            sl.w = ev
            sl.r = {}
        for sl in reads:
            if sl not in writes:
                sl.r[key] = ev

    def op(self, eng, fn, reads=(), writes=()):
        self._deps(eng, reads, writes)
        s = self.esem[eng]
        s.const += 1
        if not self.dry:
            fn(self.eng[eng]).then_inc(s.h, 1)
        ev = Ev(s, s.const, tuple(self.frames))
        self._update(eng, ev, reads, writes)
        return ev

    def mm(self, fns, reads=(), writes=()):
        self._deps("pe", reads, writes)
        s = self.esem["pe"]
        s.const += 1
        if not self.dry:
            ins = None
            for fn in fns:
                ins = fn(self.nc.tensor)
            ins.then_inc(s.h, 1)
        ev = Ev(s, s.const, tuple(self.frames))
        self._update("pe", ev, reads, writes)
        return ev

    def dma(self, q, fn, slot, reads=(), writes=(), **kw):
        self._deps(q, reads, writes)
        if slot.dsem is None:
            slot.dsem = self.alloc_dsem()
        s = slot.dsem
        s.const += 16
        if not self.dry:
            o, i = fn()
            try:
                self.eng[q].dma_start(out=o, in_=i, **kw).then_inc(s.h, 16)
            except Exception:
                print("DMA FAIL", q, o, i)
                raise
        ev = Ev(s, s.const, tuple(self.frames))
        self._update(("dma", s.name), ev, reads, writes)
        return ev

    def barrier(self):
        if self.dry:
            return
        fr = tuple(self.frames)
        for eng in ("pe", "act", "dve", "pool", "sp"):
            for s in self.sems:
                if s.const == 0 and not any(f.per.get(s.name, 0) for f in fr):
                    continue
                self.eng[eng].wait_ge(s.h, self._val(Ev(s, s.const, fr)))
        self.waited = {}

    def loop(self, n, body):
        if getattr(self, "unroll", False):
            for it in range(n):
                body(it)
            return
        fr = Frame(n)
        snap = {s.name: s.const for s in self.sems}
        nsem0 = len(self.sems)
        self.frames.append(fr)
        self.dry += 1
        self.waited = {}
        body(None)
        self.dry -= 1
        self.frames.pop()
        fr.per = {s.name: s.const - snap.get(s.name, 0) for s in self.sems}
        final = {sl: (sl.w, dict(sl.r)) for sl in self.slots}
        for s in self.sems:
            s.const = snap.get(s.name, 0)

        def settle(ev):
            if ev is None or fr not in ev.frames:
                return ev
            return Ev(ev.sem, ev.const + (n - 1) * fr.per[ev.sem.name], tuple(f for f in ev.frames if f is not fr))

        def finish():
            for s in self.sems:
                s.const = snap.get(s.name, 0) + n * fr.per[s.name]
            for sl, (w, r) in final.items():
                sl.w = settle(w)
                sl.r = {k: settle(e) for k, e in r.items()}
            self.waited = {}

        if self.dry:
            finish()
            return
        self.barrier()

        def prev(ev):
            if ev is None or fr not in ev.frames:
                return None
            return Ev(ev.sem, ev.const - fr.per[ev.sem.name], ev.frames)

        with self.nc.Fori(0, n) as i:
            fr.ivar = i
            self.frames.append(fr)
            for sl, (w, r) in final.items():
                sl.w = prev(w)
                sl.r = {k: e2 for k, e2 in ((k, prev(e)) for k, e in r.items()) if e2 is not None}
            self.waited = {}
            body(i)
            self.frames.pop()
        finish()


class Cfg:
    def __init__(self, n_prompt, s_p, s_s, L):
        self.n_prompt, self.s_p, self.s_s, self.L = n_prompt, s_p, s_s, L
        self.seqs = [(k * s_p, s_p) for k in range(n_prompt)] + [(n_prompt * s_p, s_s)]
        self.NT = n_prompt * s_p + s_s
        self.nseq = n_prompt + 1
        self.NTP = self.NT + 2 * self.nseq


def build(cfg, debug=False):
    nc = bass.Bass("TRN2", target_bir_lowering=False)
    NT, L = cfg.NT, cfg.L
    ntile = NT // TT

    def din(name, shape, dt=F32):
        return nc.dram_tensor(name, shape, dt, kind="ExternalInput").ap()

    def dscr(name, shape, dt):
        if debug:
            return nc.dram_tensor(name, shape, dt, kind="ExternalOutput").ap()
        return nc.dram_tensor(name, shape, dt).ap()

    x_in = din("x", [NT, D])
    p_in = din("p", [L, NT, PLE])
    ropeC = din("ropeC", [128, NT])
    ropeS = din("ropeS", [128, NT])
    ident_in = din("ident", [128, 128])
    w_in = din("w_in", [L, D, INC])
    g_pre = din("norm_mix_pre", [L, D])
    g_post = din("norm_mix_post", [L, D])
    lamv = din("lamv", [L, 4, 64])
    subln = din("subln_g", [L, 128])
    lruprm = din("lruprm", [L, 8, 128, 11])
    rgw = din("rgw", [L, 8, 4, 128, 128])
    w_ba = din("w_branch_attn", [L, D, D])
    w_bl = din("w_branch_lru", [L, D, D])
    w_out = din("w_out", [L, D, D])
    g_fpre = din("norm_ffn_pre", [L, D])
    g_fpost = din("norm_ffn_post", [L, D])
    w_up = din("w_ffn_up", [L, D, 2 * DFF])
    ffnprm = din("ffnprm", [L, 128, 24, 4])
    w_dn = din("w_ffn_down", [L, DFF, D])
    w_pp = din("w_ple_proj", [L, PLE, D])
    w_pg = din("w_ple_gate", [L, D, D])
    y_out = nc.dram_tensor("y", [NT, D], F32, kind="ExternalOutput").ap()

    QT = dscr("QT", [D, NT], BF16)
    KT = dscr("KT", [D, NT], BF16)
    VV = dscr("VV", [NT, D], BF16)
    UT = dscr("UT", [D, NT], F32)
    GLT = dscr("GLT", [D, NT], BF16)
    TGA = dscr("TGA", [D, NT], BF16)
    TGB = dscr("TGB", [D, NT], BF16)
    OAT = dscr("OAT", [D, NT], BF16)
    OLT = dscr("OLT", [D, NT], BF16)
    X1 = dscr("X1", [NT, D], F32)
    H2T = dscr("H2T", [D, cfg.NTP], BF16)
    GT = dscr("GT", [DFF, NT], BF16)
    XS = dscr("XS", [NT, D], F32)

    top = ExitStack()
    with top:
        P = Prog(nc, top)
        P.unroll = UNROLL
        ident = P.tile(top, "ident", [128, 128], BF16)
        ones = P.tile(top, "ones", [128, 128], BF16)
        mhalf = P.tile(top, "mhalf", [128, 512], F32)
        zcol = P.tile(top, "zcol", [128, 8, 2], BF16)
        P.dma("pool", lambda: (ident.t[:], ident_in[:, :]), ident.s, writes=[ident.s])
        P.op("dve", lambda e: e.memset(ones.t[:], 1.0), writes=[ones.s])
        P.op("pool", lambda e: e.memset(mhalf.t[:], -0.5), writes=[mhalf.s])
        P.op("dve", lambda e: e.memset(zcol.t[:], 0.0), writes=[zcol.s])
        for k, (st, S) in enumerate(cfg.seqs):
            c0 = st + 2 * k
            for cc in (c0, c0 + S + 1):
                P.dma("sp", lambda cc=cc: (H2T[:, cc:cc + 1].rearrange("(c p) t -> p c t", p=128), zcol.t[:, :, 0:1]),
                      zcol.s, reads=[zcol.s], allow_slow_non_contiguous=True)

        def rms_to_bf16(src, dst, gt, ms, rs, junk, eps=EPS):
            for b in range(4):
                P.op("act", lambda e, b=b: e.activation(out=junk.t[:], in_=src.t[:, b, :], func=AF.Square,
                                                         scale=1.0 / 32.0, accum_out=ms.t[:, b:b + 1]),
                     reads=[src.s], writes=[junk.s, ms.s])
            P.op("pool", lambda e: e.tensor_scalar(out=rs.t[:], in0=ms.t[:], scalar1=eps, scalar2=None, op0=ALU.add),
                 reads=[ms.s], writes=[rs.s])
            P.op("pool", lambda e: e.tensor_tensor(out=rs.t[:], in0=rs.t[:], in1=mhalf.t[:, 0:4], op=ALU.pow),
                 reads=[mhalf.s], writes=[rs.s])
            for b in range(4):
                P.op("dve", lambda e, b=b: e.scalar_tensor_tensor(out=dst.t[:, b, :], in0=src.t[:, b, :],
                                                                   scalar=rs.t[:, b:b + 1], in1=gt.t[:],
                                                                   op0=ALU.mult, op1=ALU.mult),
                     reads=[src.s, rs.s, gt.s], writes=[dst.s])

        def transpose_blocks(src, nchunk, dstT, pTs, cnt):
            for fc in range(nchunk):
                pT = pTs[cnt[0] % len(pTs)]
                P.mm([lambda e, b=b, fc=fc, pT=pT: e.transpose(out=pT.t[:, b * 128:(b + 1) * 128],
                                                              in_=src.t[:, b, fc * 128:(fc + 1) * 128],
                                                              identity=ident.t[:]) for b in range(4)],
                     reads=[src.s, ident.s], writes=[pT.s])
                eng = "act" if cnt[0] % 2 == 0 else "dve"
                if eng == "act":
                    P.op("act", lambda e, fc=fc, pT=pT: e.activation(out=dstT.t[:, fc, :], in_=pT.t[:], func=AF.Copy),
                         reads=[pT.s], writes=[dstT.s])
                else:
                    P.op("dve", lambda e, fc=fc, pT=pT: e.tensor_copy(out=dstT.t[:, fc, :], in_=pT.t[:]),
                         reads=[pT.s], writes=[dstT.s])
                cnt[0] += 1

        def load_w(Wt, src, nk, eng="pool"):
            for kc in range(nk):
                P.dma(eng, lambda kc=kc: (Wt.t[:, kc, :], src[kc * 128:(kc + 1) * 128, :]), Wt.s, writes=[Wt.s])

        def load_bcast(Tt, vec):
            P.dma("sp", lambda: (Tt.t[:], vec.partition_broadcast(128)), Tt.s, writes=[Tt.s])

        def phase1(l, Xsrc):
            mark_ = P.phase_begin()
            with ExitStack() as es:
                W = P.tile(es, "w1", [128, 8, INC], BF16)
                g1 = P.tile(es, "g1", [128, D], F32)
                xt = P.tile(es, "xt1", [128, 4, D], F32)
                h = P.tile(es, "h1", [128, 4, D], BF16)
                hT = P.tile(es, "hT1", [128, 8, TT], BF16)
                junk = P.tile(es, "junk1", [128, D], BF16)
                ms = P.tile(es, "ms1", [128, 4], F32)
                rs = P.tile(es, "rs1", [128, 4], F32)
                cs = P.tile(es, "cos1", [128, TT], F32)
                sn = P.tile(es, "sin1", [128, TT], F32)
                fa = P.tile(es, "fa1", [128, TT], F32)
                fb = P.tile(es, "fb1", [128, TT], F32)
                t1 = P.tile(es, "t11", [128, TT], F32)
                t2 = P.tile(es, "t21", [128, TT], F32)
                t3 = P.tile(es, "t31", [128, TT], F32)
                t4 = P.tile(es, "t41", [128, TT], F32)
                stg = [P.tile(es, "stg1_%d" % k, [128, TT], BF16) for k in range(4)]
                stf = [P.tile(es, "stf1_%d" % k, [128, TT], F32) for k in range(2)]
                pTs = [P.tile(es, "pT1_%d" % k, [128, TT], BF16, psum=True) for k in range(2)]
                acc = [P.tile(es, "acc1_%d" % k, [128, TT], F32, psum=True) for k in range(6)]
                load_w(W, w_in[l], 8)
                load_bcast(g1, g_pre[l])

                def body(i):
                    tok0 = None if i is None else i * TT
                    P.dma("sp", lambda: (xt.t[:], Xsrc[bass.ds(tok0, TT), :].rearrange("(b p) d -> p b d", p=128)),
                          xt.s, writes=[xt.s])
                    P.dma("sp", lambda: (cs.t[:], ropeC[:, bass.ds(tok0, TT)]), cs.s, writes=[cs.s])
                    P.dma("sp", lambda: (sn.t[:], ropeS[:, bass.ds(tok0, TT)]), sn.s, writes=[sn.s])
                    rms_to_bf16(xt, h, g1, ms, rs, junk)
                    cnt = [0]
                    transpose_blocks(h, 8, hT, pTs, cnt)
                    na = [0]
                    ns = [0]
                    nf = [0]

                    def group_fm(col0):
                        a = acc[na[0] % len(acc)]
                        na[0] += 1
                        P.mm([lambda e, kc=kc, a=a: e.matmul(a.t[:], lhsT=W.t[:, kc, col0:col0 + 128], rhs=hT.t[:, kc, :],
                                                             start=(kc == 0), stop=(kc == 7)) for kc in range(8)],
                             reads=[W.s, hT.s], writes=[a.s])
                        return a

                    def next_stg():
                        s_ = stg[ns[0] % len(stg)]
                        ns[0] += 1
                        return s_

                    for which, dst in ((0, QT), (1, KT)):
                        for g in range(4):
                            base = which * D + g * 256
                            pa = group_fm(base)
                            pb = group_fm(base + 128)
                            P.op("act", lambda e, pa=pa: e.activation(out=fa.t[:], in_=pa.t[:], func=AF.Copy),
                                 reads=[pa.s], writes=[fa.s])
                            P.op("act", lambda e, pb=pb: e.activation(out=fb.t[:], in_=pb.t[:], func=AF.Copy),
                                 reads=[pb.s], writes=[fb.s])
                            P.op("dve", lambda e: e.tensor_tensor(out=t1.t[:], in0=fa.t[:], in1=cs.t[:], op=ALU.mult),
                                 reads=[fa.s, cs.s], writes=[t1.s])
                            P.op("dve", lambda e: e.tensor_tensor(out=t2.t[:], in0=fb.t[:], in1=sn.t[:], op=ALU.mult),
                                 reads=[fb.s, sn.s], writes=[t2.s])
                            P.op("dve", lambda e: e.tensor_tensor(out=t3.t[:], in0=fb.t[:], in1=cs.t[:], op=ALU.mult),
                                 reads=[fb.s, cs.s], writes=[t3.s])
                            P.op("dve", lambda e: e.tensor_tensor(out=t4.t[:], in0=fa.t[:], in1=sn.t[:], op=ALU.mult),
                                 reads=[fa.s, sn.s], writes=[t4.s])
                            sa = next_stg()
                            P.op("dve", lambda e, sa=sa: e.tensor_tensor(out=sa.t[:], in0=t1.t[:], in1=t2.t[:], op=ALU.subtract),
                                 reads=[t1.s, t2.s], writes=[sa.s])
                            sb = next_stg()
                            P.op("dve", lambda e, sb=sb: e.tensor_tensor(out=sb.t[:], in0=t3.t[:], in1=t4.t[:], op=ALU.add),
                                 reads=[t3.s, t4.s], writes=[sb.s])
                            for half, st_ in ((0, sa), (1, sb)):
                                for m in range(4):
                                    r0 = (4 * g + m) * 64 + half * 32
                                    P.dma("sp", lambda st_=st_, m=m, r0=r0, dst=dst: (
                                        dst[r0:r0 + 32, bass.ds(tok0, TT)], st_.t[m * 32:(m + 1) * 32, :]),
                                        st_.s, reads=[st_.s])
                    for tb in range(4):
                        for hf in range(2):
                            a = acc[na[0] % len(acc)]
                            na[0] += 1
                            P.mm([lambda e, kc=kc, a=a, tb=tb, hf=hf: e.matmul(
                                a.t[:], lhsT=hT.t[:, kc, tb * 128:(tb + 1) * 128],
                                rhs=W.t[:, kc, 2 * D + hf * 512:2 * D + (hf + 1) * 512],
                                start=(kc == 0), stop=(kc == 7)) for kc in range(8)],
                                reads=[W.s, hT.s], writes=[a.s])
                            s_ = next_stg()
                            P.op("act", lambda e, a=a, s_=s_: e.activation(out=s_.t[:], in_=a.t[:], func=AF.Copy),
                                 reads=[a.s], writes=[s_.s])
                            P.dma("sp", lambda s_=s_, tb=tb, hf=hf: (
                                VV[bass.ds(tok0 + tb * 128, 128), hf * 512:(hf + 1) * 512], s_.t[:]), s_.s, reads=[s_.s])
                    for oc in range(8):
                        a = group_fm(3 * D + oc * 128)
                        f_ = stf[nf[0] % 2]
                        nf[0] += 1
                        P.op("dve", lambda e, a=a, f_=f_: e.tensor_copy(out=f_.t[:], in_=a.t[:]), reads=[a.s], writes=[f_.s])
                        P.dma("sp", lambda f_=f_, oc=oc: (UT[oc * 128:(oc + 1) * 128, bass.ds(tok0, TT)], f_.t[:]),
                              f_.s, reads=[f_.s])
                    for sec, dst, fn, sc in ((4, GLT, AF.Gelu_apprx_tanh, 1.0), (5, TGA, AF.Tanh, 0.5), (6, TGB, AF.Tanh, 0.5)):
                        for oc in range(8):
                            a = group_fm(sec * D + oc * 128)
                            s_ = next_stg()
                            P.op("act", lambda e, a=a, s_=s_, fn=fn, sc=sc: e.activation(out=s_.t[:], in_=a.t[:], func=fn, scale=sc),
                                 reads=[a.s], writes=[s_.s])
                            P.dma("sp", lambda s_=s_, oc=oc, dst=dst: (dst[oc * 128:(oc + 1) * 128, bass.ds(tok0, TT)], s_.t[:]),
                                  s_.s, reads=[s_.s])

                P.loop(ntile, body)
                P.phase_end(mark_)

        def phase2(l, lam_init):
            mark_ = P.phase_begin()
            Smax = max(cfg.s_p, cfg.s_s)
            with ExitStack() as es:
                KTh = P.tile(es, "kth", [128, Smax], BF16)
                QTh = P.tile(es, "qth", [128, Smax], BF16)
                Vh = P.tile(es, "vh", [128, Smax // 128, 128], BF16)
                ET = [P.tile(es, "et%d" % k, [128, 1024], BF16) for k in range(4)]
                sacc = P.tile(es, "sacc", [128, 1024], F32)
                onesf = P.tile(es, "onesf", [128, 128], F32)
                P.op("dve", lambda e: e.memset(onesf.t[:], 1.0), writes=[onesf.s])
                lv = P.tile(es, "lv", [128, 4, 64], F32)
                lj = P.tile(es, "lj", [128, 64], F32)
                ls = P.tile(es, "ls", [128, 2], F32)
                nlam = P.tile(es, "nlam", [128, 1], F32)
                gs = P.tile(es, "gs", [128, 1], F32)
                epsc = P.tile(es, "epsc", [128, 1], F32)
                P.op("dve", lambda e: e.memset(epsc.t[:], EPS), writes=[epsc.s])
                rs0 = P.tile(es, "rs0", [128, TT], F32)
                rs1 = P.tile(es, "rs1", [128, TT], F32)
                o0 = P.tile(es, "o0", [128, TT], F32)
                o1 = P.tile(es, "o1", [128, TT], F32)
                o2 = P.tile(es, "o2", [128, TT], F32)
                sq = P.tile(es, "sq", [128, TT], BF16)
                vv = P.tile(es, "vv2", [128, TT], F32)
                on = P.tile(es, "on", [128, TT], F32)
                oa = P.tile(es, "oa", [128, TT], BF16)
                ST = [P.tile(es, "st%d" % k, [128, 1024], F32, psum=True) for k in range(2)]
                O = [P.tile(es, "oacc%d" % k, [128, TT], F32, psum=True) for k in range(2)]
                SM = [P.tile(es, "sacc%d" % k, [128, TT], F32, psum=True) for k in range(2)]
                P.dma("sp", lambda: (lv.t[:], lamv[l].partition_broadcast(128)), lv.s, writes=[lv.s])
                for k in range(2):
                    P.op("dve", lambda e, k=k: e.scalar_tensor_tensor(out=lj.t[:], in0=lv.t[:, 2 * k, :], scalar=1.0,
                                                                       in1=lv.t[:, 2 * k + 1, :], op0=ALU.mult, op1=ALU.mult,
                                                                       accum_out=ls.t[:, k:k + 1]),
                         reads=[lv.s], writes=[lj.s, ls.s])
                P.op("act", lambda e: e.activation(out=ls.t[:], in_=ls.t[:], func=AF.Exp), writes=[ls.s])
                P.op("dve", lambda e: e.tensor_tensor(out=nlam.t[:], in0=ls.t[:, 1:2], in1=ls.t[:, 0:1], op=ALU.subtract),
                     reads=[ls.s], writes=[nlam.s])
                P.op("dve", lambda e: e.tensor_scalar(out=nlam.t[:], in0=nlam.t[:], scalar1=-lam_init, scalar2=None, op0=ALU.add),
                     writes=[nlam.s])
                P.dma("sp", lambda: (gs.t[:], subln[l].rearrange("(p o) -> p o", o=1)), gs.s, writes=[gs.s])
                P.op("dve", lambda e: e.tensor_scalar(out=gs.t[:], in0=gs.t[:], scalar1=1.0 - lam_init, scalar2=None, op0=ALU.mult),
                     writes=[gs.s])

                def head_body(S, tokf):
                    nkb = S // 128
                    nqt = S // TT

                    def body(hi):
                        tok0 = None if hi is None else tokf()
                        P.dma("sp", lambda: (KTh.t[:, :S], KT[bass.ds(hi * 128, 128), bass.ds(tok0, S)]), KTh.s, writes=[KTh.s])
                        P.dma("sp", lambda: (QTh.t[:, :S], QT[bass.ds(hi * 128, 128), bass.ds(tok0, S)]), QTh.s, writes=[QTh.s])
                        P.dma("sp", lambda: (Vh.t[:, :nkb, :],
                                             VV[bass.ds(tok0, S), bass.ds(hi * 128, 128)].rearrange("(kb p) v -> p kb v", p=128)),
                              Vh.s, writes=[Vh.s])
                        step = 0
                        pending = []
                        for qt in range(nqt):
                            qs = slice(qt * TT, (qt + 1) * TT)

                            def qk(kb, st):
                                ks = slice(kb * 128, (kb + 1) * 128)
                                P.mm([lambda e: e.matmul(st.t[:, 0:512], lhsT=KTh.t[0:64, ks], rhs=QTh.t[0:64, qs], start=True, stop=True),
                                      lambda e: e.matmul(st.t[:, 512:1024], lhsT=KTh.t[64:128, ks], rhs=QTh.t[64:128, qs], start=True, stop=True)],
                                     reads=[KTh.s, QTh.s], writes=[st.s])

                            def pv(kb, et):
                                fns = []
                                for mp in range(2):
                                    fns.append(lambda e, mp=mp: e.matmul(
                                        O[mp].t[:], lhsT=Vh.t[:, kb, :], rhs=et.t[:, mp * 512:(mp + 1) * 512],
                                        start=(kb == 0), stop=(kb == nkb - 1)))
                                fns.append(lambda e: e.matmul(SM[1].t[:], lhsT=ones.t[:], rhs=et.t[:, 512:1024],
                                                              start=(kb == 0), stop=(kb == nkb - 1)))
                                P.mm(fns, reads=[Vh.s, et.s, ones.s], writes=[O[0].s, O[1].s, SM[1].s])
                                if kb == 0:
                                    P.op("dve", lambda e: e.tensor_copy(out=sacc.t[:, 0:512], in_=et.t[:, 0:512]), reads=[et.s], writes=[sacc.s])
                                else:
                                    P.op("dve", lambda e: e.tensor_tensor(out=sacc.t[:, 0:512], in0=sacc.t[:, 0:512], in1=et.t[:, 0:512], op=ALU.add),
                                         reads=[et.s], writes=[sacc.s])

                            base = step
                            qk(0, ST[base % 2])
                            for kb in range(nkb):
                                st = ST[(base + kb) % 2]
                                et = ET[(base + kb) % 4]
                                P.op("act", lambda e, st=st, et=et: e.activation(out=et.t[:], in_=st.t[:], func=AF.Exp, scale=0.125),
                                     reads=[st.s], writes=[et.s])
                                if kb + 1 < nkb:
                                    qk(kb + 1, ST[(base + kb + 1) % 2])
                                pv(kb, et)
                                if kb >= 3 and pending:
                                    pending.pop(0)()
                            step = base + nkb
                            while pending:
                                pending.pop(0)()
                            P.op("dve", lambda e: e.tensor_copy(out=o0.t[:], in_=O[0].t[:]), reads=[O[0].s], writes=[o0.s])
                            P.op("dve", lambda e: e.tensor_copy(out=o1.t[:], in_=O[1].t[:]), reads=[O[1].s], writes=[o1.s])
                            P.op("dve", lambda e: e.tensor_copy(out=rs1.t[:], in_=SM[1].t[:]), reads=[SM[1].s], writes=[rs1.s])
                            P.mm([lambda e: e.matmul(SM[0].t[:], lhsT=onesf.t[:], rhs=sacc.t[:, 0:512], start=True, stop=True)],
                                 reads=[onesf.s, sacc.s], writes=[SM[0].s])
                            P.op("dve", lambda e: e.tensor_copy(out=rs0.t[:], in_=SM[0].t[:]), reads=[SM[0].s], writes=[rs0.s])

                            ops_b = []
                            def _op(qt=qt):
                                P.op("dve", lambda e: e.reciprocal(out=rs0.t[:], in_=rs0.t[:]), writes=[rs0.s])
                            ops_b.append(_op)
                            def _op(qt=qt):
                                P.op("dve", lambda e: e.tensor_tensor(out=o0.t[:], in0=o0.t[:], in1=rs0.t[:], op=ALU.mult),
                                     reads=[rs0.s], writes=[o0.s])
                            ops_b.append(_op)
                            def _op(qt=qt):
                                P.op("dve", lambda e: e.reciprocal(out=rs1.t[:], in_=rs1.t[:]), writes=[rs1.s])
                            ops_b.append(_op)
                            def _op(qt=qt):
                                P.op("dve", lambda e: e.tensor_tensor(out=o1.t[:], in0=o1.t[:], in1=rs1.t[:], op=ALU.mult),
                                     reads=[rs1.s], writes=[o1.s])
                            ops_b.append(_op)
                            def _op(qt=qt):
                                P.op("dve", lambda e: e.scalar_tensor_tensor(out=o2.t[:], in0=o1.t[:], scalar=nlam.t[:, 0:1], in1=o0.t[:],
                                                                              op0=ALU.mult, op1=ALU.add),
                                     reads=[o1.s, o0.s, nlam.s], writes=[o2.s])
                            ops_b.append(_op)
                            def _op(qt=qt):
                                P.op("act", lambda e: e.activation(out=sq.t[:], in_=o2.t[:], func=AF.Square), reads=[o2.s], writes=[sq.s])
                            ops_b.append(_op)
                            def _op(qt=qt):
                                P.mm([lambda e: e.matmul(SM[0].t[:], lhsT=ones.t[:], rhs=sq.t[:], start=True, stop=True)],
                                     reads=[ones.s, sq.s], writes=[SM[0].s])
                            ops_b.append(_op)
                            def _op(qt=qt):
                                P.op("act", lambda e: e.activation(out=vv.t[:], in_=SM[0].t[:], func=AF.Ln,
                                                                   scale=1.0 / 128.0, bias=epsc.t[:, 0:1]),
                                     reads=[SM[0].s, epsc.s], writes=[vv.s])
                            ops_b.append(_op)
                            def _op(qt=qt):
                                P.op("act", lambda e: e.activation(out=vv.t[:], in_=vv.t[:], func=AF.Exp, scale=-0.5), writes=[vv.s])
                            ops_b.append(_op)
                            def _op(qt=qt):
                                P.op("dve", lambda e: e.tensor_tensor(out=on.t[:], in0=o2.t[:], in1=vv.t[:], op=ALU.mult),
                                     reads=[o2.s, vv.s], writes=[on.s])
                            ops_b.append(_op)
                            def _op(qt=qt):
                                P.op("dve", lambda e: e.tensor_scalar(out=oa.t[:], in0=on.t[:], scalar1=gs.t[:, 0:1], scalar2=None, op0=ALU.mult),
                                     reads=[on.s, gs.s], writes=[oa.s])
                            ops_b.append(_op)
                            def _op(qt=qt):
                                P.dma("sp", lambda: (OAT[bass.ds(hi * 128, 128), bass.ds(tok0 + qt * TT, TT)], oa.t[:]),
                                      oa.s, reads=[oa.s])
                            ops_b.append(_op)
                            pending.extend(ops_b)
                        while pending:
                            pending.pop(0)()
                    return body

                if cfg.n_prompt > 0:
                    def seq_body(si):
                        P.loop(NH, head_body(cfg.s_p, lambda: si * cfg.s_p))
                    P.loop(cfg.n_prompt, seq_body)
                P.loop(NH, head_body(cfg.s_s, lambda: cfg.n_prompt * cfg.s_p))
                P.phase_end(mark_)

        def phase3(l):
            mark_ = P.phase_begin()
            Smax = max(cfg.s_p, cfg.s_s)
            with ExitStack() as es:
                upad = P.tile(es, "upad", [128, Smax + 3], F32)
                xc = P.tile(es, "xc", [128, Smax], F32)
                xcb = P.tile(es, "xcb", [128, Smax], BF16)
                glt = P.tile(es, "glt", [128, Smax], BF16)
                hf = P.tile(es, "hf", [128, Smax], F32)
                prm = P.tile(es, "prm", [128, 11], F32)
                drv = P.tile(es, "drv", [128, 8], F32)
                wrg = P.tile(es, "wrg", [128, 4, 128], BF16)
                er = P.tile(es, "er", [128, TT], F32)
                ei = P.tile(es, "ei", [128, TT], F32)
                av = P.tile(es, "av", [128, TT], F32)
                a2 = P.tile(es, "a2", [128, TT], F32)
                ix = P.tile(es, "ix", [128, TT], F32)
                bv = P.tile(es, "bv", [128, TT], F32)
                hb = [P.tile(es, "hb%d" % k, [128, TT], F32) for k in range(2)]
                ob = [P.tile(es, "ob%d" % k, [128, TT], BF16) for k in range(2)]
                psr = [P.tile(es, "psr%d" % k, [128, TT], F32, psum=True) for k in range(2)]
                psi = [P.tile(es, "psi%d" % k, [128, TT], F32, psum=True) for k in range(2)]
                P.op("dve", lambda e: e.memset(upad.t[:], 0.0), writes=[upad.s])

                def chunk_body(S, tokf):
                    nT = S // TT

                    def body(ci):
                        tok0 = None if ci is None else tokf()
                        P.dma("sp", lambda: (prm.t[:], lruprm[l, bass.ds(ci, 1), :, :].rearrange("o p k -> (o p) k")), prm.s, writes=[prm.s])
                        P.dma("pool", lambda: (wrg.t[:], rgw[l, bass.ds(ci, 1), :, :, :].rearrange("o f i j -> (o i) f j")), wrg.s, writes=[wrg.s])
                        P.dma("sp", lambda: (upad.t[:, 2:S + 2], UT[bass.ds(ci * 128, 128), bass.ds(tok0, S)]), upad.s, writes=[upad.s])
                        P.dma("sp", lambda: (glt.t[:, :S], GLT[bass.ds(ci * 128, 128), bass.ds(tok0, S)]), glt.s, writes=[glt.s])
                        P.op("dve", lambda e: e.tensor_scalar(out=drv.t[:, 0:4], in0=prm.t[:, 5:9], scalar1=-1.0, scalar2=None, op0=ALU.mult),
                             reads=[prm.s], writes=[drv.s])
                        P.op("act", lambda e: e.activation(out=drv.t[:, 4:6], in_=prm.t[:, 9:11], func=AF.Exp, scale=-1.0),
                             reads=[prm.s], writes=[drv.s])
                        P.op("act", lambda e: e.activation(out=drv.t[:, 4:6], in_=drv.t[:, 4:6], func=AF.Ln, bias=1.0, scale=1.0),
                             writes=[drv.s])
                        P.op("dve", lambda e: e.tensor_scalar(out=drv.t[:, 6:8], in0=drv.t[:, 4:6], scalar1=-2.0 * LRU_C, scalar2=None, op0=ALU.mult),
                             writes=[drv.s])
                        P.op("dve", lambda e: e.tensor_scalar(out=drv.t[:, 4:6], in0=drv.t[:, 4:6], scalar1=-LRU_C, scalar2=None, op0=ALU.mult),
                             writes=[drv.s])
                        P.op("dve", lambda e: e.tensor_scalar(out=xc.t[:, :S], in0=upad.t[:, 0:S], scalar1=prm.t[:, 0:1], scalar2=prm.t[:, 4:5],
                                                              op0=ALU.mult, op1=ALU.add),
                             reads=[upad.s, prm.s], writes=[xc.s])
                        for k in range(1, 4):
                            P.op("dve", lambda e, k=k: e.scalar_tensor_tensor(out=xc.t[:, :S], in0=upad.t[:, k:S + k], scalar=prm.t[:, k:k + 1],
                                                                               in1=xc.t[:, :S], op0=ALU.mult, op1=ALU.add),
                                 reads=[upad.s, prm.s], writes=[xc.s])
                        P.op("act", lambda e: e.activation(out=xcb.t[:, :S], in_=xc.t[:, :S], func=AF.Copy), reads=[xc.s], writes=[xcb.s])
                        n = 0
                        for d in range(2):
                            order = range(nT) if d == 0 else range(nT - 1, -1, -1)
                            prev_hb = None
                            for j in order:
                                cs_ = slice(j * TT, (j + 1) * TT)
                                pr = psr[n % 2]
                                pi = psi[n % 2]
                                P.mm([lambda e, pr=pr, cs_=cs_: e.matmul(pr.t[:], lhsT=wrg.t[:, 2 * d, :], rhs=xcb.t[:, cs_], start=True, stop=True)],
                                     reads=[wrg.s, xcb.s], writes=[pr.s])
                                P.mm([lambda e, pi=pi, cs_=cs_: e.matmul(pi.t[:], lhsT=wrg.t[:, 2 * d + 1, :], rhs=xcb.t[:, cs_], start=True, stop=True)],
                                     reads=[wrg.s, xcb.s], writes=[pi.s])
                                P.op("act", lambda e, pr=pr: e.activation(out=er.t[:], in_=pr.t[:], func=AF.Exp, scale=-1.0, bias=drv.t[:, d:d + 1]),
                                     reads=[pr.s, drv.s], writes=[er.s])
                                P.op("act", lambda e, pi=pi: e.activation(out=ei.t[:], in_=pi.t[:], func=AF.Exp, scale=-1.0, bias=drv.t[:, 2 + d:3 + d]),
                                     reads=[pi.s, drv.s], writes=[ei.s])
                                P.op("act", lambda e: e.activation(out=er.t[:], in_=er.t[:], func=AF.Ln, bias=1.0, scale=1.0), writes=[er.s])
                                P.op("act", lambda e: e.activation(out=er.t[:], in_=er.t[:], func=AF.Exp, scale=-1.0), writes=[er.s])
                                P.op("act", lambda e: e.activation(out=ei.t[:], in_=ei.t[:], func=AF.Ln, bias=1.0, scale=1.0), writes=[ei.s])
                                P.op("act", lambda e: e.activation(out=ei.t[:], in_=ei.t[:], func=AF.Exp, scale=-1.0), writes=[ei.s])
                                P.op("act", lambda e: e.activation(out=av.t[:], in_=er.t[:], func=AF.Exp, scale=drv.t[:, 4 + d:5 + d]),
                                     reads=[er.s, drv.s], writes=[av.s])
                                P.op("act", lambda e: e.activation(out=a2.t[:], in_=er.t[:], func=AF.Exp, scale=drv.t[:, 6 + d:7 + d]),
                                     reads=[er.s, drv.s], writes=[a2.s])
                                P.op("act", lambda e: e.activation(out=a2.t[:], in_=a2.t[:], func=AF.Ln, scale=-1.0, bias=1.0), writes=[a2.s])
                                P.op("act", lambda e: e.activation(out=a2.t[:], in_=a2.t[:], func=AF.Exp, scale=0.5), writes=[a2.s])
                                P.op("dve", lambda e, cs_=cs_: e.tensor_tensor(out=ix.t[:], in0=ei.t[:], in1=xc.t[:, cs_], op=ALU.mult),
                                     reads=[ei.s, xc.s], writes=[ix.s])
                                P.op("dve", lambda e: e.tensor_tensor(out=bv.t[:], in0=ix.t[:], in1=a2.t[:], op=ALU.mult),
                                     reads=[ix.s, a2.s], writes=[bv.s])
                                if d == 0:
                                    init = 0.0 if j == 0 else hf.t[:, j * TT - 1:j * TT]
                                    P.op("dve", lambda e, cs_=cs_, init=init: e.tensor_tensor_scan(
                                        out=hf.t[:, cs_], data0=av.t[:], data1=bv.t[:], initial=init, op0=ALU.mult, op1=ALU.add),
                                        reads=[av.s, bv.s], writes=[hf.s])
                                else:
                                    hbt = hb[n % 2]
                                    init = 0.0 if prev_hb is None else prev_hb.t[:, 0:1]
                                    rd = [av.s, bv.s] + ([prev_hb.s] if prev_hb is not None else [])
                                    P.op("dve", lambda e, hbt=hbt, init=init: e.tensor_tensor_scan(
                                        out=hbt.t[:, ::-1], data0=av.t[:, ::-1], data1=bv.t[:, ::-1], initial=init, op0=ALU.mult, op1=ALU.add),
                                        reads=rd, writes=[hbt.s])
                                    prev_hb = hbt
                                    P.op("pool", lambda e, hbt=hbt, cs_=cs_: e.tensor_tensor(out=ix.t[:], in0=hbt.t[:], in1=hf.t[:, cs_], op=ALU.add),
                                         reads=[hbt.s, hf.s], writes=[ix.s])
                                    obt = ob[n % 2]
                                    P.op("pool", lambda e, obt=obt, cs_=cs_: e.tensor_tensor(out=obt.t[:], in0=ix.t[:], in1=glt.t[:, cs_], op=ALU.mult),
                                         reads=[ix.s, glt.s], writes=[obt.s])
                                    P.dma("sp", lambda obt=obt, j=j: (OLT[bass.ds(ci * 128, 128), bass.ds(tok0 + j * TT, TT)], obt.t[:]),
                                          obt.s, reads=[obt.s])
                                n += 1
                    return body

                if cfg.n_prompt > 0:
                    def seq_body(si):
                        P.loop(8, chunk_body(cfg.s_p, lambda: si * cfg.s_p))
                    P.loop(cfg.n_prompt, seq_body)
                P.loop(8, chunk_body(cfg.s_s, lambda: cfg.n_prompt * cfg.s_p))
                P.phase_end(mark_)

        def phase4(l, Xsrc):
            mark_ = P.phase_begin()
            with ExitStack() as es:
                Wa = P.tile(es, "wa4", [128, 8, D], BF16)
                Wl = P.tile(es, "wl4", [128, 8, D], BF16)
                Wo = P.tile(es, "wo4", [128, 8, D], BF16)
                gpo = P.tile(es, "gpo4", [128, D], F32)
                gfp = P.tile(es, "gfp4", [128, D], F32)
                oat = P.tile(es, "oat4", [128, 8, TT], BF16)
                olt = P.tile(es, "olt4", [128, 8, TT], BF16)
                tga = P.tile(es, "tga4", [128, 8, TT], BF16)
                tgb = P.tile(es, "tgb4", [128, 8, TT], BF16)
                xt = P.tile(es, "xt4", [128, 4, D], F32)
                mT = P.tile(es, "mT4", [128, 8, TT], BF16)
                yt = P.tile(es, "yt4", [128, 4, D], F32)
                h2 = P.tile(es, "h24", [128, 4, D], BF16)
                h2T = P.tile(es, "h2T4", [128, 8, TT], BF16)
                junk = P.tile(es, "junk4", [128, D], BF16)
                ms = P.tile(es, "ms4", [128, 4], F32)
                rs = P.tile(es, "rs4", [128, 4], F32)
                ms2 = P.tile(es, "ms42", [128, 4], F32)
                rs2 = P.tile(es, "rs42", [128, 4], F32)
                ta = P.tile(es, "ta4", [128, TT], F32)
                tb_ = P.tile(es, "tb4", [128, TT], F32)
                pTs = [P.tile(es, "pT4_%d" % k, [128, TT], BF16, psum=True) for k in range(2)]
                acc = [P.tile(es, "acc4_%d" % k, [128, TT], F32, psum=True) for k in range(6)]
                load_w(Wa, w_ba[l], 8)
                load_w(Wl, w_bl[l], 8)
                load_w(Wo, w_out[l], 8)
                load_bcast(gpo, g_post[l])
                load_bcast(gfp, g_fpre[l])

                def mk_body(tokf, colf):
                    def body(i):
                        tok0 = None if i is None else tokf(i)
                        col0 = None if i is None else colf(i)
                        for T_, src in ((oat, OAT), (olt, OLT), (tga, TGA), (tgb, TGB)):
                            P.dma("sp", lambda T_=T_, src=src: (T_.t[:], src[:, bass.ds(tok0, TT)].rearrange("(c p) t -> p c t", p=128)),
                                  T_.s, writes=[T_.s])
                        P.dma("sp", lambda: (xt.t[:], Xsrc[bass.ds(tok0, TT), :].rearrange("(b p) d -> p b d", p=128)), xt.s, writes=[xt.s])
                        na = 0
                        for oc in range(8):
                            pa = acc[na % 6]
                            pb = acc[(na + 1) % 6]
                            na += 2
                            P.mm([lambda e, kc=kc, pa=pa, oc=oc: e.matmul(pa.t[:], lhsT=Wa.t[:, kc, oc * 128:(oc + 1) * 128], rhs=oat.t[:, kc, :],
                                                                           start=(kc == 0), stop=(kc == 7)) for kc in range(8)],
                                 reads=[Wa.s, oat.s], writes=[pa.s])
                            P.mm([lambda e, kc=kc, pb=pb, oc=oc: e.matmul(pb.t[:], lhsT=Wl.t[:, kc, oc * 128:(oc + 1) * 128], rhs=olt.t[:, kc, :],
                                                                           start=(kc == 0), stop=(kc == 7)) for kc in range(8)],
                                 reads=[Wl.s, olt.s], writes=[pb.s])
                            P.op("dve", lambda e, pa=pa, oc=oc: e.scalar_tensor_tensor(out=ta.t[:], in0=tga.t[:, oc, :], scalar=1.0, in1=pa.t[:],
                                                                                        op0=ALU.add, op1=ALU.mult),
                                 reads=[tga.s, pa.s], writes=[ta.s])
                            P.op("dve", lambda e, pb=pb, oc=oc: e.scalar_tensor_tensor(out=tb_.t[:], in0=tgb.t[:, oc, :], scalar=1.0, in1=pb.t[:],
                                                                                        op0=ALU.add, op1=ALU.mult),
                                 reads=[tgb.s, pb.s], writes=[tb_.s])
                            P.op("dve", lambda e, oc=oc: e.tensor_tensor(out=mT.t[:, oc, :], in0=ta.t[:], in1=tb_.t[:], op=ALU.add),
                                 reads=[ta.s, tb_.s], writes=[mT.s])
                        for tb in range(4):
                            for hf_ in range(2):
                                a = acc[na % 6]
                                na += 1
                                P.mm([lambda e, kc=kc, a=a, tb=tb, hf_=hf_: e.matmul(a.t[:], lhsT=mT.t[:, kc, tb * 128:(tb + 1) * 128],
                                                                                     rhs=Wo.t[:, kc, hf_ * 512:(hf_ + 1) * 512],
                                                                                     start=(kc == 0), stop=(kc == 7)) for kc in range(8)],
                                     reads=[Wo.s, mT.s], writes=[a.s])
                                P.op("act", lambda e, a=a, tb=tb, hf_=hf_: e.activation(out=yt.t[:, tb, hf_ * 512:(hf_ + 1) * 512], in_=a.t[:], func=AF.Copy),
                                     reads=[a.s], writes=[yt.s])
                        for b in range(4):
                            P.op("act", lambda e, b=b: e.activation(out=junk.t[:], in_=yt.t[:, b, :], func=AF.Square, scale=1.0 / 32.0,
                                                                     accum_out=ms.t[:, b:b + 1]),
                                 reads=[yt.s], writes=[junk.s, ms.s])
                        P.op("pool", lambda e: e.tensor_scalar(out=rs.t[:], in0=ms.t[:], scalar1=4.0 * EPS, scalar2=None, op0=ALU.add),
                             reads=[ms.s], writes=[rs.s])
                        P.op("pool", lambda e: e.tensor_tensor(out=rs.t[:], in0=rs.t[:], in1=mhalf.t[:, 0:4], op=ALU.pow), reads=[mhalf.s], writes=[rs.s])
                        for b in range(4):
                            P.op("dve", lambda e, b=b: e.scalar_tensor_tensor(out=yt.t[:, b, :], in0=yt.t[:, b, :], scalar=rs.t[:, b:b + 1], in1=gpo.t[:],
                                                                               op0=ALU.mult, op1=ALU.mult),
                                 reads=[rs.s, gpo.s], writes=[yt.s])
                            P.op("dve", lambda e, b=b: e.tensor_tensor(out=xt.t[:, b, :], in0=xt.t[:, b, :], in1=yt.t[:, b, :], op=ALU.add),
                                 reads=[yt.s], writes=[xt.s])
                        P.dma("sp", lambda: (X1[bass.ds(tok0, TT), :].rearrange("(b p) d -> p b d", p=128), xt.t[:]), xt.s, reads=[xt.s])
                        rms_to_bf16(xt, h2, gfp, ms2, rs2, junk)
                        cnt = [0]
                        transpose_blocks(h2, 8, h2T, pTs, cnt)
                        P.dma("sp", lambda: (H2T[:, bass.ds(col0, TT)].rearrange("(c p) t -> p c t", p=128), h2T.t[:]), h2T.s, reads=[h2T.s])
                    return body

                run_token_loops(mk_body)
                P.phase_end(mark_)

        def run_token_loops(mk_body):
            tp = cfg.s_p // TT
            if cfg.n_prompt > 0:
                def seq_body(si):
                    P.loop(tp, mk_body(lambda j: si * cfg.s_p + j * TT, lambda j: si * (cfg.s_p + 2) + j * TT + 1))
                P.loop(cfg.n_prompt, seq_body)
            b0 = cfg.n_prompt * cfg.s_p
            c0 = cfg.n_prompt * (cfg.s_p + 2) + 1
            P.loop(cfg.s_s // TT, mk_body(lambda j: j * TT + b0, lambda j: j * TT + c0))

        def phase5(l):
            mark_ = P.phase_begin()
            with ExitStack() as es:
                Wu = P.tile(es, "wu5", [128, 8, 2 * DFF], BF16)
                fp = P.tile(es, "fp5", [128, 24, 4], F32)
                h2e = P.tile(es, "h2e5", [128, 8, TT + 2], BF16)
                cext = [P.tile(es, "cext5_%d" % k, [128, TT + 2], F32) for k in range(2)]
                cacc = [P.tile(es, "cacc5_%d" % k, [128, TT], F32) for k in range(2)]
                gg = [P.tile(es, "gg5_%d" % k, [128, TT], F32) for k in range(2)]
                go = [P.tile(es, "go5_%d" % k, [128, TT], BF16) for k in range(3)]
                psc = [P.tile(es, "psc5_%d" % k, [128, TT], F32, psum=True) for k in range(3)]
                psl = [P.tile(es, "psl5_%d" % k, [128, TT], F32, psum=True) for k in range(3)]
                psh = [P.tile(es, "psh5_%d" % k, [128, 2], F32, psum=True) for k in range(2)]
                load_w(Wu, w_up[l], 8)
                P.dma("sp", lambda: (fp.t[:], ffnprm[l]), fp.s, writes=[fp.s])

                def mk_body(tokf, colf):
                    def body(i):
                        tok0 = None if i is None else tokf(i)
                        col0 = None if i is None else colf(i)
                        P.dma("sp", lambda: (h2e.t[:], H2T[:, bass.ds(col0 - 1, TT + 2)].rearrange("(c p) t -> p c t", p=128)), h2e.s, writes=[h2e.s])
                        for j in range(24):
                            pc = psc[j % 3]
                            pl = psl[j % 3]
                            ph = psh[j % 2]
                            ce = cext[j % 2]
                            ca = cacc[j % 2]
                            g_ = gg[j % 2]
                            o_ = go[j % 3]
                            P.mm([lambda e, kc=kc, pc=pc, j=j: e.matmul(pc.t[:], lhsT=Wu.t[:, kc, j * 128:(j + 1) * 128], rhs=h2e.t[:, kc, 1:TT + 1],
                                                                         start=(kc == 0), stop=(kc == 7)) for kc in range(8)],
                                 reads=[Wu.s, h2e.s], writes=[pc.s])
                            P.mm([lambda e, kc=kc, ph=ph, j=j: e.matmul(ph.t[:], lhsT=Wu.t[:, kc, j * 128:(j + 1) * 128], rhs=h2e.t[:, kc, 0:TT + 2:TT + 1],
                                                                         start=(kc == 0), stop=(kc == 7)) for kc in range(8)],
                                 reads=[Wu.s, h2e.s], writes=[ph.s])
                            P.mm([lambda e, kc=kc, pl=pl, j=j: e.matmul(pl.t[:], lhsT=Wu.t[:, kc, DFF + j * 128:DFF + (j + 1) * 128], rhs=h2e.t[:, kc, 1:TT + 1],
                                                                         start=(kc == 0), stop=(kc == 7)) for kc in range(8)],
                                 reads=[Wu.s, h2e.s], writes=[pl.s])
                            P.op("act", lambda e, pc=pc, ce=ce: e.activation(out=ce.t[:, 1:TT + 1], in_=pc.t[:], func=AF.Copy), reads=[pc.s], writes=[ce.s])
                            P.op("dve", lambda e, ph=ph, ce=ce: e.tensor_copy(out=ce.t[:, 0:TT + 2:TT + 1], in_=ph.t[:]), reads=[ph.s], writes=[ce.s])
                            P.op("dve", lambda e, ce=ce, ca=ca, j=j: e.tensor_scalar(out=ca.t[:], in0=ce.t[:, 0:TT], scalar1=fp.t[:, j, 0:1], scalar2=fp.t[:, j, 3:4],
                                                                                       op0=ALU.mult, op1=ALU.add),
                                 reads=[ce.s, fp.s], writes=[ca.s])
                            for k in (1, 2):
                                P.op("dve", lambda e, ce=ce, ca=ca, j=j, k=k: e.scalar_tensor_tensor(out=ca.t[:], in0=ce.t[:, k:TT + k], scalar=fp.t[:, j, k:k + 1],
                                                                                                       in1=ca.t[:], op0=ALU.mult, op1=ALU.add),
                                     reads=[ce.s, fp.s], writes=[ca.s])
                            P.op("act", lambda e, ca=ca, g_=g_: e.activation(out=g_.t[:], in_=ca.t[:], func=AF.Gelu_apprx_tanh), reads=[ca.s], writes=[g_.s])
                            P.op("dve", lambda e, g_=g_, pl=pl, o_=o_: e.tensor_tensor(out=o_.t[:], in0=g_.t[:], in1=pl.t[:], op=ALU.mult),
                                 reads=[g_.s, pl.s], writes=[o_.s])
                            P.dma("sp", lambda o_=o_, j=j: (GT[j * 128:(j + 1) * 128, bass.ds(tok0, TT)], o_.t[:]), o_.s, reads=[o_.s])
                    return body

                run_token_loops(mk_body)
                P.phase_end(mark_)

        def phase6(l, Xdst):
            mark_ = P.phase_begin()
            with ExitStack() as es:
                Wd = P.tile(es, "wd6", [128, 24, D], BF16)
                Wg = P.tile(es, "wg6", [128, 8, D], BF16)
                Wp = P.tile(es, "wp6", [128, 2, D], BF16)
                gfo = P.tile(es, "gfo6", [128, D], F32)
                gT = P.tile(es, "gT6", [128, 24, TT], BF16)
                xt = P.tile(es, "xt6", [128, 4, D], F32)
                pt = P.tile(es, "pt6", [128, 4, PLE], F32)
                ft = P.tile(es, "ft6", [128, 4, D], F32)
                xb = P.tile(es, "xb6", [128, 4, D], BF16)
                pb = P.tile(es, "pb6", [128, 4, PLE], BF16)
                xT = P.tile(es, "xT6", [128, 8, TT], BF16)
                pT_ = P.tile(es, "ppT6", [128, 2, TT], BF16)
                junk = P.tile(es, "junk6", [128, D], BF16)
                ms = P.tile(es, "ms6", [128, 4], F32)
                rs = P.tile(es, "rs6", [128, 4], F32)
                tg = [P.tile(es, "tg6_%d" % k, [128, TT], F32) for k in range(2)]
                pTs = [P.tile(es, "pT6_%d" % k, [128, TT], BF16, psum=True) for k in range(2)]
                acc = [P.tile(es, "acc6_%d" % k, [128, TT], F32, psum=True) for k in range(6)]
                load_w(Wd, w_dn[l], 24)
                load_w(Wg, w_pg[l], 8)
                load_w(Wp, w_pp[l], 2)
                load_bcast(gfo, g_fpost[l])

                def body(i):
                    tok0 = None if i is None else i * TT
                    P.dma("sp", lambda: (gT.t[:], GT[:, bass.ds(tok0, TT)].rearrange("(c p) t -> p c t", p=128)), gT.s, writes=[gT.s])
                    P.dma("sp", lambda: (xt.t[:], X1[bass.ds(tok0, TT), :].rearrange("(b p) d -> p b d", p=128)), xt.s, writes=[xt.s])
                    P.dma("sp", lambda: (pt.t[:], p_in[l, bass.ds(tok0, TT), :].rearrange("(b p) d -> p b d", p=128)), pt.s, writes=[pt.s])
                    na = 0
                    for tb in range(4):
                        for hf_ in range(2):
                            a = acc[na % 6]
                            na += 1
                            P.mm([lambda e, kc=kc, a=a, tb=tb, hf_=hf_: e.matmul(a.t[:], lhsT=gT.t[:, kc, tb * 128:(tb + 1) * 128],
                                                                                 rhs=Wd.t[:, kc, hf_ * 512:(hf_ + 1) * 512],
                                                                                 start=(kc == 0), stop=(kc == 23)) for kc in range(24)],
                                 reads=[Wd.s, gT.s], writes=[a.s])
                            P.op("act", lambda e, a=a, tb=tb, hf_=hf_: e.activation(out=ft.t[:, tb, hf_ * 512:(hf_ + 1) * 512], in_=a.t[:], func=AF.Copy),
                                 reads=[a.s], writes=[ft.s])
                    for b in range(4):
                        P.op("act", lambda e, b=b: e.activation(out=junk.t[:], in_=ft.t[:, b, :], func=AF.Square, scale=1.0 / 32.0,
                                                                 accum_out=ms.t[:, b:b + 1]),
                             reads=[ft.s], writes=[junk.s, ms.s])
                    P.op("pool", lambda e: e.tensor_scalar(out=rs.t[:], in0=ms.t[:], scalar1=EPS, scalar2=None, op0=ALU.add), reads=[ms.s], writes=[rs.s])
                    P.op("pool", lambda e: e.tensor_tensor(out=rs.t[:], in0=rs.t[:], in1=mhalf.t[:, 0:4], op=ALU.pow), reads=[mhalf.s], writes=[rs.s])
                    for b in range(4):
                        P.op("dve", lambda e, b=b: e.scalar_tensor_tensor(out=ft.t[:, b, :], in0=ft.t[:, b, :], scalar=rs.t[:, b:b + 1], in1=gfo.t[:],
                                                                           op0=ALU.mult, op1=ALU.mult),
                             reads=[rs.s, gfo.s], writes=[ft.s])
                        P.op("dve", lambda e, b=b: e.tensor_tensor(out=xt.t[:, b, :], in0=xt.t[:, b, :], in1=ft.t[:, b, :], op=ALU.add),
                             reads=[ft.s], writes=[xt.s])
                    P.op("act", lambda e: e.activation(out=xb.t[:], in_=xt.t[:], func=AF.Copy), reads=[xt.s], writes=[xb.s])
                    P.op("act", lambda e: e.activation(out=pb.t[:], in_=pt.t[:], func=AF.Copy), reads=[pt.s], writes=[pb.s])
                    cnt = [0]
                    transpose_blocks(xb, 8, xT, pTs, cnt)
                    transpose_blocks(pb, 2, pT_, pTs, cnt)
                    n2 = 0
                    for tb in range(4):
                        for hf_ in range(2):
                            ag = acc[na % 6]
                            ap_ = acc[(na + 1) % 6]
                            na += 2
                            t_ = tg[n2 % 2]
                            n2 += 1
                            P.mm([lambda e, kc=kc, ag=ag, tb=tb, hf_=hf_: e.matmul(ag.t[:], lhsT=xT.t[:, kc, tb * 128:(tb + 1) * 128],
                                                                                   rhs=Wg.t[:, kc, hf_ * 512:(hf_ + 1) * 512],
                                                                                   start=(kc == 0), stop=(kc == 7)) for kc in range(8)],
                                 reads=[Wg.s, xT.s], writes=[ag.s])
                            P.mm([lambda e, kc=kc, ap_=ap_, tb=tb, hf_=hf_: e.matmul(ap_.t[:], lhsT=pT_.t[:, kc, tb * 128:(tb + 1) * 128],
                                                                                     rhs=Wp.t[:, kc, hf_ * 512:(hf_ + 1) * 512],
                                                                                     start=(kc == 0), stop=(kc == 1)) for kc in range(2)],
                                 reads=[Wp.s, pT_.s], writes=[ap_.s])
                            P.op("act", lambda e, ag=ag, t_=t_: e.activation(out=t_.t[:], in_=ag.t[:], func=AF.Tanh, scale=0.5), reads=[ag.s], writes=[t_.s])
                            P.op("dve", lambda e, ap_=ap_, t_=t_: e.scalar_tensor_tensor(out=t_.t[:], in0=t_.t[:], scalar=1.0, in1=ap_.t[:],
                                                                                          op0=ALU.add, op1=ALU.mult),
                                 reads=[ap_.s], writes=[t_.s])
                            P.op("dve", lambda e, t_=t_, tb=tb, hf_=hf_: e.scalar_tensor_tensor(
                                out=ft.t[:, tb, hf_ * 512:(hf_ + 1) * 512], in0=t_.t[:], scalar=0.5, in1=xt.t[:, tb, hf_ * 512:(hf_ + 1) * 512],
                                op0=ALU.mult, op1=ALU.add),
                                reads=[t_.s, xt.s], writes=[ft.s])
                    P.dma("sp", lambda: (Xdst[bass.ds(tok0, TT), :].rearrange("(b p) d -> p b d", p=128), ft.t[:]), ft.s, reads=[ft.s])

                P.loop(ntile, body)
                P.phase_end(mark_)

        for l in range(L):
            lam_init = 0.8 - 0.6 * math.exp(-0.3 * l)
            Xsrc = x_in if l == 0 else XS
            Xdst = y_out if l == L - 1 else XS
            phase1(l, Xsrc)
            phase2(l, lam_init)
            phase3(l)
            phase4(l, Xsrc)
            phase5(l)
            phase6(l, Xdst)
        P.barrier()
    return nc


def _rope_tables(cfg):
    inv = THETA ** (-np.arange(0, 64, 2, dtype=np.float32) / 64.0)
    pos = np.concatenate([np.arange(S, dtype=np.float32) for (_, S) in cfg.seqs])
    ang = pos[None, :] * inv[:, None].astype(np.float32)
    c = np.tile(np.cos(ang).astype(np.float32), (4, 1))
    s = np.tile(np.sin(ang).astype(np.float32), (4, 1))
    return np.ascontiguousarray(c), np.ascontiguousarray(s)


def _qk_perm():
    perm = []
    for g in range(4):
        for half in range(2):
            for m in range(4):
                for d in range(32):
                    perm.append((4 * g + m) * 64 + half * 32 + d)
    return np.array(perm)


def prep_shared(inp, L):
    f = lambda k: np.ascontiguousarray(np.asarray(inp[k], dtype=np.float32))
    w_in = f("w_in").copy()
    perm = _qk_perm()
    w_in[:, :, 0:D] = w_in[:, :, 0:D][:, :, perm]
    w_in[:, :, D:2 * D] = w_in[:, :, D:2 * D][:, :, perm]
    lamv = np.stack([f("lam_q1"), f("lam_k1"), f("lam_q2"), f("lam_k2")], axis=1)
    cw = f("lru_conv_w")
    cols = [cw[:, k, :] for k in range(4)] + [f("lru_conv_b")]
    cols += [f("rg_ba")[:, 0], f("rg_ba")[:, 1], f("rg_bx")[:, 0], f("rg_bx")[:, 1], f("rg_lambda")[:, 0], f("rg_lambda")[:, 1]]
    lruprm = np.stack(cols, axis=-1).reshape(L, 8, 128, 11)
    wa, wx = f("rg_wa"), f("rg_wx")
    rgw = np.stack([wa[:, 0], wx[:, 0], wa[:, 1], wx[:, 1]], axis=2)
    fw = f("ffn_conv_w")
    fcols = [fw[:, k, :] for k in range(3)] + [f("ffn_conv_b")]
    ffnprm = np.stack(fcols, axis=-1).reshape(L, 24, 128, 4).transpose(0, 2, 1, 3)
    sh = {
        "ident": np.eye(128, dtype=np.float32),
        "w_in": np.ascontiguousarray(w_in),
        "norm_mix_pre": f("norm_mix_pre"), "norm_mix_post": f("norm_mix_post"),
        "lamv": np.ascontiguousarray(lamv), "subln_g": f("subln_g"),
        "lruprm": np.ascontiguousarray(lruprm), "rgw": np.ascontiguousarray(rgw),
        "w_branch_attn": f("w_branch_attn"), "w_branch_lru": f("w_branch_lru"), "w_out": f("w_out"),
        "norm_ffn_pre": f("norm_ffn_pre"), "norm_ffn_post": f("norm_ffn_post"),
        "w_ffn_up": f("w_ffn_up"), "ffnprm": np.ascontiguousarray(ffnprm),
        "w_ffn_down": f("w_ffn_down"), "w_ple_proj": f("w_ple_proj"), "w_ple_gate": f("w_ple_gate"),
    }
    return sh


def run(cfg, inp, n_cores, debug=False):
    L = cfg.L
    sh = prep_shared(inp, L)
    rc, rs = _rope_tables(cfg)
    xp = np.asarray(inp["x_prompt"], dtype=np.float32)
    xs = np.asarray(inp["x_sample"], dtype=np.float32)
    pp = np.asarray(inp["p_prompt"], dtype=np.float32)
    ps = np.asarray(inp["p_sample"], dtype=np.float32)
    in_maps = []
    npc = cfg.n_prompt
    for c in range(n_cores):
        xc = np.concatenate([xp[c * npc:(c + 1) * npc].reshape(-1, D), xs[c].reshape(-1, D)], axis=0)
        pc = np.concatenate([pp[:, c * npc:(c + 1) * npc].reshape(L, -1, PLE), ps[:, c].reshape(L, -1, PLE)], axis=1)
        m = dict(sh)
        m["x"] = np.ascontiguousarray(xc)
        m["p"] = np.ascontiguousarray(pc)
        m["ropeC"] = rc
        m["ropeS"] = rs
        in_maps.append(m)
    nc = build(cfg, debug=debug)
    res = run_bass_kernel_spmd(nc, in_maps, core_ids=list(range(n_cores)))
    ys = [r["y"] for r in res.results]
    yp = np.stack([y[:npc * cfg.s_p].reshape(npc, cfg.s_p, D) for y in ys]).reshape(n_cores * npc, cfg.s_p, D)
    ysm = np.stack([y[npc * cfg.s_p:].reshape(cfg.s_s, D) for y in ys])
    return (yp.astype(np.float32), ysm.astype(np.float32)), res


def run_seqs(S, L, inp, xs, ps, n_cores, rounds):
    cfg = Cfg(0, 512, S, L)
    sh = prep_shared(inp, L)
    rc, rs = _rope_tables(cfg)
    nc = build(cfg)
    outs = []
    for r in range(rounds):
        in_maps = []
        for c in range(n_cores):
            k = r * n_cores + c
            m = dict(sh)
            m["x"] = np.ascontiguousarray(xs[k].reshape(S, D))
            m["p"] = np.ascontiguousarray(ps[:, k].reshape(L, S, PLE))
            m["ropeC"] = rc
            m["ropeS"] = rs
            in_maps.append(m)
        res = run_bass_kernel_spmd(nc, in_maps, core_ids=list(range(n_cores)))
        outs.extend([np.asarray(rr["y"], dtype=np.float32).reshape(S, D) for rr in res.results])
    return np.stack(outs)


def kernel(**inputs):
    cfg = Cfg(4, 2048, 8192, 2)
    out, _ = run(cfg, inputs, 8)
    return out
```

```python
import math
from contextlib import ExitStack
import numpy as np
import concourse.bass as bass
import concourse.mybir as mybir
from concourse.bass_utils import run_bass_kernel_spmd

F32 = mybir.dt.float32
BF16 = mybir.dt.bfloat16
AF = mybir.ActivationFunctionType
ALU = mybir.AluOpType

D = 1024
NH = 8
DFF = 3072
PLE = 256
INC = 7168
EPS = 1e-6
LRU_C = 8.0
THETA = 10000.0
TT = 512
UNROLL = True


class Sem:
    def __init__(self, handle, name):
        self.h = handle
        self.name = name
        self.const = 0


class Frame:
    def __init__(self, n):
        self.n = n
        self.per = None
        self.ivar = None


class Ev:
    __slots__ = ("sem", "const", "frames")

    def __init__(self, sem, const, frames):
        self.sem = sem
        self.const = const
        self.frames = frames


class Slot:
    def __init__(self, P, name):
        self.name = name
        self.w = None
        self.r = {}
        self.dsem = None
        P.slots.append(self)


class Tile:
    def __init__(self, P, t, name):
        self.t = t
        self.s = Slot(P, name)


class Prog:
    def __init__(self, nc, es):
        self.nc = nc
        self.es = es
        self.dry = 0
        self.frames = []
        self.slots = []
        self.sems = []
        self.eng = {"pe": nc.tensor, "act": nc.scalar, "dve": nc.vector, "pool": nc.gpsimd, "sp": nc.sync}
        self.esem = {k: self.new_sem("e_" + k) for k in ("pe", "act", "dve", "pool")}
        self.waited = {}
        self.free_dsems = []

    def alloc_dsem(self):
        if self.free_dsems:
            return self.free_dsems.pop()
        return self.new_sem("d_%d" % len(self.sems))

    def phase_begin(self):
        return len(self.slots)

    def phase_end(self, mark):
        self.barrier()
        for sl in self.slots[mark:]:
            if sl.dsem is not None:
                self.free_dsems.append(sl.dsem)
                sl.dsem = None
        del self.slots[mark:]

    def new_sem(self, name):
        h = self.es.enter_context(self.nc.semaphore(name))
        s = Sem(h, name)
        self.sems.append(s)
        return s

    def tile(self, es, name, shape, dtype, psum=False):
        self.ntile = getattr(self, "ntile", 0) + 1
        name = "t%d_%s" % (self.ntile, name)
        if psum:
            t = es.enter_context(self.nc.psum_tensor(name, shape, dtype))
        else:
            t = es.enter_context(self.nc.sbuf_tensor(name, shape, dtype))
        return Tile(self, t, name)

    def _val(self, ev):
        v = ev.const
        for f in ev.frames:
            c = f.per.get(ev.sem.name, 0)
            if c:
                v = f.ivar * c + v
        return v

    def _wait(self, eng, ev):
        if ev is None:
            return
        if eng == "pe" and ev.sem is self.esem["pe"]:
            return
        key = (eng, ev.sem.name)
        last = self.waited.get(key)
        if last is not None and last[1] == ev.frames and last[0] >= ev.const:
            return
        self.waited[key] = (ev.const, ev.frames)
        if not self.dry:
            self.eng[eng].wait_ge(ev.sem.h, self._val(ev))

    def _deps(self, eng, reads, writes):
        for sl in reads:
            self._wait(eng, sl.w)
        for sl in writes:
            self._wait(eng, sl.w)
            for ev in list(sl.r.values()):
                self._wait(eng, ev)

    def _update(self, key, ev, reads, writes):
        for sl in writes:
            sl.w = ev
            sl.r = {}
        for sl in reads:
            if sl not in writes:
                sl.r[key] = ev

    def op(self, eng, fn, reads=(), writes=()):
        self._deps(eng, reads, writes)
        s = self.esem[eng]
        s.const += 1
        if not self.dry:
            fn(self.eng[eng]).then_inc(s.h, 1)
        ev = Ev(s, s.const, tuple(self.frames))
        self._update(eng, ev, reads, writes)
        return ev

    def mm(self, fns, reads=(), writes=()):
        self._deps("pe", reads, writes)
        s = self.esem["pe"]
        s.const += 1
        if not self.dry:
            ins = None
            for fn in fns:
                ins = fn(self.nc.tensor)
            ins.then_inc(s.h, 1)
        ev = Ev(s, s.const, tuple(self.frames))
        self._update("pe", ev, reads, writes)
        return ev

    def dma(self, q, fn, slot, reads=(), writes=(), **kw):
        self._deps(q, reads, writes)
        if slot.dsem is None:
            slot.dsem = self.alloc_dsem()
        s = slot.dsem
        s.const += 16
        if not self.dry:
            o, i = fn()
            try:
                self.eng[q].dma_start(out=o, in_=i, **kw).then_inc(s.h, 16)
            except Exception:
                print("DMA FAIL", q, o, i)
                raise
        ev = Ev(s, s.const, tuple(self.frames))
        self._update(("dma", s.name), ev, reads, writes)
        return ev

    def barrier(self):
        if self.dry:
            return
        fr = tuple(self.frames)
        for eng in ("pe", "act", "dve", "pool", "sp"):
            for s in self.sems:
                if s.const == 0 and not any(f.per.get(s.name, 0) for f in fr):
                    continue
                self.eng[eng].wait_ge(s.h, self._val(Ev(s, s.const, fr)))
        self.waited = {}

    def loop(self, n, body):
        if getattr(self, "unroll", False):
            for it in range(n):
                body(it)
            return
        fr = Frame(n)
        snap = {s.name: s.const for s in self.sems}
        nsem0 = len(self.sems)
        self.frames.append(fr)
        self.dry += 1
        self.waited = {}
        body(None)
        self.dry -= 1
        self.frames.pop()
        fr.per = {s.name: s.const - snap.get(s.name, 0) for s in self.sems}
        final = {sl: (sl.w, dict(sl.r)) for sl in self.slots}
        for s in self.sems:
            s.const = snap.get(s.name, 0)

        def settle(ev):
            if ev is None or fr not in ev.frames:
                return ev
            return Ev(ev.sem, ev.const + (n - 1) * fr.per[ev.sem.name], tuple(f for f in ev.frames if f is not fr))

        def finish():
            for s in self.sems:
                s.const = snap.get(s.name, 0) + n * fr.per[s.name]
            for sl, (w, r) in final.items():
                sl.w = settle(w)
                sl.r = {k: settle(e) for k, e in r.items()}
            self.waited = {}

        if self.dry:
            finish()
            return
        self.barrier()

        def prev(ev):
            if ev is None or fr not in ev.frames:
                return None
            return Ev(ev.sem, ev.const - fr.per[ev.sem.name], ev.frames)

        with self.nc.Fori(0, n) as i:
            fr.ivar = i
            self.frames.append(fr)
            for sl, (w, r) in final.items():
                sl.w = prev(w)
                sl.r = {k: e2 for k, e2 in ((k, prev(e)) for k, e in r.items()) if e2 is not None}
            self.waited = {}
            body(i)
            self.frames.pop()
        finish()


class Cfg:
    def __init__(self, n_prompt, s_p, s_s, L):
        self.n_prompt, self.s_p, self.s_s, self.L = n_prompt, s_p, s_s, L
        self.seqs = [(k * s_p, s_p) for k in range(n_prompt)] + [(n_prompt * s_p, s_s)]
        self.NT = n_prompt * s_p + s_s
        self.nseq = n_prompt + 1
        self.NTP = self.NT + 2 * self.nseq


def build(cfg, debug=False):
    nc = bass.Bass("TRN2", target_bir_lowering=False)
    NT, L = cfg.NT, cfg.L
    ntile = NT // TT

    def din(name, shape, dt=F32):
        return nc.dram_tensor(name, shape, dt, kind="ExternalInput").ap()

    def dscr(name, shape, dt):
        if debug:
            return nc.dram_tensor(name, shape, dt, kind="ExternalOutput").ap()
        return nc.dram_tensor(name, shape, dt).ap()

    x_in = din("x", [NT, D])
    p_in = din("p", [L, NT, PLE])
    ropeC = din("ropeC", [128, NT])
    ropeS = din("ropeS", [128, NT])
    ident_in = din("ident", [128, 128])
    w_in = din("w_in", [L, D, INC])
    g_pre = din("norm_mix_pre", [L, D])
    g_post = din("norm_mix_post", [L, D])
    lamv = din("lamv", [L, 4, 64])
    subln = din("subln_g", [L, 128])
    lruprm = din("lruprm", [L, 8, 128, 11])
    rgw = din("rgw", [L, 8, 4, 128, 128])
    w_ba = din("w_branch_attn", [L, D, D])
    w_bl = din("w_branch_lru", [L, D, D])
    w_out = din("w_out", [L, D, D])
    g_fpre = din("norm_ffn_pre", [L, D])
    g_fpost = din("norm_ffn_post", [L, D])
    w_up = din("w_ffn_up", [L, D, 2 * DFF])
    ffnprm = din("ffnprm", [L, 128, 24, 4])
    w_dn = din("w_ffn_down", [L, DFF, D])
    w_pp = din("w_ple_proj", [L, PLE, D])
    w_pg = din("w_ple_gate", [L, D, D])
    y_out = nc.dram_tensor("y", [NT, D], F32, kind="ExternalOutput").ap()

    QT = dscr("QT", [D, NT], BF16)
    KT = dscr("KT", [D, NT], BF16)
    VV = dscr("VV", [NT, D], BF16)
    UT = dscr("UT", [D, NT], F32)
    GLT = dscr("GLT", [D, NT], BF16)
    TGA = dscr("TGA", [D, NT], BF16)
    TGB = dscr("TGB", [D, NT], BF16)
    OAT = dscr("OAT", [D, NT], BF16)
    OLT = dscr("OLT", [D, NT], BF16)
    X1 = dscr("X1", [NT, D], F32)
    H2T = dscr("H2T", [D, cfg.NTP], BF16)
    GT = dscr("GT", [DFF, NT], BF16)
    XS = dscr("XS", [NT, D], F32)

    top = ExitStack()
    with top:
        P = Prog(nc, top)
        P.unroll = UNROLL
        ident = P.tile(top, "ident", [128, 128], BF16)
        ones = P.tile(top, "ones", [128, 128], BF16)
        mhalf = P.tile(top, "mhalf", [128, 512], F32)
        zcol = P.tile(top, "zcol", [128, 8, 2], BF16)
        P.dma("pool", lambda: (ident.t[:], ident_in[:, :]), ident.s, writes=[ident.s])
        P.op("dve", lambda e: e.memset(ones.t[:], 1.0), writes=[ones.s])
        P.op("pool", lambda e: e.memset(mhalf.t[:], -0.5), writes=[mhalf.s])
        P.op("dve", lambda e: e.memset(zcol.t[:], 0.0), writes=[zcol.s])
        for k, (st, S) in enumerate(cfg.seqs):
            c0 = st + 2 * k
            for cc in (c0, c0 + S + 1):
                P.dma("sp", lambda cc=cc: (H2T[:, cc:cc + 1].rearrange("(c p) t -> p c t", p=128), zcol.t[:, :, 0:1]),
                      zcol.s, reads=[zcol.s], allow_slow_non_contiguous=True)

        def rms_to_bf16(src, dst, gt, ms, rs, junk, eps=EPS):
            for b in range(4):
                P.op("act", lambda e, b=b: e.activation(out=junk.t[:], in_=src.t[:, b, :], func=AF.Square,
                                                         scale=1.0 / 32.0, accum_out=ms.t[:, b:b + 1]),
                     reads=[src.s], writes=[junk.s, ms.s])
            P.op("pool", lambda e: e.tensor_scalar(out=rs.t[:], in0=ms.t[:], scalar1=eps, scalar2=None, op0=ALU.add),
                 reads=[ms.s], writes=[rs.s])
            P.op("pool", lambda e: e.tensor_tensor(out=rs.t[:], in0=rs.t[:], in1=mhalf.t[:, 0:4], op=ALU.pow),
                 reads=[mhalf.s], writes=[rs.s])
            for b in range(4):
                P.op("dve", lambda e, b=b: e.scalar_tensor_tensor(out=dst.t[:, b, :], in0=src.t[:, b, :],
                                                                   scalar=rs.t[:, b:b + 1], in1=gt.t[:],
                                                                   op0=ALU.mult, op1=ALU.mult),
                     reads=[src.s, rs.s, gt.s], writes=[dst.s])

        def transpose_blocks(src, nchunk, dstT, pTs, cnt):
            for fc in range(nchunk):
                pT = pTs[cnt[0] % len(pTs)]
                P.mm([lambda e, b=b, fc=fc, pT=pT: e.transpose(out=pT.t[:, b * 128:(b + 1) * 128],
                                                              in_=src.t[:, b, fc * 128:(fc + 1) * 128],
                                                              identity=ident.t[:]) for b in range(4)],
                     reads=[src.s, ident.s], writes=[pT.s])
                eng = "act" if cnt[0] % 2 == 0 else "dve"
                if eng == "act":
                    P.op("act", lambda e, fc=fc, pT=pT: e.activation(out=dstT.t[:, fc, :], in_=pT.t[:], func=AF.Copy),
                         reads=[pT.s], writes=[dstT.s])
                else:
                    P.op("dve", lambda e, fc=fc, pT=pT: e.tensor_copy(out=dstT.t[:, fc, :], in_=pT.t[:]),
                         reads=[pT.s], writes=[dstT.s])
                cnt[0] += 1

        def load_w(Wt, src, nk, eng="pool"):
            for kc in range(nk):
                P.dma(eng, lambda kc=kc: (Wt.t[:, kc, :], src[kc * 128:(kc + 1) * 128, :]), Wt.s, writes=[Wt.s])

        def load_bcast(Tt, vec):
            P.dma("sp", lambda: (Tt.t[:], vec.partition_broadcast(128)), Tt.s, writes=[Tt.s])

        def phase1(l, Xsrc):
            mark_ = P.phase_begin()
            with ExitStack() as es:
                W = P.tile(es, "w1", [128, 8, INC], BF16)
                g1 = P.tile(es, "g1", [128, D], F32)
                xt = P.tile(es, "xt1", [128, 4, D], F32)
                h = P.tile(es, "h1", [128, 4, D], BF16)
                hT = P.tile(es, "hT1", [128, 8, TT], BF16)
                junk = P.tile(es, "junk1", [128, D], BF16)
                ms = P.tile(es, "ms1", [128, 4], F32)
                rs = P.tile(es, "rs1", [128, 4], F32)
                cs = P.tile(es, "cos1", [128, TT], F32)
                sn = P.tile(es, "sin1", [128, TT], F32)
                fa = P.tile(es, "fa1", [128, TT], F32)
                fb = P.tile(es, "fb1", [128, TT], F32)
                t1 = P.tile(es, "t11", [128, TT], F32)
                t2 = P.tile(es, "t21", [128, TT], F32)
                t3 = P.tile(es, "t31", [128, TT], F32)
                t4 = P.tile(es, "t41", [128, TT], F32)
                stg = [P.tile(es, "stg1_%d" % k, [128, TT], BF16) for k in range(4)]
                stf = [P.tile(es, "stf1_%d" % k, [128, TT], F32) for k in range(2)]
                pTs = [P.tile(es, "pT1_%d" % k, [128, TT], BF16, psum=True) for k in range(2)]
                acc = [P.tile(es, "acc1_%d" % k, [128, TT], F32, psum=True) for k in range(6)]
                load_w(W, w_in[l], 8)
                load_bcast(g1, g_pre[l])

                def body(i):
                    tok0 = None if i is None else i * TT
                    P.dma("sp", lambda: (xt.t[:], Xsrc[bass.ds(tok0, TT), :].rearrange("(b p) d -> p b d", p=128)),
                          xt.s, writes=[xt.s])
                    P.dma("sp", lambda: (cs.t[:], ropeC[:, bass.ds(tok0, TT)]), cs.s, writes=[cs.s])
                    P.dma("sp", lambda: (sn.t[:], ropeS[:, bass.ds(tok0, TT)]), sn.s, writes=[sn.s])
                    rms_to_bf16(xt, h, g1, ms, rs, junk)
                    cnt = [0]
                    transpose_blocks(h, 8, hT, pTs, cnt)
                    na = [0]
                    ns = [0]
                    nf = [0]

                    def group_fm(col0):
                        a = acc[na[0] % len(acc)]
                        na[0] += 1
                        P.mm([lambda e, kc=kc, a=a: e.matmul(a.t[:], lhsT=W.t[:, kc, col0:col0 + 128], rhs=hT.t[:, kc, :],
                                                             start=(kc == 0), stop=(kc == 7)) for kc in range(8)],
                             reads=[W.s, hT.s], writes=[a.s])
                        return a

                    def next_stg():
                        s_ = stg[ns[0] % len(stg)]
                        ns[0] += 1
                        return s_

                    for which, dst in ((0, QT), (1, KT)):
                        for g in range(4):
                            base = which * D + g * 256
                            pa = group_fm(base)
                            pb = group_fm(base + 128)
                            P.op("act", lambda e, pa=pa: e.activation(out=fa.t[:], in_=pa.t[:], func=AF.Copy),
                                 reads=[pa.s], writes=[fa.s])
                            P.op("act", lambda e, pb=pb: e.activation(out=fb.t[:], in_=pb.t[:], func=AF.Copy),
                                 reads=[pb.s], writes=[fb.s])
                            P.op("dve", lambda e: e.tensor_tensor(out=t1.t[:], in0=fa.t[:], in1=cs.t[:], op=ALU.mult),
                                 reads=[fa.s, cs.s], writes=[t1.s])
                            P.op("dve", lambda e: e.tensor_tensor(out=t2.t[:], in0=fb.t[:], in1=sn.t[:], op=ALU.mult),
                                 reads=[fb.s, sn.s], writes=[t2.s])
                            P.op("dve", lambda e: e.tensor_tensor(out=t3.t[:], in0=fb.t[:], in1=cs.t[:], op=ALU.mult),
                                 reads=[fb.s, cs.s], writes=[t3.s])
                            P.op("dve", lambda e: e.tensor_tensor(out=t4.t[:], in0=fa.t[:], in1=sn.t[:], op=ALU.mult),
                                 reads=[fa.s, sn.s], writes=[t4.s])
                            sa = next_stg()
                            P.op("dve", lambda e, sa=sa: e.tensor_tensor(out=sa.t[:], in0=t1.t[:], in1=t2.t[:], op=ALU.subtract),
                                 reads=[t1.s, t2.s], writes=[sa.s])
                            sb = next_stg()
                            P.op("dve", lambda e, sb=sb: e.tensor_tensor(out=sb.t[:], in0=t3.t[:], in1=t4.t[:], op=ALU.add),
                                 reads=[t3.s, t4.s], writes=[sb.s])
                            for half, st_ in ((0, sa), (1, sb)):
                                for m in range(4):
                                    r0 = (4 * g + m) * 64 + half * 32
                                    P.dma("sp", lambda st_=st_, m=m, r0=r0, dst=dst: (
                                        dst[r0:r0 + 32, bass.ds(tok0, TT)], st_.t[m * 32:(m + 1) * 32, :]),
                                        st_.s, reads=[st_.s])
                    for tb in range(4):
                        for hf in range(2):
                            a = acc[na[0] % len(acc)]
                            na[0] += 1
                            P.mm([lambda e, kc=kc, a=a, tb=tb, hf=hf: e.matmul(
                                a.t[:], lhsT=hT.t[:, kc, tb * 128:(tb + 1) * 128],
                                rhs=W.t[:, kc, 2 * D + hf * 512:2 * D + (hf + 1) * 512],
                                start=(kc == 0), stop=(kc == 7)) for kc in range(8)],
                                reads=[W.s, hT.s], writes=[a.s])
                            s_ = next_stg()
                            P.op("act", lambda e, a=a, s_=s_: e.activation(out=s_.t[:], in_=a.t[:], func=AF.Copy),
                                 reads=[a.s], writes=[s_.s])
                            P.dma("sp", lambda s_=s_, tb=tb, hf=hf: (
                                VV[bass.ds(tok0 + tb * 128, 128), hf * 512:(hf + 1) * 512], s_.t[:]), s_.s, reads=[s_.s])
                    for oc in range(8):
                        a = group_fm(3 * D + oc * 128)
                        f_ = stf[nf[0] % 2]
                        nf[0] += 1
                        P.op("dve", lambda e, a=a, f_=f_: e.tensor_copy(out=f_.t[:], in_=a.t[:]), reads=[a.s], writes=[f_.s])
                        P.dma("sp", lambda f_=f_, oc=oc: (UT[oc * 128:(oc + 1) * 128, bass.ds(tok0, TT)], f_.t[:]),
                              f_.s, reads=[f_.s])
                    for sec, dst, fn, sc in ((4, GLT, AF.Gelu_apprx_tanh, 1.0), (5, TGA, AF.Tanh, 0.5), (6, TGB, AF.Tanh, 0.5)):
                        for oc in range(8):
                            a = group_fm(sec * D + oc * 128)
                            s_ = next_stg()
                            P.op("act", lambda e, a=a, s_=s_, fn=fn, sc=sc: e.activation(out=s_.t[:], in_=a.t[:], func=fn, scale=sc),
                                 reads=[a.s], writes=[s_.s])
                            P.dma("sp", lambda s_=s_, oc=oc, dst=dst: (dst[oc * 128:(oc + 1) * 128, bass.ds(tok0, TT)], s_.t[:]),
                                  s_.s, reads=[s_.s])

                P.loop(ntile, body)
                P.phase_end(mark_)

        def phase2(l, lam_init):
            mark_ = P.phase_begin()
            Smax = max(cfg.s_p, cfg.s_s)
            with ExitStack() as es:
                KTh = P.tile(es, "kth", [128, Smax], BF16)
                QTh = P.tile(es, "qth", [128, Smax], BF16)
                Vh = P.tile(es, "vh", [128, Smax // 128, 128], BF16)
                ET = [P.tile(es, "et%d" % k, [128, 1024], BF16) for k in range(4)]
                sacc = P.tile(es, "sacc", [128, 1024], F32)
                onesf = P.tile(es, "onesf", [128, 128], F32)
                P.op("dve", lambda e: e.memset(onesf.t[:], 1.0), writes=[onesf.s])
                lv = P.tile(es, "lv", [128, 4, 64], F32)
                lj = P.tile(es, "lj", [128, 64], F32)
                ls = P.tile(es, "ls", [128, 2], F32)
                nlam = P.tile(es, "nlam", [128, 1], F32)
                gs = P.tile(es, "gs", [128, 1], F32)
                epsc = P.tile(es, "epsc", [128, 1], F32)
                P.op("dve", lambda e: e.memset(epsc.t[:], EPS), writes=[epsc.s])
                rs0 = P.tile(es, "rs0", [128, TT], F32)
                rs1 = P.tile(es, "rs1", [128, TT], F32)
                o0 = P.tile(es, "o0", [128, TT], F32)
                o1 = P.tile(es, "o1", [128, TT], F32)
                o2 = P.tile(es, "o2", [128, TT], F32)
                sq = P.tile(es, "sq", [128, TT], BF16)
                vv = P.tile(es, "vv2", [128, TT], F32)
                on = P.tile(es, "on", [128, TT], F32)
                oa = P.tile(es, "oa", [128, TT], BF16)
                ST = [P.tile(es, "st%d" % k, [128, 1024], F32, psum=True) for k in range(2)]
                O = [P.tile(es, "oacc%d" % k, [128, TT], F32, psum=True) for k in range(2)]
                SM = [P.tile(es, "sacc%d" % k, [128, TT], F32, psum=True) for k in range(2)]
                P.dma("sp", lambda: (lv.t[:], lamv[l].partition_broadcast(128)), lv.s, writes=[lv.s])
                for k in range(2):
                    P.op("dve", lambda e, k=k: e.scalar_tensor_tensor(out=lj.t[:], in0=lv.t[:, 2 * k, :], scalar=1.0,
                                                                       in1=lv.t[:, 2 * k + 1, :], op0=ALU.mult, op1=ALU.mult,
                                                                       accum_out=ls.t[:, k:k + 1]),
                         reads=[lv.s], writes=[lj.s, ls.s])
                P.op("act", lambda e: e.activation(out=ls.t[:], in_=ls.t[:], func=AF.Exp), writes=[ls.s])
                P.op("dve", lambda e: e.tensor_tensor(out=nlam.t[:], in0=ls.t[:, 1:2], in1=ls.t[:, 0:1], op=ALU.subtract),
                     reads=[ls.s], writes=[nlam.s])
                P.op("dve", lambda e: e.tensor_scalar(out=nlam.t[:], in0=nlam.t[:], scalar1=-lam_init, scalar2=None, op0=ALU.add),
                     writes=[nlam.s])
                P.dma("sp", lambda: (gs.t[:], subln[l].rearrange("(p o) -> p o", o=1)), gs.s, writes=[gs.s])
                P.op("dve", lambda e: e.tensor_scalar(out=gs.t[:], in0=gs.t[:], scalar1=1.0 - lam_init, scalar2=None, op0=ALU.mult),
                     writes=[gs.s])

                def head_body(S, tokf):
                    nkb = S // 128
                    nqt = S // TT

                    def body(hi):
                        tok0 = None if hi is None else tokf()
                        P.dma("sp", lambda: (KTh.t[:, :S], KT[bass.ds(hi * 128, 128), bass.ds(tok0, S)]), KTh.s, writes=[KTh.s])
                        P.dma("sp", lambda: (QTh.t[:, :S], QT[bass.ds(hi * 128, 128), bass.ds(tok0, S)]), QTh.s, writes=[QTh.s])
                        P.dma("sp", lambda: (Vh.t[:, :nkb, :],
                                             VV[bass.ds(tok0, S), bass.ds(hi * 128, 128)].rearrange("(kb p) v -> p kb v", p=128)),
                              Vh.s, writes=[Vh.s])
                        step = 0
                        pending = []
                        for qt in range(nqt):
                            qs = slice(qt * TT, (qt + 1) * TT)

                            def qk(kb, st):
                                ks = slice(kb * 128, (kb + 1) * 128)
                                P.mm([lambda e: e.matmul(st.t[:, 0:512], lhsT=KTh.t[0:64, ks], rhs=QTh.t[0:64, qs], start=True, stop=True),
                                      lambda e: e.matmul(st.t[:, 512:1024], lhsT=KTh.t[64:128, ks], rhs=QTh.t[64:128, qs], start=True, stop=True)],
                                     reads=[KTh.s, QTh.s], writes=[st.s])

                            def pv(kb, et):
                                fns = []
                                for mp in range(2):
                                    fns.append(lambda e, mp=mp: e.matmul(
                                        O[mp].t[:], lhsT=Vh.t[:, kb, :], rhs=et.t[:, mp * 512:(mp + 1) * 512],
                                        start=(kb == 0), stop=(kb == nkb - 1)))
                                fns.append(lambda e: e.matmul(SM[1].t[:], lhsT=ones.t[:], rhs=et.t[:, 512:1024],
                                                              start=(kb == 0), stop=(kb == nkb - 1)))
                                P.mm(fns, reads=[Vh.s, et.s, ones.s], writes=[O[0].s, O[1].s, SM[1].s])
                                if kb == 0:
                                    P.op("dve", lambda e: e.tensor_copy(out=sacc.t[:, 0:512], in_=et.t[:, 0:512]), reads=[et.s], writes=[sacc.s])
                                else:
                                    P.op("dve", lambda e: e.tensor_tensor(out=sacc.t[:, 0:512], in0=sacc.t[:, 0:512], in1=et.t[:, 0:512], op=ALU.add),
                                         reads=[et.s], writes=[sacc.s])

                            base = step
                            qk(0, ST[base % 2])
                            for kb in range(nkb):
                                st = ST[(base + kb) % 2]
                                et = ET[(base + kb) % 4]
                                P.op("act", lambda e, st=st, et=et: e.activation(out=et.t[:], in_=st.t[:], func=AF.Exp, scale=0.125),
                                     reads=[st.s], writes=[et.s])
                                if kb + 1 < nkb:
                                    qk(kb + 1, ST[(base + kb + 1) % 2])
                                pv(kb, et)
                                if kb >= 3 and pending:
                                    pending.pop(0)()
                            step = base + nkb
                            while pending:
                                pending.pop(0)()
                            P.op("dve", lambda e: e.tensor_copy(out=o0.t[:], in_=O[0].t[:]), reads=[O[0].s], writes=[o0.s])
                            P.op("dve", lambda e: e.tensor_copy(out=o1.t[:], in_=O[1].t[:]), reads=[O[1].s], writes=[o1.s])
                            P.op("dve", lambda e: e.tensor_copy(out=rs1.t[:], in_=SM[1].t[:]), reads=[SM[1].s], writes=[rs1.s])
                            P.mm([lambda e: e.matmul(SM[0].t[:], lhsT=onesf.t[:], rhs=sacc.t[:, 0:512], start=True, stop=True)],
                                 reads=[onesf.s, sacc.s], writes=[SM[0].s])
                            P.op("dve", lambda e: e.tensor_copy(out=rs0.t[:], in_=SM[0].t[:]), reads=[SM[0].s], writes=[rs0.s])

                            ops_b = []
                            def _op(qt=qt):
                                P.op("dve", lambda e: e.reciprocal(out=rs0.t[:], in_=rs0.t[:]), writes=[rs0.s])
                            ops_b.append(_op)
                            def _op(qt=qt):
                                P.op("dve", lambda e: e.tensor_tensor(out=o0.t[:], in0=o0.t[:], in1=rs0.t[:], op=ALU.mult),
                                     reads=[rs0.s], writes=[o0.s])
                            ops_b.append(_op)
                            def _op(qt=qt):
                                P.op("dve", lambda e: e.reciprocal(out=rs1.t[:], in_=rs1.t[:]), writes=[rs1.s])
                            ops_b.append(_op)
                            def _op(qt=qt):
                                P.op("dve", lambda e: e.tensor_tensor(out=o1.t[:], in0=o1.t[:], in1=rs1.t[:], op=ALU.mult),
                                     reads=[rs1.s], writes=[o1.s])
                            ops_b.append(_op)
                            def _op(qt=qt):
                                P.op("dve", lambda e: e.scalar_tensor_tensor(out=o2.t[:], in0=o1.t[:], scalar=nlam.t[:, 0:1], in1=o0.t[:],
                                                                              op0=ALU.mult, op1=ALU.add),
                                     reads=[o1.s, o0.s, nlam.s], writes=[o2.s])
                            ops_b.append(_op)
                            def _op(qt=qt):
                                P.op("act", lambda e: e.activation(out=sq.t[:], in_=o2.t[:], func=AF.Square), reads=[o2.s], writes=[sq.s])
                            ops_b.append(_op)
                            def _op(qt=qt):
                                P.mm([lambda e: e.matmul(SM[0].t[:], lhsT=ones.t[:], rhs=sq.t[:], start=True, stop=True)],
                                     reads=[ones.s, sq.s], writes=[SM[0].s])
                            ops_b.append(_op)
                            def _op(qt=qt):
                                P.op("act", lambda e: e.activation(out=vv.t[:], in_=SM[0].t[:], func=AF.Ln,
                                                                   scale=1.0 / 128.0, bias=epsc.t[:, 0:1]),
                                     reads=[SM[0].s, epsc.s], writes=[vv.s])
                            ops_b.append(_op)
                            def _op(qt=qt):
                                P.op("act", lambda e: e.activation(out=vv.t[:], in_=vv.t[:], func=AF.Exp, scale=-0.5), writes=[vv.s])
                            ops_b.append(_op)
                            def _op(qt=qt):
                                P.op("dve", lambda e: e.tensor_tensor(out=on.t[:], in0=o2.t[:], in1=vv.t[:], op=ALU.mult),
                                     reads=[o2.s, vv.s], writes=[on.s])
                            ops_b.append(_op)
                            def _op(qt=qt):
                                P.op("dve", lambda e: e.tensor_scalar(out=oa.t[:], in0=on.t[:], scalar1=gs.t[:, 0:1], scalar2=None, op0=ALU.mult),
                                     reads=[on.s, gs.s], writes=[oa.s])
                            ops_b.append(_op)
                            def _op(qt=qt):
                                P.dma("sp", lambda: (OAT[bass.ds(hi * 128, 128), bass.ds(tok0 + qt * TT, TT)], oa.t[:]),
                                      oa.s, reads=[oa.s])
                            ops_b.append(_op)
                            pending.extend(ops_b)
                        while pending:
                            pending.pop(0)()
                    return body

                if cfg.n_prompt > 0:
                    def seq_body(si):
                        P.loop(NH, head_body(cfg.s_p, lambda: si * cfg.s_p))
                    P.loop(cfg.n_prompt, seq_body)
                P.loop(NH, head_body(cfg.s_s, lambda: cfg.n_prompt * cfg.s_p))
                P.phase_end(mark_)

        def phase3(l):
            mark_ = P.phase_begin()
            Smax = max(cfg.s_p, cfg.s_s)
            with ExitStack() as es:
                upad = P.tile(es, "upad", [128, Smax + 3], F32)
                xc = P.tile(es, "xc", [128, Smax], F32)
                xcb = P.tile(es, "xcb", [128, Smax], BF16)
                glt = P.tile(es, "glt", [128, Smax], BF16)
                hf = P.tile(es, "hf", [128, Smax], F32)
                prm = P.tile(es, "prm", [128, 11], F32)
                drv = P.tile(es, "drv", [128, 8], F32)
                wrg = P.tile(es, "wrg", [128, 4, 128], BF16)
                er = P.tile(es, "er", [128, TT], F32)
                ei = P.tile(es, "ei", [128, TT], F32)
                av = P.tile(es, "av", [128, TT], F32)
                a2 = P.tile(es, "a2", [128, TT], F32)
                ix = P.tile(es, "ix", [128, TT], F32)
                bv = P.tile(es, "bv", [128, TT], F32)
                hb = [P.tile(es, "hb%d" % k, [128, TT], F32) for k in range(2)]
                ob = [P.tile(es, "ob%d" % k, [128, TT], BF16) for k in range(2)]
                psr = [P.tile(es, "psr%d" % k, [128, TT], F32, psum=True) for k in range(2)]
                psi = [P.tile(es, "psi%d" % k, [128, TT], F32, psum=True) for k in range(2)]
                P.op("dve", lambda e: e.memset(upad.t[:], 0.0), writes=[upad.s])

                def chunk_body(S, tokf):
                    nT = S // TT

                    def body(ci):
                        tok0 = None if ci is None else tokf()
                        P.dma("sp", lambda: (prm.t[:], lruprm[l, bass.ds(ci, 1), :, :].rearrange("o p k -> (o p) k")), prm.s, writes=[prm.s])
                        P.dma("pool", lambda: (wrg.t[:], rgw[l, bass.ds(ci, 1), :, :, :].rearrange("o f i j -> (o i) f j")), wrg.s, writes=[wrg.s])
                        P.dma("sp", lambda: (upad.t[:, 2:S + 2], UT[bass.ds(ci * 128, 128), bass.ds(tok0, S)]), upad.s, writes=[upad.s])
                        P.dma("sp", lambda: (glt.t[:, :S], GLT[bass.ds(ci * 128, 128), bass.ds(tok0, S)]), glt.s, writes=[glt.s])
                        P.op("dve", lambda e: e.tensor_scalar(out=drv.t[:, 0:4], in0=prm.t[:, 5:9], scalar1=-1.0, scalar2=None, op0=ALU.mult),
                             reads=[prm.s], writes=[drv.s])
                        P.op("act", lambda e: e.activation(out=drv.t[:, 4:6], in_=prm.t[:, 9:11], func=AF.Exp, scale=-1.0),
                             reads=[prm.s], writes=[drv.s])
                        P.op("act", lambda e: e.activation(out=drv.t[:, 4:6], in_=drv.t[:, 4:6], func=AF.Ln, bias=1.0, scale=1.0),
                             writes=[drv.s])
                        P.op("dve", lambda e: e.tensor_scalar(out=drv.t[:, 6:8], in0=drv.t[:, 4:6], scalar1=-2.0 * LRU_C, scalar2=None, op0=ALU.mult),
                             writes=[drv.s])
                        P.op("dve", lambda e: e.tensor_scalar(out=drv.t[:, 4:6], in0=drv.t[:, 4:6], scalar1=-LRU_C, scalar2=None, op0=ALU.mult),
                             writes=[drv.s])
                        P.op("dve", lambda e: e.tensor_scalar(out=xc.t[:, :S], in0=upad.t[:, 0:S], scalar1=prm.t[:, 0:1], scalar2=prm.t[:, 4:5],
                                                              op0=ALU.mult, op1=ALU.add),
                             reads=[upad.s, prm.s], writes=[xc.s])
                        for k in range(1, 4):
                            P.op("dve", lambda e, k=k: e.scalar_tensor_tensor(out=xc.t[:, :S], in0=upad.t[:, k:S + k], scalar=prm.t[:, k:k + 1],
                                                                               in1=xc.t[:, :S], op0=ALU.mult, op1=ALU.add),
                                 reads=[upad.s, prm.s], writes=[xc.s])
                        P.op("act", lambda e: e.activation(out=xcb.t[:, :S], in_=xc.t[:, :S], func=AF.Copy), reads=[xc.s], writes=[xcb.s])
                        n = 0
                        for d in range(2):
                            order = range(nT) if d == 0 else range(nT - 1, -1, -1)
                            prev_hb = None
                            for j in order:
                                cs_ = slice(j * TT, (j + 1) * TT)
                                pr = psr[n % 2]
                                pi = psi[n % 2]
                                P.mm([lambda e, pr=pr, cs_=cs_: e.matmul(pr.t[:], lhsT=wrg.t[:, 2 * d, :], rhs=xcb.t[:, cs_], start=True, stop=True)],
                                     reads=[wrg.s, xcb.s], writes=[pr.s])
                                P.mm([lambda e, pi=pi, cs_=cs_: e.matmul(pi.t[:], lhsT=wrg.t[:, 2 * d + 1, :], rhs=xcb.t[:, cs_], start=True, stop=True)],
                                     reads=[wrg.s, xcb.s], writes=[pi.s])
                                P.op("act", lambda e, pr=pr: e.activation(out=er.t[:], in_=pr.t[:], func=AF.Exp, scale=-1.0, bias=drv.t[:, d:d + 1]),
                                     reads=[pr.s, drv.s], writes=[er.s])
                                P.op("act", lambda e, pi=pi: e.activation(out=ei.t[:], in_=pi.t[:], func=AF.Exp, scale=-1.0, bias=drv.t[:, 2 + d:3 + d]),
                                     reads=[pi.s, drv.s], writes=[ei.s])
                                P.op("act", lambda e: e.activation(out=er.t[:], in_=er.t[:], func=AF.Ln, bias=1.0, scale=1.0), writes=[er.s])
                                P.op("act", lambda e: e.activation(out=er.t[:], in_=er.t[:], func=AF.Exp, scale=-1.0), writes=[er.s])
                                P.op("act", lambda e: e.activation(out=ei.t[:], in_=ei.t[:], func=AF.Ln, bias=1.0, scale=1.0), writes=[ei.s])
                                P.op("act", lambda e: e.activation(out=ei.t[:], in_=ei.t[:], func=AF.Exp, scale=-1.0), writes=[ei.s])
                                P.op("act", lambda e: e.activation(out=av.t[:], in_=er.t[:], func=AF.Exp, scale=drv.t[:, 4 + d:5 + d]),
                                     reads=[er.s, drv.s], writes=[av.s])
                                P.op("act", lambda e: e.activation(out=a2.t[:], in_=er.t[:], func=AF.Exp, scale=drv.t[:, 6 + d:7 + d]),
                                     reads=[er.s, drv.s], writes=[a2.s])
                                P.op("act", lambda e: e.activation(out=a2.t[:], in_=a2.t[:], func=AF.Ln, scale=-1.0, bias=1.0), writes=[a2.s])
                                P.op("act", lambda e: e.activation(out=a2.t[:], in_=a2.t[:], func=AF.Exp, scale=0.5), writes=[a2.s])
                                P.op("dve", lambda e, cs_=cs_: e.tensor_tensor(out=ix.t[:], in0=ei.t[:], in1=xc.t[:, cs_], op=ALU.mult),
                                     reads=[ei.s, xc.s], writes=[ix.s])
                                P.op("dve", lambda e: e.tensor_tensor(out=bv.t[:], in0=ix.t[:], in1=a2.t[:], op=ALU.mult),
                                     reads=[ix.s, a2.s], writes=[bv.s])
                                if d == 0:
                                    init = 0.0 if j == 0 else hf.t[:, j * TT - 1:j * TT]
                                    P.op("dve", lambda e, cs_=cs_, init=init: e.tensor_tensor_scan(
                                        out=hf.t[:, cs_], data0=av.t[:], data1=bv.t[:], initial=init, op0=ALU.mult, op1=ALU.add),
                                        reads=[av.s, bv.s], writes=[hf.s])
                                else:
                                    hbt = hb[n % 2]
                                    init = 0.0 if prev_hb is None else prev_hb.t[:, 0:1]
                                    rd = [av.s, bv.s] + ([prev_hb.s] if prev_hb is not None else [])
                                    P.op("dve", lambda e, hbt=hbt, init=init: e.tensor_tensor_scan(
                                        out=hbt.t[:, ::-1], data0=av.t[:, ::-1], data1=bv.t[:, ::-1], initial=init, op0=ALU.mult, op1=ALU.add),
                                        reads=rd, writes=[hbt.s])
                                    prev_hb = hbt
                                    P.op("pool", lambda e, hbt=hbt, cs_=cs_: e.tensor_tensor(out=ix.t[:], in0=hbt.t[:], in1=hf.t[:, cs_], op=ALU.add),
                                         reads=[hbt.s, hf.s], writes=[ix.s])
                                    obt = ob[n % 2]
                                    P.op("pool", lambda e, obt=obt, cs_=cs_: e.tensor_tensor(out=obt.t[:], in0=ix.t[:], in1=glt.t[:, cs_], op=ALU.mult),
                                         reads=[ix.s, glt.s], writes=[obt.s])
                                    P.dma("sp", lambda obt=obt, j=j: (OLT[bass.ds(ci * 128, 128), bass.ds(tok0 + j * TT, TT)], obt.t[:]),
                                          obt.s, reads=[obt.s])
                                n += 1
                    return body

                if cfg.n_prompt > 0:
                    def seq_body(si):
                        P.loop(8, chunk_body(cfg.s_p, lambda: si * cfg.s_p))
                    P.loop(cfg.n_prompt, seq_body)
                P.loop(8, chunk_body(cfg.s_s, lambda: cfg.n_prompt * cfg.s_p))
                P.phase_end(mark_)

        def phase4(l, Xsrc):
            mark_ = P.phase_begin()
            with ExitStack() as es:
                Wa = P.tile(es, "wa4", [128, 8, D], BF16)
                Wl = P.tile(es, "wl4", [128, 8, D], BF16)
                Wo = P.tile(es, "wo4", [128, 8, D], BF16)
                gpo = P.tile(es, "gpo4", [128, D], F32)
                gfp = P.tile(es, "gfp4", [128, D], F32)
                oat = P.tile(es, "oat4", [128, 8, TT], BF16)
                olt = P.tile(es, "olt4", [128, 8, TT], BF16)
                tga = P.tile(es, "tga4", [128, 8, TT], BF16)
                tgb = P.tile(es, "tgb4", [128, 8, TT], BF16)
                xt = P.tile(es, "xt4", [128, 4, D], F32)
                mT = P.tile(es, "mT4", [128, 8, TT], BF16)
                yt = P.tile(es, "yt4", [128, 4, D], F32)
                h2 = P.tile(es, "h24", [128, 4, D], BF16)
                h2T = P.tile(es, "h2T4", [128, 8, TT], BF16)
                junk = P.tile(es, "junk4", [128, D], BF16)
                ms = P.tile(es, "ms4", [128, 4], F32)
                rs = P.tile(es, "rs4", [128, 4], F32)
                ms2 = P.tile(es, "ms42", [128, 4], F32)
                rs2 = P.tile(es, "rs42", [128, 4], F32)
                ta = P.tile(es, "ta4", [128, TT], F32)
                tb_ = P.tile(es, "tb4", [128, TT], F32)
                pTs = [P.tile(es, "pT4_%d" % k, [128, TT], BF16, psum=True) for k in range(2)]
                acc = [P.tile(es, "acc4_%d" % k, [128, TT], F32, psum=True) for k in range(6)]
                load_w(Wa, w_ba[l], 8)
                load_w(Wl, w_bl[l], 8)
                load_w(Wo, w_out[l], 8)
                load_bcast(gpo, g_post[l])
                load_bcast(gfp, g_fpre[l])

                def mk_body(tokf, colf):
                    def body(i):
                        tok0 = None if i is None else tokf(i)
                        col0 = None if i is None else colf(i)
                        for T_, src in ((oat, OAT), (olt, OLT), (tga, TGA), (tgb, TGB)):
                            P.dma("sp", lambda T_=T_, src=src: (T_.t[:], src[:, bass.ds(tok0, TT)].rearrange("(c p) t -> p c t", p=128)),
                                  T_.s, writes=[T_.s])
                        P.dma("sp", lambda: (xt.t[:], Xsrc[bass.ds(tok0, TT), :].rearrange("(b p) d -> p b d", p=128)), xt.s, writes=[xt.s])
                        na = 0
                        for oc in range(8):
                            pa = acc[na % 6]
                            pb = acc[(na + 1) % 6]
                            na += 2
                            P.mm([lambda e, kc=kc, pa=pa, oc=oc: e.matmul(pa.t[:], lhsT=Wa.t[:, kc, oc * 128:(oc + 1) * 128], rhs=oat.t[:, kc, :],
                                                                           start=(kc == 0), stop=(kc == 7)) for kc in range(8)],
                                 reads=[Wa.s, oat.s], writes=[pa.s])
                            P.mm([lambda e, kc=kc, pb=pb, oc=oc: e.matmul(pb.t[:], lhsT=Wl.t[:, kc, oc * 128:(oc + 1) * 128], rhs=olt.t[:, kc, :],
                                                                           start=(kc == 0), stop=(kc == 7)) for kc in range(8)],
                                 reads=[Wl.s, olt.s], writes=[pb.s])
                            P.op("dve", lambda e, pa=pa, oc=oc: e.scalar_tensor_tensor(out=ta.t[:], in0=tga.t[:, oc, :], scalar=1.0, in1=pa.t[:],
                                                                                        op0=ALU.add, op1=ALU.mult),
                                 reads=[tga.s, pa.s], writes=[ta.s])
                            P.op("dve", lambda e, pb=pb, oc=oc: e.scalar_tensor_tensor(out=tb_.t[:], in0=tgb.t[:, oc, :], scalar=1.0, in1=pb.t[:],
                                                                                        op0=ALU.add, op1=ALU.mult),
                                 reads=[tgb.s, pb.s], writes=[tb_.s])
                            P.op("dve", lambda e, oc=oc: e.tensor_tensor(out=mT.t[:, oc, :], in0=ta.t[:], in1=tb_.t[:], op=ALU.add),
                                 reads=[ta.s, tb_.s], writes=[mT.s])
                        for tb in range(4):
                            for hf_ in range(2):
                                a = acc[na % 6]
                                na += 1
                                P.mm([lambda e, kc=kc, a=a, tb=tb, hf_=hf_: e.matmul(a.t[:], lhsT=mT.t[:, kc, tb * 128:(tb + 1) * 128],
                                                                                     rhs=Wo.t[:, kc, hf_ * 512:(hf_ + 1) * 512],
                                                                                     start=(kc == 0), stop=(kc == 7)) for kc in range(8)],
                                     reads=[Wo.s, mT.s], writes=[a.s])
                                P.op("act", lambda e, a=a, tb=tb, hf_=hf_: e.activation(out=yt.t[:, tb, hf_ * 512:(hf_ + 1) * 512], in_=a.t[:], func=AF.Copy),
                                     reads=[a.s], writes=[yt.s])
                        for b in range(4):
                            P.op("act", lambda e, b=b: e.activation(out=junk.t[:], in_=yt.t[:, b, :], func=AF.Square, scale=1.0 / 32.0,
                                                                     accum_out=ms.t[:, b:b + 1]),
                                 reads=[yt.s], writes=[junk.s, ms.s])
                        P.op("pool", lambda e: e.tensor_scalar(out=rs.t[:], in0=ms.t[:], scalar1=4.0 * EPS, scalar2=None, op0=ALU.add),
                             reads=[ms.s], writes=[rs.s])
                        P.op("pool", lambda e: e.tensor_tensor(out=rs.t[:], in0=rs.t[:], in1=mhalf.t[:, 0:4], op=ALU.pow), reads=[mhalf.s], writes=[rs.s])
                        for b in range(4):
                            P.op("dve", lambda e, b=b: e.scalar_tensor_tensor(out=yt.t[:, b, :], in0=yt.t[:, b, :], scalar=rs.t[:, b:b + 1], in1=gpo.t[:],
                                                                               op0=ALU.mult, op1=ALU.mult),
                                 reads=[rs.s, gpo.s], writes=[yt.s])
                            P.op("dve", lambda e, b=b: e.tensor_tensor(out=xt.t[:, b, :], in0=xt.t[:, b, :], in1=yt.t[:, b, :], op=ALU.add),
                                 reads=[yt.s], writes=[xt.s])
                        P.dma("sp", lambda: (X1[bass.ds(tok0, TT), :].rearrange("(b p) d -> p b d", p=128), xt.t[:]), xt.s, reads=[xt.s])
                        rms_to_bf16(xt, h2, gfp, ms2, rs2, junk)
                        cnt = [0]
                        transpose_blocks(h2, 8, h2T, pTs, cnt)
                        P.dma("sp", lambda: (H2T[:, bass.ds(col0, TT)].rearrange("(c p) t -> p c t", p=128), h2T.t[:]), h2T.s, reads=[h2T.s])
                    return body

                run_token_loops(mk_body)
                P.phase_end(mark_)

        def run_token_loops(mk_body):
            tp = cfg.s_p // TT
            if cfg.n_prompt > 0:
                def seq_body(si):
                    P.loop(tp, mk_body(lambda j: si * cfg.s_p + j * TT, lambda j: si * (cfg.s_p + 2) + j * TT + 1))
                P.loop(cfg.n_prompt, seq_body)
            b0 = cfg.n_prompt * cfg.s_p
            c0 = cfg.n_prompt * (cfg.s_p + 2) + 1
            P.loop(cfg.s_s // TT, mk_body(lambda j: j * TT + b0, lambda j: j * TT + c0))

        def phase5(l):
            mark_ = P.phase_begin()
            with ExitStack() as es:
                Wu = P.tile(es, "wu5", [128, 8, 2 * DFF], BF16)
                fp = P.tile(es, "fp5", [128, 24, 4], F32)
                h2e = P.tile(es, "h2e5", [128, 8, TT + 2], BF16)
                cext = [P.tile(es, "cext5_%d" % k, [128, TT + 2], F32) for k in range(2)]
                cacc = [P.tile(es, "cacc5_%d" % k, [128, TT], F32) for k in range(2)]
                gg = [P.tile(es, "gg5_%d" % k, [128, TT], F32) for k in range(2)]
                go = [P.tile(es, "go5_%d" % k, [128, TT], BF16) for k in range(3)]
                psc = [P.tile(es, "psc5_%d" % k, [128, TT], F32, psum=True) for k in range(3)]
                psl = [P.tile(es, "psl5_%d" % k, [128, TT], F32, psum=True) for k in range(3)]
                psh = [P.tile(es, "psh5_%d" % k, [128, 2], F32, psum=True) for k in range(2)]
                load_w(Wu, w_up[l], 8)
                P.dma("sp", lambda: (fp.t[:], ffnprm[l]), fp.s, writes=[fp.s])

                def mk_body(tokf, colf):
                    def body(i):
                        tok0 = None if i is None else tokf(i)
                        col0 = None if i is None else colf(i)
                        P.dma("sp", lambda: (h2e.t[:], H2T[:, bass.ds(col0 - 1, TT + 2)].rearrange("(c p) t -> p c t", p=128)), h2e.s, writes=[h2e.s])
                        for j in range(24):
                            pc = psc[j % 3]
                            pl = psl[j % 3]
                            ph = psh[j % 2]
                            ce = cext[j % 2]
                            ca = cacc[j % 2]
                            g_ = gg[j % 2]
                            o_ = go[j % 3]
                            P.mm([lambda e, kc=kc, pc=pc, j=j: e.matmul(pc.t[:], lhsT=Wu.t[:, kc, j * 128:(j + 1) * 128], rhs=h2e.t[:, kc, 1:TT + 1],
                                                                         start=(kc == 0), stop=(kc == 7)) for kc in range(8)],
                                 reads=[Wu.s, h2e.s], writes=[pc.s])
                            P.mm([lambda e, kc=kc, ph=ph, j=j: e.matmul(ph.t[:], lhsT=Wu.t[:, kc, j * 128:(j + 1) * 128], rhs=h2e.t[:, kc, 0:TT + 2:TT + 1],
                                                                         start=(kc == 0), stop=(kc == 7)) for kc in range(8)],
                                 reads=[Wu.s, h2e.s], writes=[ph.s])
                            P.mm([lambda e, kc=kc, pl=pl, j=j: e.matmul(pl.t[:], lhsT=Wu.t[:, kc, DFF + j * 128:DFF + (j + 1) * 128], rhs=h2e.t[:, kc, 1:TT + 1],
                                                                         start=(kc == 0), stop=(kc == 7)) for kc in range(8)],
                                 reads=[Wu.s, h2e.s], writes=[pl.s])
                            P.op("act", lambda e, pc=pc, ce=ce: e.activation(out=ce.t[:, 1:TT + 1], in_=pc.t[:], func=AF.Copy), reads=[pc.s], writes=[ce.s])
                            P.op("dve", lambda e, ph=ph, ce=ce: e.tensor_copy(out=ce.t[:, 0:TT + 2:TT + 1], in_=ph.t[:]), reads=[ph.s], writes=[ce.s])
                            P.op("act", lambda e, pc=pc, ca=ca, j=j: e.activation(out=ca.t[:], in_=pc.t[:], func=AF.Identity,
                                                                                 scale=fp.t[:, j, 1:2], bias=fp.t[:, j, 3:4]),
                                 reads=[pc.s, fp.s], writes=[ca.s])
                            for k in (0, 2):
                                P.op("dve", lambda e, ce=ce, ca=ca, j=j, k=k: e.scalar_tensor_tensor(out=ca.t[:], in0=ce.t[:, k:TT + k], scalar=fp.t[:, j, k:k + 1],
                                                                                                       in1=ca.t[:], op0=ALU.mult, op1=ALU.add),
                                     reads=[ce.s, fp.s], writes=[ca.s])
                            P.op("act", lambda e, ca=ca, g_=g_: e.activation(out=g_.t[:], in_=ca.t[:], func=AF.Gelu_apprx_tanh), reads=[ca.s], writes=[g_.s])
                            P.op("dve", lambda e, g_=g_, pl=pl, o_=o_: e.tensor_tensor(out=o_.t[:], in0=g_.t[:], in1=pl.t[:], op=ALU.mult),
                                 reads=[g_.s, pl.s], writes=[o_.s])
                            P.dma("sp", lambda o_=o_, j=j: (GT[j * 128:(j + 1) * 128, bass.ds(tok0, TT)], o_.t[:]), o_.s, reads=[o_.s])
                    return body

                run_token_loops(mk_body)
                P.phase_end(mark_)

        def phase6(l, Xdst):
            mark_ = P.phase_begin()
            with ExitStack() as es:
                Wd = P.tile(es, "wd6", [128, 24, D], BF16)
                Wg = P.tile(es, "wg6", [128, 8, D], BF16)
                Wp = P.tile(es, "wp6", [128, 2, D], BF16)
                gfo = P.tile(es, "gfo6", [128, D], F32)
                gT = P.tile(es, "gT6", [128, 24, TT], BF16)
                xt = P.tile(es, "xt6", [128, 4, D], F32)
                pt = P.tile(es, "pt6", [128, 4, PLE], F32)
                ft = P.tile(es, "ft6", [128, 4, D], F32)
                xb = P.tile(es, "xb6", [128, 4, D], BF16)
                pb = P.tile(es, "pb6", [128, 4, PLE], BF16)
                xT = P.tile(es, "xT6", [128, 8, TT], BF16)
                pT_ = P.tile(es, "ppT6", [128, 2, TT], BF16)
                junk = P.tile(es, "junk6", [128, D], BF16)
                ms = P.tile(es, "ms6", [128, 4], F32)
                rs = P.tile(es, "rs6", [128, 4], F32)
                tg = [P.tile(es, "tg6_%d" % k, [128, TT], F32) for k in range(2)]
                pTs = [P.tile(es, "pT6_%d" % k, [128, TT], BF16, psum=True) for k in range(2)]
                acc = [P.tile(es, "acc6_%d" % k, [128, TT], F32, psum=True) for k in range(6)]
                load_w(Wd, w_dn[l], 24)
                load_w(Wg, w_pg[l], 8)
                load_w(Wp, w_pp[l], 2)
                load_bcast(gfo, g_fpost[l])

                def body(i):
                    tok0 = None if i is None else i * TT
                    P.dma("sp", lambda: (gT.t[:], GT[:, bass.ds(tok0, TT)].rearrange("(c p) t -> p c t", p=128)), gT.s, writes=[gT.s])
                    P.dma("sp", lambda: (xt.t[:], X1[bass.ds(tok0, TT), :].rearrange("(b p) d -> p b d", p=128)), xt.s, writes=[xt.s])
                    P.dma("sp", lambda: (pt.t[:], p_in[l, bass.ds(tok0, TT), :].rearrange("(b p) d -> p b d", p=128)), pt.s, writes=[pt.s])
                    na = 0
                    for tb in range(4):
                        for hf_ in range(2):
                            a = acc[na % 6]
                            na += 1
                            P.mm([lambda e, kc=kc, a=a, tb=tb, hf_=hf_: e.matmul(a.t[:], lhsT=gT.t[:, kc, tb * 128:(tb + 1) * 128],
                                                                                 rhs=Wd.t[:, kc, hf_ * 512:(hf_ + 1) * 512],
                                                                                 start=(kc == 0), stop=(kc == 23)) for kc in range(24)],
                                 reads=[Wd.s, gT.s], writes=[a.s])
                            P.op("act", lambda e, a=a, tb=tb, hf_=hf_: e.activation(out=ft.t[:, tb, hf_ * 512:(hf_ + 1) * 512], in_=a.t[:], func=AF.Copy),
                                 reads=[a.s], writes=[ft.s])
                    for b in range(4):
                        P.op("act", lambda e, b=b: e.activation(out=junk.t[:], in_=ft.t[:, b, :], func=AF.Square, scale=1.0 / 32.0,
                                                                 accum_out=ms.t[:, b:b + 1]),
                             reads=[ft.s], writes=[junk.s, ms.s])
                    P.op("pool", lambda e: e.tensor_scalar(out=rs.t[:], in0=ms.t[:], scalar1=EPS, scalar2=None, op0=ALU.add), reads=[ms.s], writes=[rs.s])
                    P.op("pool", lambda e: e.tensor_tensor(out=rs.t[:], in0=rs.t[:], in1=mhalf.t[:, 0:4], op=ALU.pow), reads=[mhalf.s], writes=[rs.s])
                    for b in range(4):
                        P.op("dve", lambda e, b=b: e.scalar_tensor_tensor(out=ft.t[:, b, :], in0=ft.t[:, b, :], scalar=rs.t[:, b:b + 1], in1=gfo.t[:],
                                                                           op0=ALU.mult, op1=ALU.mult),
                             reads=[rs.s, gfo.s], writes=[ft.s])
                        P.op("dve", lambda e, b=b: e.tensor_tensor(out=xt.t[:, b, :], in0=xt.t[:, b, :], in1=ft.t[:, b, :], op=ALU.add),
                             reads=[ft.s], writes=[xt.s])
                    P.op("act", lambda e: e.activation(out=xb.t[:], in_=xt.t[:], func=AF.Copy), reads=[xt.s], writes=[xb.s])
                    P.op("act", lambda e: e.activation(out=pb.t[:], in_=pt.t[:], func=AF.Copy), reads=[pt.s], writes=[pb.s])
                    cnt = [0]
                    transpose_blocks(xb, 8, xT, pTs, cnt)
                    transpose_blocks(pb, 2, pT_, pTs, cnt)
                    n2 = 0
                    for tb in range(4):
                        for hf_ in range(2):
                            ag = acc[na % 6]
                            ap_ = acc[(na + 1) % 6]
                            na += 2
                            t_ = tg[n2 % 2]
                            n2 += 1
                            P.mm([lambda e, kc=kc, ag=ag, tb=tb, hf_=hf_: e.matmul(ag.t[:], lhsT=xT.t[:, kc, tb * 128:(tb + 1) * 128],
                                                                                   rhs=Wg.t[:, kc, hf_ * 512:(hf_ + 1) * 512],
                                                                                   start=(kc == 0), stop=(kc == 7)) for kc in range(8)],
                                 reads=[Wg.s, xT.s], writes=[ag.s])
                            P.mm([lambda e, kc=kc, ap_=ap_, tb=tb, hf_=hf_: e.matmul(ap_.t[:], lhsT=pT_.t[:, kc, tb * 128:(tb + 1) * 128],
                                                                                     rhs=Wp.t[:, kc, hf_ * 512:(hf_ + 1) * 512],
                                                                                     start=(kc == 0), stop=(kc == 1)) for kc in range(2)],
                                 reads=[Wp.s, pT_.s], writes=[ap_.s])
                            P.op("act", lambda e, ag=ag, t_=t_: e.activation(out=t_.t[:], in_=ag.t[:], func=AF.Tanh, scale=0.5), reads=[ag.s], writes=[t_.s])
                            P.op("dve", lambda e, ap_=ap_, t_=t_: e.scalar_tensor_tensor(out=t_.t[:], in0=t_.t[:], scalar=1.0, in1=ap_.t[:],
                                                                                          op0=ALU.add, op1=ALU.mult),
                                 reads=[ap_.s], writes=[t_.s])
                            P.op("dve", lambda e, t_=t_, tb=tb, hf_=hf_: e.scalar_tensor_tensor(
                                out=ft.t[:, tb, hf_ * 512:(hf_ + 1) * 512], in0=t_.t[:], scalar=0.5, in1=xt.t[:, tb, hf_ * 512:(hf_ + 1) * 512],
                                op0=ALU.mult, op1=ALU.add),
                                reads=[t_.s, xt.s], writes=[ft.s])
                    P.dma("sp", lambda: (Xdst[bass.ds(tok0, TT), :].rearrange("(b p) d -> p b d", p=128), ft.t[:]), ft.s, reads=[ft.s])

                P.loop(ntile, body)
                P.phase_end(mark_)

        for l in range(L):
            lam_init = 0.8 - 0.6 * math.exp(-0.3 * l)
            Xsrc = x_in if l == 0 else XS
            Xdst = y_out if l == L - 1 else XS
            phase1(l, Xsrc)
            phase2(l, lam_init)
            phase3(l)
            phase4(l, Xsrc)
            phase5(l)
            phase6(l, Xdst)
        P.barrier()
    return nc


def _rope_tables(cfg):
    inv = THETA ** (-np.arange(0, 64, 2, dtype=np.float32) / 64.0)
    pos = np.concatenate([np.arange(S, dtype=np.float32) for (_, S) in cfg.seqs])
    ang = pos[None, :] * inv[:, None].astype(np.float32)
    c = np.tile(np.cos(ang).astype(np.float32), (4, 1))
    s = np.tile(np.sin(ang).astype(np.float32), (4, 1))
    return np.ascontiguousarray(c), np.ascontiguousarray(s)


def _qk_perm():
    perm = []
    for g in range(4):
        for half in range(2):
            for m in range(4):
                for d in range(32):
                    perm.append((4 * g + m) * 64 + half * 32 + d)
    return np.array(perm)


def prep_shared(inp, L):
    f = lambda k: np.ascontiguousarray(np.asarray(inp[k], dtype=np.float32))
    w_in = f("w_in").copy()
    perm = _qk_perm()
    w_in[:, :, 0:D] = w_in[:, :, 0:D][:, :, perm]
    w_in[:, :, D:2 * D] = w_in[:, :, D:2 * D][:, :, perm]
    lamv = np.stack([f("lam_q1"), f("lam_k1"), f("lam_q2"), f("lam_k2")], axis=1)
    cw = f("lru_conv_w")
    cols = [cw[:, k, :] for k in range(4)] + [f("lru_conv_b")]
    cols += [f("rg_ba")[:, 0], f("rg_ba")[:, 1], f("rg_bx")[:, 0], f("rg_bx")[:, 1], f("rg_lambda")[:, 0], f("rg_lambda")[:, 1]]
    lruprm = np.stack(cols, axis=-1).reshape(L, 8, 128, 11)
    wa, wx = f("rg_wa"), f("rg_wx")
    rgw = np.stack([wa[:, 0], wx[:, 0], wa[:, 1], wx[:, 1]], axis=2)
    fw = f("ffn_conv_w")
    fcols = [fw[:, k, :] for k in range(3)] + [f("ffn_conv_b")]
    ffnprm = np.stack(fcols, axis=-1).reshape(L, 24, 128, 4).transpose(0, 2, 1, 3)
    sh = {
        "ident": np.eye(128, dtype=np.float32),
        "w_in": np.ascontiguousarray(w_in),
        "norm_mix_pre": f("norm_mix_pre"), "norm_mix_post": f("norm_mix_post"),
        "lamv": np.ascontiguousarray(lamv), "subln_g": f("subln_g"),
        "lruprm": np.ascontiguousarray(lruprm), "rgw": np.ascontiguousarray(rgw),
        "w_branch_attn": f("w_branch_attn"), "w_branch_lru": f("w_branch_lru"), "w_out": f("w_out"),
        "norm_ffn_pre": f("norm_ffn_pre"), "norm_ffn_post": f("norm_ffn_post"),
        "w_ffn_up": f("w_ffn_up"), "ffnprm": np.ascontiguousarray(ffnprm),
        "w_ffn_down": f("w_ffn_down"), "w_ple_proj": f("w_ple_proj"), "w_ple_gate": f("w_ple_gate"),
    }
    return sh


def run(cfg, inp, n_cores, debug=False):
    L = cfg.L
    sh = prep_shared(inp, L)
    rc, rs = _rope_tables(cfg)
    xp = np.asarray(inp["x_prompt"], dtype=np.float32)
    xs = np.asarray(inp["x_sample"], dtype=np.float32)
    pp = np.asarray(inp["p_prompt"], dtype=np.float32)
    ps = np.asarray(inp["p_sample"], dtype=np.float32)
    in_maps = []
    npc = cfg.n_prompt
    for c in range(n_cores):
        xc = np.concatenate([xp[c * npc:(c + 1) * npc].reshape(-1, D), xs[c].reshape(-1, D)], axis=0)
        pc = np.concatenate([pp[:, c * npc:(c + 1) * npc].reshape(L, -1, PLE), ps[:, c].reshape(L, -1, PLE)], axis=1)
        m = dict(sh)
        m["x"] = np.ascontiguousarray(xc)
        m["p"] = np.ascontiguousarray(pc)
        m["ropeC"] = rc
        m["ropeS"] = rs
        in_maps.append(m)
    nc = build(cfg, debug=debug)
    res = run_bass_kernel_spmd(nc, in_maps, core_ids=list(range(n_cores)))
    ys = [r["y"] for r in res.results]
    yp = np.stack([y[:npc * cfg.s_p].reshape(npc, cfg.s_p, D) for y in ys]).reshape(n_cores * npc, cfg.s_p, D)
    ysm = np.stack([y[npc * cfg.s_p:].reshape(cfg.s_s, D) for y in ys])
    return (yp.astype(np.float32), ysm.astype(np.float32)), res


def run_seqs(S, L, inp, xs, ps, n_cores, rounds):
    cfg = Cfg(0, 512, S, L)
    sh = prep_shared(inp, L)
    rc, rs = _rope_tables(cfg)
    nc = build(cfg)
    outs = []
    for r in range(rounds):
        in_maps = []
        for c in range(n_cores):
            k = r * n_cores + c
            m = dict(sh)
            m["x"] = np.ascontiguousarray(xs[k].reshape(S, D))
            m["p"] = np.ascontiguousarray(ps[:, k].reshape(L, S, PLE))
            m["ropeC"] = rc
            m["ropeS"] = rs
            in_maps.append(m)
        res = run_bass_kernel_spmd(nc, in_maps, core_ids=list(range(n_cores)))
        outs.extend([np.asarray(rr["y"], dtype=np.float32).reshape(S, D) for rr in res.results])
    return np.stack(outs)


def kernel(**inputs):
    cfg = Cfg(4, 2048, 8192, 2)
    out, _ = run(cfg, inputs, 8)
    return out
```
